# Optimizing a Trainium2 kernel written in Bass

```python
import numpy as np
import jax
import jax.numpy as jnp
from jax import lax

D_MODEL = 1024
BATCH = 32
SEQ = 256
DEPTH = 4
DEC_BATCH = 8
DEC_SEQ = 4096
PAST_LEN = 512

GRID_W = 64
HEAD_DIM = 64
BRANCH_W = D_MODEL // 4
POOL_WINDOWS = (2, 4, 8, 16)
POOL_GROUP_W = BRANCH_W // len(POOL_WINDOWS)
NAT_HEADS = BRANCH_W // HEAD_DIM
NAT_ROWS = 8
NAT_COLS = 16
CONV_WIDTH = 31
GQA_HEADS = BRANCH_W // HEAD_DIM
GQA_KV_HEADS = GQA_HEADS // 2
ROPE_THETA = 10000.0
Q_BLOCK = 128
LN_EPS = 1e-5
RMS_EPS = 1e-6
NEG_INF = -1e30
DEEPNORM_ALPHA = (2 * DEPTH) ** 0.25
DEEPNORM_BETA = (8 * DEPTH) ** -0.25
IN_SPLITS = (BRANCH_W, BRANCH_W,
             BRANCH_W, BRANCH_W, BRANCH_W, BRANCH_W,
             BRANCH_W, BRANCH_W, BRANCH_W,
             GQA_HEADS * HEAD_DIM, GQA_KV_HEADS * HEAD_DIM, GQA_KV_HEADS * HEAD_DIM, BRANCH_W)
IN_WIDTH = sum(IN_SPLITS)
SPLIT_POINTS = tuple(np.cumsum(IN_SPLITS)[:-1].tolist())

kernel_name = 'hybrid_diffusion_prefix_step'


def layer_norm(x, g, b):
    xf = x.astype(jnp.float32)
    mu = jnp.mean(xf, axis=-1, keepdims=True)
    var = jnp.mean(jnp.square(xf - mu), axis=-1, keepdims=True)
    return ((xf - mu) * lax.rsqrt(var + LN_EPS)).astype(x.dtype) * g + b


def rms_norm(x, g):
    xf = x.astype(jnp.float32)
    return (xf * lax.rsqrt(jnp.mean(xf * xf, axis=-1, keepdims=True) + RMS_EPS)).astype(x.dtype) * g


def axial_rope(x):
    L = x.shape[1]
    half = HEAD_DIM // 2
    nf = half // 2
    t = jnp.arange(L)
    inv = ROPE_THETA ** (-jnp.arange(nf, dtype=jnp.float32) * 2.0 / half)

    def rot(xa, pos):
        ang = pos.astype(jnp.float32)[:, None] * inv[None, :]
        cos = jnp.cos(ang)[None, :, None, :].astype(x.dtype)
        sin = jnp.sin(ang)[None, :, None, :].astype(x.dtype)
        x1, x2 = xa[..., :nf], xa[..., nf:]
        return jnp.concatenate([x1 * cos - x2 * sin, x2 * cos + x1 * sin], axis=-1)

    return jnp.concatenate([rot(x[..., :half], t // GRID_W), rot(x[..., half:], t % GRID_W)], axis=-1)


def blocked_attention(q, k, v):
    B, Lq, H, dh = q.shape
    KV = k.shape[2]
    G = H // KV
    qb = jnp.moveaxis(q.reshape(B, Lq // Q_BLOCK, Q_BLOCK, KV, G, dh), 1, 0)

    def one_block(qi):
        s = jnp.einsum('bqkgd,bskd->bkgqs', qi, k).astype(jnp.float32) * (dh ** -0.5)
        p = jax.nn.softmax(s, axis=-1).astype(v.dtype)
        return jnp.einsum('bkgqs,bskd->bqkgd', p, v)

    o = lax.map(one_block, qb)
    return jnp.moveaxis(o, 0, 1).reshape(B, Lq, H * dh)


def nat_attention(q, k, v, k_ctx, v_ctx, bias_tab):
    B, L, H, dh = q.shape
    rows = L // GRID_W
    wr = min(NAT_ROWS, rows)
    r = np.arange(rows)
    row_start = np.clip(r - wr // 2, 0, rows - wr)
    row_idx = row_start[:, None] + np.arange(wr)[None, :]
    col = np.arange(GRID_W)
    col_start = np.clip(col - NAT_COLS // 2, 0, GRID_W - NAT_COLS)
    col_in = (col[None, :] >= col_start[:, None]) & (col[None, :] < col_start[:, None] + NAT_COLS)
    dc = np.clip(col[None, :] - col[:, None], -(NAT_COLS - 1), NAT_COLS - 1) + NAT_COLS - 1
    dr = row_idx - r[:, None] + NAT_ROWS - 1
    scale = dh ** -0.5

    qg = q.reshape(B, rows, GRID_W, H, dh)
    kg = jnp.take(k.reshape(B, rows, GRID_W, H, dh), row_idx, axis=1)
    vg = jnp.take(v.reshape(B, rows, GRID_W, H, dh), row_idx, axis=1)
    s_loc = jnp.einsum('brchd,brjwhd->brhcjw', qg, kg).astype(jnp.float32) * scale
    bias = bias_tab[:, dr[:, :, None, None], dc[None, None, :, :]]
    bias = jnp.transpose(bias, (1, 0, 3, 2, 4)).astype(jnp.float32)
    s_loc = jnp.where(col_in[:, None, :], s_loc + bias[None], NEG_INF)
    s_ctx = jnp.einsum('brchd,bshd->brhcs', qg, k_ctx).astype(jnp.float32) * scale
    n_loc = wr * GRID_W
    s_all = jnp.concatenate([s_loc.reshape(B, rows, H, GRID_W, n_loc), s_ctx], axis=-1)
    p = jax.nn.softmax(s_all, axis=-1).astype(v.dtype)
    p_loc = p[..., :n_loc].reshape(B, rows, H, GRID_W, wr, GRID_W)
    p_ctx = p[..., n_loc:]
    o = jnp.einsum('brhcjw,brjwhd->brchd', p_loc, vg) + jnp.einsum('brhcs,bshd->brchd', p_ctx, v_ctx)
    return o.reshape(B, L, H * dh)


def pool_mix(x, w, scale):
    B, L, C = x.shape
    xf = x.astype(jnp.float32)
    cs = jnp.concatenate([jnp.zeros((B, 1, C), jnp.float32), jnp.cumsum(xf, axis=1)], axis=1)
    t = np.arange(L)
    outs = []
    for g, win in enumerate(POOL_WINDOWS):
        lo = np.clip(t - win // 2, 0, L)
        hi = np.clip(t - win // 2 + win, 0, L)
        sl = slice(g * POOL_GROUP_W, (g + 1) * POOL_GROUP_W)
        cg = cs[..., sl]
        mean = (cg[:, hi] - cg[:, lo]) / jnp.asarray(hi - lo, jnp.float32)[None, :, None]
        outs.append(mean - xf[..., sl])
    y = jnp.stack(outs, axis=2).astype(x.dtype)
    y = jnp.einsum('blgc,gcd->blgd', y, w)
    return y.reshape(B, L, C) * scale


def conv_mix(val, glu, w_dw, b_dw, g, b, w_pw):
    z = val * jax.nn.sigmoid(glu)
    z = lax.conv_general_dilated(z, w_dw[:, None, :], window_strides=(1,),
                                 padding=[(CONV_WIDTH // 2, CONV_WIDTH // 2)],
                                 dimension_numbers=('NWC', 'WIO', 'NWC'),
                                 feature_group_count=z.shape[-1]) + b_dw
    z = jax.nn.silu(layer_norm(z, g, b))
    return z @ w_pw


def trunk_layer(x, cond, p, ctx=None):
    B, L, _ = x.shape
    mod = jax.nn.silu(cond) @ p['w_mod'] + p['b_mod']
    shift, scale, gate = jnp.split(mod, 3, axis=-1)
    h = x * (1 + scale) + shift
    u = h @ p['w_in'] + p['b_in']
    (a_x, a_g, nq, nk, nv, n_g, c_v, c_glu, c_g, gq, gk, gv, g_g) = jnp.split(u, SPLIT_POINTS, axis=-1)

    a_out = pool_mix(a_x, p['pool_w'], p['pool_scale'])
    c_out = conv_mix(c_v, c_glu, p['conv_w'], p['conv_b'], p['conv_ln_g'], p['conv_ln_b'], p['conv_pw'])

    nq = nq.reshape(B, L, NAT_HEADS, HEAD_DIM)
    nk = nk.reshape(B, L, NAT_HEADS, HEAD_DIM)
    nv = nv.reshape(B, L, NAT_HEADS, HEAD_DIM)
    gq = rms_norm(gq.reshape(B, L, GQA_HEADS, HEAD_DIM), p['q_norm'])
    gk = rms_norm(gk.reshape(B, L, GQA_KV_HEADS, HEAD_DIM), p['k_norm'])
    gv = gv.reshape(B, L, GQA_KV_HEADS, HEAD_DIM)

    if ctx is None:
        n_out = blocked_attention(nq, nk, nv)
        g_out = blocked_attention(gq, gk, gv)
        new = (nk, nv, gk, gv)
    else:
        ck_n, cv_n, ck_g, cv_g = ctx
        n_out = nat_attention(nq, nk, nv, ck_n, cv_n, p['nat_bias'])
        g_out = blocked_attention(axial_rope(gq),
                                  jnp.concatenate([axial_rope(gk), ck_g], axis=1),
                                  jnp.concatenate([gv, cv_g], axis=1))
        new = None

    mixed = jnp.concatenate([a_out * jax.nn.silu(a_g), n_out * jax.nn.silu(n_g),
                             c_out * jax.nn.silu(c_g), g_out * jax.nn.silu(g_g)], axis=-1)
    out = mixed @ p['w_out'] + p['b_out']
    x = layer_norm(DEEPNORM_ALPHA * x + gate * out, p['ln_g'], p['ln_b'])
    return x, new


def setup_inputs(seed: int = 0) -> dict:
    key = jax.random.key(seed)
    ks = jax.random.split(key, 26)
    D = D_MODEL

    def nrm(k, shape, s):
        return jax.random.normal(k, shape, jnp.float32) * s

    return {
        'x_prompt': nrm(ks[0], (BATCH, SEQ, D), 1.0),
        'x_sample': nrm(ks[1], (DEC_BATCH, DEC_SEQ, D), 1.0),
        'c': nrm(ks[2], (DEC_BATCH, D), 1.0),
        'cache_nat_k': nrm(ks[3], (DEC_BATCH, DEPTH, PAST_LEN, NAT_HEADS, HEAD_DIM), 1.0),
        'cache_nat_v': nrm(ks[4], (DEC_BATCH, DEPTH, PAST_LEN, NAT_HEADS, HEAD_DIM), 1.0),
        'cache_gqa_k': nrm(ks[5], (DEC_BATCH, DEPTH, PAST_LEN, GQA_KV_HEADS, HEAD_DIM), 1.0),
        'cache_gqa_v': nrm(ks[6], (DEC_BATCH, DEPTH, PAST_LEN, GQA_KV_HEADS, HEAD_DIM), 1.0),
        'c_ctx': nrm(ks[7], (D,), 1.0),
        'w_mod': nrm(ks[8], (DEPTH, D, 3 * D), 0.5 * D ** -0.5),
        'b_mod': nrm(ks[9], (DEPTH, 3 * D), 0.01),
        'w_in': nrm(ks[10], (DEPTH, D, IN_WIDTH), D ** -0.5),
        'b_in': nrm(ks[11], (DEPTH, IN_WIDTH), 0.01),
        'pool_w': nrm(ks[12], (DEPTH, len(POOL_WINDOWS), POOL_GROUP_W, POOL_GROUP_W), POOL_GROUP_W ** -0.5),
        'pool_scale': 1.0 + nrm(ks[13], (DEPTH, BRANCH_W), 0.1),
        'nat_bias': nrm(ks[14], (DEPTH, NAT_HEADS, 2 * NAT_ROWS - 1, 2 * NAT_COLS - 1), 0.1),
        'q_norm': 1.0 + nrm(ks[15], (DEPTH, HEAD_DIM), 0.05),
        'k_norm': 1.0 + nrm(ks[16], (DEPTH, HEAD_DIM), 0.05),
        'conv_w': nrm(ks[17], (DEPTH, CONV_WIDTH, BRANCH_W), CONV_WIDTH ** -0.5),
        'conv_b': nrm(ks[18], (DEPTH, BRANCH_W), 0.01),
        'conv_ln_g': 1.0 + nrm(ks[19], (DEPTH, BRANCH_W), 0.05),
        'conv_ln_b': nrm(ks[20], (DEPTH, BRANCH_W), 0.01),
        'conv_pw': nrm(ks[21], (DEPTH, BRANCH_W, BRANCH_W), BRANCH_W ** -0.5 * DEEPNORM_BETA),
        'w_out': nrm(ks[22], (DEPTH, D, D), D ** -0.5 * DEEPNORM_BETA),
        'b_out': nrm(ks[23], (DEPTH, D), 0.01),
        'ln_g': 1.0 + nrm(ks[24], (DEPTH, D), 0.05),
        'ln_b': nrm(ks[25], (DEPTH, D), 0.01),
    }


def reference(x_prompt, x_sample, c, cache_nat_k, cache_nat_v, cache_gqa_k, cache_gqa_v, c_ctx,
              w_mod, b_mod, w_in, b_in, pool_w, pool_scale, nat_bias, q_norm, k_norm,
              conv_w, conv_b, conv_ln_g, conv_ln_b, conv_pw, w_out, b_out, ln_g, ln_b):
    cond_ctx = c_ctx[None, None, :]
    cond_lat = c[:, None, :]
    y_prompt = x_prompt
    y_sample = x_sample
    nat_k, nat_v, gqa_k, gqa_v = [], [], [], []
    for l in range(DEPTH):
        p = {'w_mod': w_mod[l], 'b_mod': b_mod[l], 'w_in': w_in[l], 'b_in': b_in[l],
             'pool_w': pool_w[l], 'pool_scale': pool_scale[l], 'nat_bias': nat_bias[l],
             'q_norm': q_norm[l], 'k_norm': k_norm[l], 'conv_w': conv_w[l], 'conv_b': conv_b[l],
             'conv_ln_g': conv_ln_g[l], 'conv_ln_b': conv_ln_b[l], 'conv_pw': conv_pw[l],
             'w_out': w_out[l], 'b_out': b_out[l], 'ln_g': ln_g[l], 'ln_b': ln_b[l]}
        y_prompt, (nk, nv, gk, gv) = trunk_layer(y_prompt, cond_ctx, p)
        nat_k.append(nk)
        nat_v.append(nv)
        gqa_k.append(gk)
        gqa_v.append(gv)
        ctx = (cache_nat_k[:, l], cache_nat_v[:, l], cache_gqa_k[:, l], cache_gqa_v[:, l])
        y_sample, _ = trunk_layer(y_sample, cond_lat, p, ctx)
    new_nat_k = jnp.stack(nat_k, axis=1)
    new_nat_v = jnp.stack(nat_v, axis=1)
    new_gqa_k = jnp.stack(gqa_k, axis=1)
    new_gqa_v = jnp.stack(gqa_v, axis=1)
    return (y_prompt, y_sample, new_nat_k, new_nat_v, new_gqa_k, new_gqa_v)
```

```python
import math
from contextlib import ExitStack

import numpy as np
import concourse.bass as bass
import concourse.mybir as mybir
from concourse.bass_utils import run_bass_kernel_spmd

F32 = mybir.dt.float32
BF16 = mybir.dt.bfloat16
AF = mybir.ActivationFunctionType
ALU = mybir.AluOpType

D = 1024
NL = 4
GW = 64
HD = 64
PAST = 512
LN_EPS = 1e-5
RMS_EPS = 1e-6
ALPHA = (2 * NL) ** 0.25
N_CORES = 8
PADW = 16

ENGS = ("pe", "act", "dve", "pool", "sp")
EPOCH = 12000
NRING = 12


class Op:
    __slots__ = ("eng", "idx", "gid", "fn", "deps", "gdeps", "dma", "sig", "signo", "dmano", "busy", "lat")


HOP_NS = 850.0


class Sched:
    def __init__(self):
        self.all = []
        self.ops = {e: [] for e in ENGS}
        self.lastw = {}
        self.readers = {}
        self.outkeys = []
        self.stopped = False
        self.reorder = True

    def add(self, eng, fn, reads=(), writes=(), dma=False, cost=300.0, lat=None):
        if self.stopped:
            return None
        op = Op()
        op.eng = eng
        op.gid = len(self.all)
        op.idx = -1
        op.fn = fn
        op.dma = dma
        op.sig = False
        op.signo = -1
        op.dmano = -1
        op.busy = float(cost)
        op.lat = float(cost if lat is None else lat)
        deps = set()
        for k in reads:
            lw = self.lastw.get(k)
            if lw is not None:
                deps.add(lw)
            if isinstance(k, tuple) and k[0] == "bk":
                for r in self.readers.get(k, ()):
                    if self.all[r].eng != eng:
                        deps.add(r)
        for k in writes:
            lw = self.lastw.get(k)
            if lw is not None:
                deps.add(lw)
            for r in self.readers.get(k, ()):
                deps.add(r)
        deps.discard(op.gid)
        op.gdeps = deps
        self.all.append(op)
        for k in reads:
            self.readers.setdefault(k, []).append(op.gid)
        for k in writes:
            self.lastw[k] = op.gid
            self.readers[k] = []
        return op

    def schedule(self):
        import heapq
        n = len(self.all)
        if not self.reorder:
            for op in self.all:
                op.idx = len(self.ops[op.eng])
                self.ops[op.eng].append(op)
        else:
            succ = [[] for _ in range(n)]
            indeg = [0] * n
            for op in self.all:
                indeg[op.gid] = len(op.gdeps)
                for d in op.gdeps:
                    succ[d].append(op.gid)
            bl = [0.0] * n
            for op in reversed(self.all):
                m = 0.0
                for s_ in succ[op.gid]:
                    if bl[s_] > m:
                        m = bl[s_]
                bl[op.gid] = op.lat + m
            finish = [0.0] * n
            ready_t = [0.0] * n
            future = {e: [] for e in ENGS}
            avail = {e: [] for e in ENGS}
            free = {e: 0.0 for e in ENGS}
            for op in self.all:
                if indeg[op.gid] == 0:
                    heapq.heappush(future[op.eng], (0.0, op.gid))
            done = 0
            while done < n:
                best = None
                for e in ENGS:
                    f, a = future[e], avail[e]
                    while f and f[0][0] <= free[e]:
                        g_ = heapq.heappop(f)[1]
                        heapq.heappush(a, (-bl[g_], g_))
                    if a:
                        cand = (free[e], a[0][1], e, True)
                    elif f:
                        cand = (f[0][0], f[0][1], e, False)
                    else:
                        continue
                    if best is None or cand[:2] < best[:2]:
                        best = cand
                assert best is not None, "scheduler: dependency cycle"
                st, g, e, from_avail = best
                if from_avail:
                    heapq.heappop(avail[e])
                else:
                    heapq.heappop(future[e])
                op = self.all[g]
                op.idx = len(self.ops[e])
                self.ops[e].append(op)
                free[e] = st + op.busy
                finish[g] = st + op.lat
                done += 1
                for s_ in succ[g]:
                    t = finish[g] + (HOP_NS if self.all[s_].eng != e else 300.0)
                    if t > ready_t[s_]:
                        ready_t[s_] = t
                    indeg[s_] -= 1
                    if indeg[s_] == 0:
                        heapq.heappush(future[self.all[s_].eng], (ready_t[s_], s_))
            self.model_ns = max(finish) if n else 0.0
        for e in ENGS:
            k = 0
            for op in self.ops[e]:
                if op.dma:
                    op.dmano = k
                    k += 1
        for op in self.all:
            op.deps = {(self.all[d].eng, self.all[d].idx) for d in op.gdeps
                       if not (op.eng == "pe" and self.all[d].eng == "pe")}

    def emit(self, nc, stack):
        self.schedule()
        ndma = {e: sum(1 for o in self.ops[e] if o.dma) for e in ENGS}
        for e in ENGS:
            for op in self.ops[e]:
                for (se, si) in op.deps:
                    so = self.ops[se][si]
                    if not so.dma:
                        so.sig = True
        esems = {}
        for e in ENGS:
            n = 0
            for op in self.ops[e]:
                if op.sig:
                    op.signo = n
                    n += 1
            esems[e] = [stack.enter_context(nc.semaphore(f"s_{e}_{i}")) for i in range(n // EPOCH + 1)]
        rsems = {}
        for e in ENGS:
            if ndma[e]:
                rsems[e] = [stack.enter_context(nc.semaphore(f"r_{e}_{i}")) for i in range(NRING)]
        block = stack.enter_context(nc.Block())
        deco = {"pe": block.tensor, "act": block.scalar, "dve": block.vector, "pool": block.gpsimd, "sp": block.sync}

        def resolve(se, si):
            so = self.ops[se][si]
            if so.dma:
                return (se, "r", so.dmano % NRING), rsems[se][so.dmano % NRING], 16 * (so.dmano // NRING + 1)
            return (se, "s", so.signo // EPOCH), esems[se][so.signo // EPOCH], so.signo % EPOCH + 1

        def body_for(e):
            def body(eng):
                waited = {}

                def wait(key, sem, val):
                    if waited.get(key, 0) >= val:
                        return
                    waited[key] = val
                    eng.wait_ge(sem, val)

                for op in self.ops[e]:
                    need = {}
                    for (se, si) in op.deps:
                        k_, sem_, v_ = resolve(se, si)
                        if k_ not in need or need[k_][1] < v_:
                            need[k_] = (sem_, v_)
                    for k_ in sorted(need):
                        wait(k_, need[k_][0], need[k_][1])
                    if op.dma and op.dmano >= NRING:
                        slot = op.dmano % NRING
                        wait((e, "r", slot), rsems[e][slot], 16 * (op.dmano // NRING))
                    ins = op.fn(eng)
                    if op.dma:
                        ins.then_inc(rsems[e][op.dmano % NRING], 16)
                    elif op.sig:
                        ins.then_inc(esems[e][op.signo // EPOCH], 1)
            return body

        for e in ENGS:
            deco[e](body_for(e))


def param_layout(nl):
    off = {}
    n = 0
    for name, cnt in (("bmod", 24), ("bin", 24), ("pscale", 2), ("convw", 62), ("convb", 2), ("clng", 2),
                      ("clnb", 2), ("bout", 8), ("lng", 8), ("lnb", 8), ("qn", 1), ("kn", 1)):
        off[name] = (n, cnt)
        n += cnt * nl
    for name, cnt in (("cond", 16), ("invwin", 2), ("edgeL", 16), ("edgeR", 16)):
        off[name] = (n, cnt)
        n += cnt
    return off, n


_C = dict(a_x=0, a_g=256, nq=512, nk=768, nv=1024, n_g=1280, c_v=1536, c_glu=1792, c_g=2048, gq=2304, gk=2560,
          gv=2688, g_g=2816)


def col_perm():
    r = lambda a, n: list(range(a, a + n))
    A = r(_C["a_x"], 256) + r(_C["nk"], 256) + r(_C["c_v"], 256) + r(_C["c_glu"], 256) + r(_C["gk"], 128) \
        + r(_C["nv"], 256) + r(_C["gv"], 128)
    gq = []
    for h in (0, 2, 1, 3):
        gq += r(_C["gq"] + 64 * h, 64)
    B = r(_C["a_g"], 256) + r(_C["n_g"], 256) + r(_C["c_g"], 256) + r(_C["g_g"], 256) + r(_C["nq"], 256) + gq
    return np.array(A + B, dtype=np.int64)


def nat_rs(qr, R):
    return min(max(qr - 4, 0), R - 8)


class _StopBuild(Exception):
    pass


def build_program(nl=NL, LS=4096, NPS=4, LP=256, dbg=False, limit=None):
    nc = bass.Bass("TRN2", target_bir_lowering=False)
    S = Sched()
    PO, NPAR = param_layout(nl)
    R = LS // GW
    TP = LP * NPS
    NKT_S = LS // 128 + PAST // 128
    stack = ExitStack()

    def din(name, shape, dt=F32):
        return nc.dram_tensor(name, list(shape), dt, kind="ExternalInput").ap()

    def dout(name, shape, dt=F32):
        return nc.dram_tensor(name, list(shape), dt, kind="ExternalOutput").ap()

    def dscr(name, shape, dt):
        return nc.dram_tensor(name, list(shape), dt).ap()

    xT_p = din("xT_p", [D, TP])
    xT_s = din("xT_s", [D, LS])
    params_d = din("params", [128, NPAR])
    wmod_d = din("w_mod", [nl, D, 3 * D])
    win_d = din("w_in_p", [nl, D, 3 * D])
    brow_d = din("b_row", [1, nl * 384])
    wout_d = din("w_out", [nl, D, D])
    pwbd_d = din("pool_bd", [nl, 2, 128, 128])
    cpw_d = din("conv_pw", [nl, 256, 256])
    bm_d = din("nat_bm", [nl, 4, 128, 14 * 64])
    cmask_d = din("colmask", [128, 64])
    nkc_d = din("nkc", [nl, 256, PAST])
    nvc_d = din("nvc", [nl, PAST, 256])
    gkc_d = din("gkc", [nl, 128, PAST])
    gvc_d = din("gvc", [nl, PAST, 128])
    rope_d = din("rope", [2, 128, LS])
    consts_d = din("consts", [4, 128, 128])

    yT_p = dout("yT_p", [D, TP])
    yT_s = dout("yT_s", [D, LS])
    nkT_o = dout("nkT_o", [nl, 256, TP])
    gkT_o = dout("gkT_o", [nl, 128, TP])
    nv_o = dout("nv_o", [nl, TP, 256])
    gv_o = dout("gv_o", [nl, TP, 128])
    dbg_out = {}

    xs = {("P", 0): dscr("xsP0", [D, TP], F32), ("P", 1): dscr("xsP1", [D, TP], F32),
          ("S", 0): dscr("xsS0", [D, LS], F32), ("S", 1): dscr("xsS1", [D, LS], F32)}
    SEQW = {"P": LP + 2 * PADW, "S": LS + 2 * PADW}
    axs = {"P": dscr("axP", [256, NPS * SEQW["P"]], BF16), "S": dscr("axS", [256, SEQW["S"]], BF16)}
    zs = {"P": dscr("zP", [256, NPS * SEQW["P"]], BF16), "S": dscr("zS", [256, SEQW["S"]], BF16)}
    nks = dscr("nkS", [256, LS], BF16)
    nvs = dscr("nvS", [LS, 256], BF16)

    def sb(name, shape, dt):
        return stack.enter_context(nc.sbuf_tensor(name, list(shape), dt))

    TM = 512
    WPAD = TM + 4 * PADW
    GK = sb("GK", [128, max(LS + PAST, TP)], BF16)
    GV = sb("GV", [128, max(NKT_S, TP // 128), 2, 128], BF16)
    WA = sb("WA", [128, 8, 1536], BF16)
    WB = sb("WB", [128, 8, 1536], BF16)
    WO = sb("WO", [128, 8, D], BF16)
    CPW = sb("CPW", [128, 2, 256], BF16)
    PWBD = sb("PWBD", [128, 2, 128], BF16)
    EM = sb("EM", [128, 4, 14 * 64], BF16)
    NKC = sb("NKC", [128, 2, PAST], BF16)
    n_p = 2 * TP + (TP // 128) * 4 * 128
    NVC_OFF = 5760
    ONES_OFF = NVC_OFF + (PAST // 128) * 256
    n_s = ONES_OFF + 64
    assert n_p <= ONES_OFF
    JR = sb("JR", [128, max(n_p, n_s)], BF16)
    NVC = JR[:, NVC_OFF:ONES_OFF].rearrange("p (i c) -> p i c", c=256)

    NKP = JR[:, 0:2 * TP].rearrange("p (t n) -> p t n", t=2)
    NVP = JR[:, 2 * TP:n_p].rearrange("p (i h c) -> p i h c", h=4, c=128)
    NKW = JR[:, 0:1920].rearrange("p (t n) -> p t n", t=2)
    NVWe = JR[:, 1920:1920 + 2048].rearrange("p (j c) -> p j c", c=256)
    NVWo = JR[:, 3968:3968 + 1792].rearrange("p (j c) -> p j c", c=256)
    IDB = sb("IDB", [128, 128], BF16)
    ONESB = sb("ONESB", [128, 128], BF16)
    BD64 = sb("BD64", [128, 128], BF16)
    PERM = sb("PERM", [128, 128], F32)
    BROW = sb("BROW", [128, 384], BF16)
    CMASK = sb("CMASK", [128, 64], BF16)
    PRM = sb("PRM", [128, NPAR], F32)
    MODT = sb("MODT", [128, nl, 24, 2], F32)
    LV2 = sb("LV", [128, 2, 4, 8], F32)
    SC = sb("SC", [128, 8, 2], F32)
    EPSV = sb("EPSV", [128, 2], F32)
    XB = sb("XB", [128, 8, TM], F32)
    H = sb("H", [128, 8, TM], BF16)
    GATES = sb("GATES", [128, 8, TM], BF16)
    NQM = sb("NQM", [128, 4, TM], BF16)
    GQM = sb("GQM", [128, 4, TM], BF16)
    MIX = sb("MIX", [128, 8, TM], BF16)
    AXW = sb("AXW", [128, 2, WPAD], BF16)
    ZW = sb("ZW", [128, 2, WPAD], BF16)
    NDG = 5
    DG = sb("DG", [128, NDG, 128], BF16)
    NPT = 3
    PT = sb("PT", [128, NPT, TM], BF16)
    COS = sb("COS", [128, TM], F32)
    SIN = sb("SIN", [128, TM], F32)
    NFS = 9
    FS = sb("FS", [128, NFS, WPAD], F32)
    SQ = sb("SQ", [128, TM], BF16)
    VBF = sb("VBF", [128, TM], BF16)
    ZN = sb("ZN", [128, 2, TM], BF16)
    YP = sb("YP", [128, 2, TM], BF16)
    ZERO = sb("ZERO", [128, 2 * PADW], BF16)
    DUMMY = sb("DUMMY", [128, 8], F32)

    G32 = GATES[:, :, :].rearrange("p a w -> p (a w)").bitcast(F32).rearrange("p (s w) -> p s w", w=TM)
    M32 = MIX[:, :, :].rearrange("p a w -> p (a w)").bitcast(F32).rearrange("p (s w) -> p s w", w=TM)
    FSV = FS[:, :, :].rearrange("p a w -> p (a w)")[:, 0:8 * TM].rearrange("p (k w) -> p k w", w=TM)
    banks = [stack.enter_context(nc.psum_tensor(f"bk{i}", [128, 512], F32)) for i in range(8)]

    def P(name, l=0, j=0):
        o, c = PO[name]
        return o + l * c + j

    def pv(name, l=0, j=0, n=1):
        o = P(name, l, j)
        return PRM[:, o:o + n]

    rot = {"mm": [0, 1, 2, 3], "o": [4, 5], "st": [6, 7]}
    rpos = {k: 0 for k in rot}

    def bank(role):
        i = rot[role][rpos[role] % len(rot[role])]
        rpos[role] += 1
        return banks[i], ("bk", i)

    cnt = {"pt": 0, "dg": 0}

    stg_n = [0]

    def stage(name):
        stg_n[0] += 1
        if limit is not None and stg_n[0] >= limit and not S.stopped:
            print("STOP at stage", stg_n[0], name)
            S.stopped = True

    def fsz(ap):
        n = 1
        for d in ap.shape[1:]:
            n *= d
        return n

    def mm(out, lhsT, rhs, start, stop, reads, writes):
        n = fsz(out)
        c = max(n / 2.4, 90.0) * (4.0 if rhs.dtype == F32 else 1.0)
        S.add("pe", lambda e: e.matmul(out, lhsT=lhsT, rhs=rhs, start=start, stop=stop), reads, writes, cost=c,
              lat=c + 160.0)

    def act(out, in_, func, reads, writes, bias=None, scale=None):
        kw = {}
        if bias is not None:
            kw["bias"] = bias
        if scale is not None:
            kw["scale"] = scale
        S.add("act", lambda e: e.activation(out=out, in_=in_, func=func, **kw), reads, writes,
              cost=140.0 + 0.68 * fsz(out))

    def ecost(eng, kind, n):
        if eng == "pool":
            return {"tt": 150 + 1.7 * n, "ts": 230 + 1.4 * n, "cp": 150 + 2.5 * n}[kind]
        return {"tt": 100 + 0.7 * n, "ts": 100 + 0.6 * n, "cp": 100 + 0.5 * n, "stt": 120 + 1.1 * n}[kind]

    def tt(eng, out, in0, in1, op, reads, writes):
        S.add(eng, lambda e: e.tensor_tensor(out=out, in0=in0, in1=in1, op=op), reads, writes,
              cost=ecost(eng, "tt", fsz(out)))

    def ts(eng, out, in0, s1, s2, op0, op1, reads, writes):
        c = ecost(eng, "ts", fsz(out))
        if op1 is None:
            S.add(eng, lambda e: e.tensor_scalar(out=out, in0=in0, scalar1=s1, scalar2=None, op0=op0), reads, writes,
                  cost=c)
        else:
            S.add(eng, lambda e: e.tensor_scalar(out=out, in0=in0, scalar1=s1, scalar2=s2, op0=op0, op1=op1),
                  reads, writes, cost=c)

    def stt(out, in0, scalar, in1, op0, op1, reads, writes):
        S.add("dve", lambda e: e.scalar_tensor_tensor(out=out, in0=in0, scalar=scalar, in1=in1, op0=op0, op1=op1),
              reads, writes, cost=ecost("dve", "stt", fsz(out)))

    def cp(eng, out, in_, reads, writes):
        if eng == "act":
            S.add(eng, lambda e: e.copy(out=out, in_=in_), reads, writes, cost=140.0 + 0.68 * fsz(out))
        else:
            S.add(eng, lambda e: e.tensor_copy(out=out, in_=in_), reads, writes, cost=ecost(eng, "cp", fsz(out)))

    def dma(q, out, in_, reads, writes):
        nbytes = fsz(out) * out.shape[0] * (4 if out.dtype == F32 else 2)
        S.add(q, lambda e: e.dma_start(out=out, in_=in_), reads, writes, dma=True,
              cost=(150.0 if q == "sp" else 1500.0), lat=2500.0 + nbytes / 120.0)

    def okey():
        k = ("out", len(S.outkeys))
        S.outkeys.append(k)
        return k

    def dbg_dump(name, ap, shape, key):
        if not dbg:
            return
        d = dout("dbg_" + name, shape, ap.dtype)
        dbg_out[name] = d
        dma("sp", d, ap, [key] if not isinstance(key, list) else key, [okey()])

    dma("sp", PRM[:, :], params_d[:, :], [], ["PRM"])
    dma("pool", IDB[:, :], consts_d[0], [], ["IDB"])
    dma("pool", ONESB[:, :], consts_d[1], [], ["ONESB"])
    dma("pool", BD64[:, :], consts_d[2], [], ["BD64"])
    dma("sp", PERM[:, :], consts_d[3], [], ["PERM"])
    dma("pool", CMASK[:, :], cmask_d[:, :], [], ["CMASK"])
    S.add("dve", lambda e: e.memset(ZERO[:, :], 0.0), [], ["ZERO"])
    S.add("dve", lambda e: e.memset(EPSV[:, 0:1], LN_EPS), [], ["EPSV"])
    S.add("dve", lambda e: e.memset(EPSV[:, 1:2], RMS_EPS), [], ["EPSV"])
    S.add("pool", lambda e: e.memset(GV[:, :, :, 64:128], 1.0), [], [("GV", i) for i in range(GV.shape[1])])
    S.add("pool", lambda e: e.memset(NVP[:, :, :, 64:128], 1.0), [], [("NVP", i) for i in range(TP // 128)])
    S.add("pool", lambda e: e.memset(BROW[:, :], 0.0), [], ["BROW"])
    S.add("pool", lambda e: e.memset(NQM[:, :, :], 0.0), [], [("NQ", h) for h in range(4)])
    S.add("pool", lambda e: e.memset(GQM[:, :, :], 0.0), [], [("GQ", h) for h in range(4)])
    for job, nseq in (("P", NPS), ("S", 1)):
        L = LP if job == "P" else LS
        for s in range(nseq):
            for scr in (axs[job], zs[job]):
                for t in range(2):
                    b0 = s * SEQW[job]
                    dma("sp", scr[t * 128:(t + 1) * 128, b0:b0 + PADW], ZERO[:, 0:PADW], ["ZERO"],
                        [("pad", job, s, id(scr), t, 0)])
                    dma("sp", scr[t * 128:(t + 1) * 128, b0 + PADW + L:b0 + 2 * PADW + L], ZERO[:, 0:PADW],
                        ["ZERO"], [("pad", job, s, id(scr), t, 1)])

    stage("prologue-loads")
    o_c, _ = PO["cond"]
    act(SC[:, :, :].rearrange("p k j -> p (k j)"), PRM[:, o_c:o_c + 16], AF.Silu, ["PRM"], ["SC"])
    idle_tiles = GV.shape[1] - TP // 128
    bgw = 128 if idle_tiles >= 8 else 0
    NBG = min(3, idle_tiles // 8)
    ob, _ = PO["bmod"]

    def mod_layer(l, bg):
        cw = bgw if bg else 512
        nch = 3072 // cw
        for ch in range(nch):
            if bg:
                r0 = TP // 128 + 8 * ((l * nch + ch) % NBG)
                WM = GV[:, r0:r0 + 8, :, :].rearrange("p i k c -> p (i k c)").bitcast(F32).rearrange(
                    "p (k w) -> p k w", w=cw)
                wkeys = [("GV", r0 + i) for i in range(8)]
            else:
                par = ch % 2
                WM = XB if par == 0 else FSV
                wkeys = [("XB", k) for k in range(8)] if par == 0 else [("FS", i) for i in range(NFS)]
            dma("sp", WM[:, :, :], wmod_d[l].rearrange("(kt p) n -> p kt n", p=128)[:, :, ch * cw:(ch + 1) * cw],
                [], wkeys)
            bk, bkk = bank("st")
            ng = cw // 128
            for c4 in range(ng):
                for kt in range(8):
                    mm(bk[:, c4 * 2:c4 * 2 + 2], WM[:, kt, c4 * 128:(c4 + 1) * 128], SC[:, kt, :], kt == 0, kt == 7,
                       wkeys + ["SC"], [bkk])
            ct0 = ch * ng
            cp("dve", MODT[:, l, ct0:ct0 + ng, :].rearrange("p c j -> p (c j)"), bk[:, 0:2 * ng], [bkk],
               [("MODT", l)])
        for j in range(2):
            tt("dve", MODT[:, l, :, j], MODT[:, l, :, j], PRM[:, ob + l * 24:ob + (l + 1) * 24], ALU.add,
               [("MODT", l), "PRM"], [("MODT", l)])

    mod_layer(0, False)
    for l in range(1, nl):
        mod_layer(l, bgw > 0)
    if bgw > 0 and nl > 1:
        nt = 8 * NBG
        S.add("pool", lambda e: e.memset(GV[:, TP // 128:TP // 128 + nt, :, 64:128], 1.0), [],
              [("GV", TP // 128 + i) for i in range(nt)], cost=1500.0)

    stage("prologue-mod")
    def x_src(job, l):
        if l == 0:
            return xT_p if job == "P" else xT_s
        return xs[(job, l % 2)]

    def load_x(job, l, t0, T):
        dma("sp", XB[:, :, 0:T], x_src(job, l).rearrange("(kt p) t -> p kt t", p=128)[:, :, t0:t0 + T],
            [("xs", job, l % 2, t0 // T)] if l > 0 else [], [("XB", k) for k in range(8)])

    HK = {"H": lambda kt: [("H", kt)], "MIX": lambda kt: [("MIX", kt, 0), ("MIX", kt, 1)]}
    HB = {"H": H, "MIX": MIX}

    def h_from_xb(T, hb):
        for kt in range(8):
            ts("pool", HB[hb][:, kt, 0:T], XB[:, kt, 0:T], LV[:, 0, kt:kt + 1], LV[:, 1, kt:kt + 1], ALU.mult, ALU.add,
               [("XB", kt), LVK], HK[hb](kt))

    def stage_h(job, l, t0, T):
        src = x_src(job, l).rearrange("(kt p) t -> p kt t", p=128)
        rk = [("xs", job, l % 2, t0 // T)] if l > 0 else []
        gk = [("G", j) for j in range(8)]
        mk = [("MIX", j, hh) for j in range(8) for hh in range(2)]
        dma("sp", G32[:, :, 0:T], src[:, 0:4, t0:t0 + T], rk, gk)
        dma("sp", M32[:, :, 0:T], src[:, 4:8, t0:t0 + T], rk, mk)
        for kt in range(8):
            st_, sk = (G32, gk) if kt < 4 else (M32, mk)
            ts("pool", H[:, kt, 0:T], st_[:, kt % 4, 0:T], LV[:, 0, kt:kt + 1], LV[:, 1, kt:kt + 1], ALU.mult, ALU.add,
               sk + [LVK], [("H", kt)])

    def proj_fm(W, wkey, ct, T, hb="H"):
        bk, bkk = bank("mm")
        for kt in range(8):
            mm(bk[:, 0:T], W[:, kt, ct * 128:(ct + 1) * 128], HB[hb][:, kt, 0:T], kt == 0, kt == 7,
               [wkey] + HK[hb](kt), [bkk])
        return bk, bkk

    def rms_rope(bk, bkk, bias_ap, gain_ap, T, rope, dst, fout=None):
        G0, G1, R1 = FS[:, 0, 0:T], FS[:, 1, 0:T], FS[:, 2, 0:T]
        ts("dve", G0, bk[:, 0:T], bias_ap, None, ALU.add, None, [bkk, "PRM"], [("FS", 0)])
        stage("rr-ts")
        act(SQ[:, 0:T], bk[:, 0:T], AF.Square, [bkk, "PRM"], ["SQ"], bias=bias_ap)
        stage("rr-square")
        b2, b2k = bank("st")
        mm(b2[:, 0:T], BD64[:, :], SQ[:, 0:T], True, True, ["BD64", "SQ"], [b2k])
        act(R1, b2[:, 0:T], AF.Ln, [b2k, "EPSV"], [("FS", 2)], bias=EPSV[:, 1:2], scale=1.0 / HD)
        stage("rr-ln")
        act(R1, R1, AF.Exp, [("FS", 2)], [("FS", 2)], scale=-0.5)
        stage("rr-exp")
        stt(G1, G0, gain_ap, R1, ALU.mult, ALU.mult, [("FS", 0), ("FS", 2), "PRM"], [("FS", 1)])
        stage("rr-stt")
        if fout is not None:
            dma("sp", fout, G1, [("FS", 1)], [okey()])
        if not rope:
            for (ps, d_ap, dk) in dst:
                cp("pool", d_ap, G1[ps, :], [("FS", 1)], dk)
            return
        b3, b3k = bank("st")
        mm(b3[:, 0:T], PERM[:, :], G1, True, True, ["PERM", ("FS", 1)], [b3k])
        R2, R3 = FS[:, 3, 0:T], FS[:, 4, 0:T]
        tt("dve", R2, G1, COS[:, 0:T], ALU.mult, [("FS", 1), "ROPE"], [("FS", 3)])
        tt("dve", R3, b3[:, 0:T], SIN[:, 0:T], ALU.mult, [b3k, "ROPE"], [("FS", 4)])
        for (ps, d_ap, dk) in dst:
            tt("pool", d_ap, R2[ps, :], R3[ps, :], ALU.add, [("FS", 3), ("FS", 4)], dk)

    def attn_finish(ob_, obk, po, db_, dbk, pd, T, mt, pb, gate_j, c0=0):
        RC, TMP = FS[:, 5, 0:T], FS[:, 6, 0:T]
        qo = slice(po, po + 64)
        qb = slice(pb, pb + 64)
        hk = "lo" if po == 0 else "hi"
        act(RC[qo, :], db_[pd:pd + 64, 0:T], AF.Ln, [dbk], [("FS", 5, hk)])
        act(RC[qo, :], RC[qo, :], AF.Exp, [("FS", 5, hk)], [("FS", 5, hk)], scale=-1.0)
        tt("dve", TMP[qo, :], ob_[qo, 0:T], RC[qo, :], ALU.mult, [obk, ("FS", 5, hk)], [("FS", 6, hk)])
        if po != pb:
            hk2 = "lo" if pb == 0 else "hi"
            cp("act", TMP[qb, :], TMP[qo, :], [("FS", 6, hk)], [("FS", 6, hk2)])
            hk = hk2
        tt("dve", MIX[qb, mt, c0:c0 + T], TMP[qb, :], GATES[qb, gate_j, c0:c0 + T], ALU.mult,
           [("FS", 6, hk), ("G", gate_j)], [("MIX", mt, pb // 64)])

    def full_attn(T, nkt, qf, kf, vf, mt, pb, gate_j, c0=0):
        ob_, obk = bank("o")
        q_ap, qk = qf()
        sb_ = {}

        def qk_exp(i):
            bk, bkk = bank("mm")
            k_ap, kk = kf(i)
            mm(bk[:, 0:T], k_ap, q_ap, True, True, qk + kk, [bkk])
            slot = cnt["pt"] % NPT
            cnt["pt"] += 1
            act(PT[:, slot, 0:T], bk[:, 0:T], AF.Exp, [bkk], [("PT", slot)], scale=HD ** -0.5)
            sb_[i] = slot

        def pv_(i):
            v_ap, vk = vf(i)
            slot = sb_[i]
            mm(ob_[:, 0:T], v_ap, PT[:, slot, 0:T], i == 0, i == nkt - 1, vk + [("PT", slot)], [obk])

        LOOK = 2
        for i in range(nkt + LOOK):
            if i < nkt:
                qk_exp(i)
            if i >= LOOK:
                pv_(i - LOOK)
        attn_finish(ob_, obk, 0, ob_, obk, 64, T, mt, pb, gate_j, c0)

    assert NPS % 2 == 0 and LP == 256
    jobs = [("P", 2 * LP, NPS // 2, LP), ("S", 512, LS // 512, LS)]
    seq_jl = [(jb[0], l) for jb in jobs for l in range(nl)]

    def load_weights_A(l):
        dma("pool", WA[:, :, :], win_d[l].rearrange("(kt p) n -> p kt n", p=128)[:, :, 0:1536], [], ["WA"])
        dma("pool", BROW[0:1, :], brow_d[:, l * 384:(l + 1) * 384], [], ["BROW"])

    def load_weights_B(l):
        dma("pool", WB[:, :, :], win_d[l].rearrange("(kt p) n -> p kt n", p=128)[:, :, 1536:3072], [], ["WB"])
        dma("pool", WO[:, :, :], wout_d[l].rearrange("(kt p) n -> p kt n", p=128), [], ["WO"])
        dma("pool", CPW[:, :, :], cpw_d[l].rearrange("(t p) n -> p t n", p=128), [], ["CPW"])
        dma("pool", PWBD[:, :, :], pwbd_d[l].rearrange("t p n -> p t n"), [], ["PWBD"])

    load_weights_A(0)
    for (job, T, NB, L) in jobs:
        jc = 0 if job == "P" else 1
        if job == "S":
            S.add("pool", lambda e: e.memset(DUMMY[:, 0:1], 0.0), [],
                  [("NKP", x) for x in range(NPS)] + [("NVP", x) for x in range(TP // 128)]
                  + [("NKW", 0), ("NKW", 1), "NVWe", "NVWo", "NVC"])
        for l in range(nl):
            lvp = seq_jl.index((job, l)) % 2
            LV = LV2[:, lvp]
            LVK = ("LV", lvp)
            ts("dve", LV[:, 0, :], MODT[:, l, 8:16, jc], 1.0, None, ALU.add, None, [("MODT", l)], [LVK])
            cp("dve", LV[:, 1, :], MODT[:, l, 0:8, jc], [("MODT", l)], [LVK])
            cp("dve", LV[:, 2, :], MODT[:, l, 16:24, jc], [("MODT", l)], [LVK])
            tt("dve", LV[:, 3, :], MODT[:, l, 16:24, jc], PRM[:, P("bout", l):P("bout", l) + 8], ALU.mult,
               [("MODT", l), "PRM"], [LVK])
            load_weights_B(l)
            if job == "S":
                dma("pool", NKC[:, :, :], nkc_d[l].rearrange("(t p) s -> p t s", p=128), [], ["NKC"])
                dma("pool", NVC[:, :, :], nvc_d[l].rearrange("(i p) c -> p i c", p=128), [], ["NVC"])
                dma("pool", GK[:, LS:LS + PAST], gkc_d[l], [], [("GK", "ctx")])
                for i in range(PAST // 128):
                    dma("pool", GV[:, LS // 128 + i, :, 0:64],
                        gvc_d[l][i * 128:(i + 1) * 128, :].rearrange("p (k d) -> p k d", k=2), [],
                        [("GV", LS // 128 + i)])
                for h in range(4):
                    F = FS[:, 7:9, :].rearrange("p a w -> p (a w)")[:, 0:896]
                    dma("sp", F, bm_d[l, h], [], [("FS", 7), ("FS", 8)])
                    act(F, F, AF.Exp, [("FS", 7), ("FS", 8)], [("FS", 7), ("FS", 8)])
                    tt("dve", EM[:, h, :].rearrange("p (m c) -> p m c", c=64), F.rearrange("p (m c) -> p m c", c=64),
                       CMASK[:, :].unsqueeze(1).to_broadcast([128, 14, 64]), ALU.mult,
                       [("FS", 7), ("FS", 8), "CMASK"], [("EM", h)])

            stage(f"{job}{l}-setup")
            for b in range(NB):
                t0 = b * T
                CW = L if job == "P" else T
                NCH = T // CW
                sbase = (b * NCH * SEQW["P"]) if job == "P" else 0
                tin = 0 if job == "P" else t0

                def scr_dst(scr, t):
                    if NCH == 1:
                        return scr[t * 128:(t + 1) * 128, sbase + PADW + tin:sbase + PADW + tin + T]
                    return scr[t * 128:(t + 1) * 128, sbase:sbase + NCH * (CW + 2 * PADW)].rearrange(
                        "p (s w) -> p s w", w=CW + 2 * PADW)[:, :, PADW:PADW + CW]

                def chunked(ap):
                    return ap if NCH == 1 else ap.rearrange("p (s w) -> p s w", w=CW)
                hb = "H" if b % 2 == 0 else "MIX"
                if b == 0:
                    stage_h(job, l, t0, T)
                else:
                    load_x(job, l, t0, T)
                    h_from_xb(T, hb)
                if job == "S":
                    dma("sp", COS[:, 0:T], rope_d[0][:, t0:t0 + T], [], ["ROPE"])
                    dma("sp", SIN[:, 0:T], rope_d[1][:, t0:t0 + T], [], ["ROPE"])
                stage(f"{job}{l}A{b}-loadxh")
                for t in range(2):
                    bk, bkk = proj_fm(WA, "WA", t, T, hb)
                    ts("dve", GATES[:, t, 0:T], bk[:, 0:T], pv("bin", l, t), None, ALU.add, None, [bkk, "PRM"],
                       [("G", t)])
                    dma("sp", scr_dst(axs[job], t), chunked(GATES[:, t, 0:T]), [("G", t)], [("ax", job, b, t)])
                stage(f"{job}{l}A{b}-ax")
                for t in range(2):
                    bk, bkk = proj_fm(WA, "WA", 2 + t, T, hb)
                    if job == "S":
                        ts("dve", GATES[:, 4 + t, 0:T], bk[:, 0:T], pv("bin", l, 2 + t), None, ALU.add, None,
                           [bkk, "PRM"], [("G", 4 + t)])
                        dma("sp", nks[t * 128:(t + 1) * 128, t0:t0 + T], GATES[:, 4 + t, 0:T], [("G", 4 + t)],
                            [("nks", b, t)])
                    else:
                        F = FS[:, 7 + t, 0:T]
                        ts("dve", F, bk[:, 0:T], pv("bin", l, 2 + t), None, ALU.add, None, [bkk, "PRM"],
                           [("FS", 7 + t)])
                        dma("sp", nkT_o[l, t * 128:(t + 1) * 128, t0:t0 + T], F, [("FS", 7 + t)], [okey()])
                        cp("pool", NKP[:, t, t0:t0 + T], F, [("FS", 7 + t)], [("NKP", b)])
                stage(f"{job}{l}A{b}-nk")
                for t in range(2):
                    bv, bvk = proj_fm(WA, "WA", 4 + t, T, hb)
                    bg, bgk = proj_fm(WA, "WA", 6 + t, T, hb)
                    SG = FS[:, 5 + t, 0:T]
                    act(SG, bg[:, 0:T], AF.Sigmoid, [bgk, "PRM"], [("FS", 5 + t)], bias=pv("bin", l, 6 + t))
                    stt(GATES[:, 2 + t, 0:T], bv[:, 0:T], pv("bin", l, 4 + t), SG, ALU.add, ALU.mult,
                        [bvk, ("FS", 5 + t), "PRM"], [("G", 2 + t)])
                    dma("sp", scr_dst(zs[job], t), chunked(GATES[:, 2 + t, 0:T]), [("G", 2 + t)], [("z", job, b, t)])
                stage(f"{job}{l}A{b}-z")
                bk, bkk = proj_fm(WA, "WA", 8, T, hb)
                rms_rope(bk, bkk, pv("bin", l, 8), pv("kn", l), T, job == "S",
                         [(slice(0, 128), GK[:, t0:t0 + T], [("GK", b)])],
                         fout=(gkT_o[l, :, t0:t0 + T] if job == "P" else None))
                stage(f"{job}{l}A{b}-gk")
                for tt_ in range(T // 128):
                    bk, bkk = bank("o")
                    mm(bk[:, 0:384], ONESB[:, :], BROW[:, :], True, False,
                       ["ONESB", "BROW"], [bkk])
                    for kt in range(8):
                        mm(bk[:, 0:384], HB[hb][:, kt, tt_ * 128:(tt_ + 1) * 128], WA[:, kt, 1152:1536], False, kt == 7,
                           HK[hb](kt) + ["WA"], [bkk])
                    gtile = (t0 + tt_ * 128) // 128
                    if job == "S":
                        stg = GATES[:, 6:8, :].rearrange("p a w -> p (a w)")[:, tt_ * 256:(tt_ + 1) * 256]
                        cp("act", stg, bk[:, 0:256], [bkk], [("G", 6 + tt_ // 2)])
                        dma("sp", nvs[t0 + tt_ * 128:t0 + (tt_ + 1) * 128, :], stg, [("G", 6 + tt_ // 2)],
                            [("nvs", b, tt_)])
                        cp("dve", GV[:, gtile, :, 0:64], bk[:, 256:384].rearrange("p (k d) -> p k d", k=2), [bkk],
                           [("GV", gtile)])
                    else:
                        F = FS[:, 3 + (tt_ % 2), 0:384]
                        cp("act", F, bk[:, 0:384], [bkk], [("FS", 3 + (tt_ % 2))])
                        dma("sp", nv_o[l, t0 + tt_ * 128:t0 + (tt_ + 1) * 128, :], F[:, 0:256],
                            [("FS", 3 + (tt_ % 2))], [okey()])
                        dma("sp", gv_o[l, t0 + tt_ * 128:t0 + (tt_ + 1) * 128, :], F[:, 256:384],
                            [("FS", 3 + (tt_ % 2))], [okey()])
                        cp("pool", NVP[:, gtile, :, 0:64], F[:, 0:256].rearrange("p (h d) -> p h d", h=4),
                           [("FS", 3 + (tt_ % 2))], [("NVP", gtile)])
                        cp("pool", GV[:, gtile, :, 0:64], F[:, 256:384].rearrange("p (k d) -> p k d", k=2),
                           [("FS", 3 + (tt_ % 2))], [("GV", gtile)])

            stage(f"{job}{l}-phaseA")
            nxt = seq_jl.index((job, l)) + 1
            if nxt < len(seq_jl):
                load_weights_A(seq_jl[nxt][1])

            for b in range(NB):
                t0 = b * T
                CW = L if job == "P" else T
                NCH = T // CW
                CP = CW + 2 * PADW
                sbase = (b * NCH * SEQW["P"]) if job == "P" else 0
                tin = 0 if job == "P" else t0
                if b == 0:
                    stage_h(job, l, t0, T)
                load_x(job, l, t0, T)
                W = NCH * CP
                for t in range(2):
                    nb_ = [b] if job == "P" else [x for x in (b - 1, b, b + 1) if 0 <= x < NB]
                    sqs = [b * NCH + x for x in range(NCH)] if job == "P" else [0]
                    dma("sp", AXW[:, t, 0:W], axs[job][t * 128:(t + 1) * 128, sbase + tin:sbase + tin + W],
                        [("ax", job, x, t) for x in nb_]
                        + [("pad", job, sq_, id(axs[job]), t, sd) for sd in (0, 1) for sq_ in sqs], [("AXW", t)])
                    dma("sp", ZW[:, t, 0:W], zs[job][t * 128:(t + 1) * 128, sbase + tin:sbase + tin + W],
                        [("z", job, x, t) for x in nb_]
                        + [("pad", job, sq_, id(zs[job]), t, sd) for sd in (0, 1) for sq_ in sqs], [("ZW", t)])
                if job == "S":
                    dma("sp", COS[:, 0:T], rope_d[0][:, t0:t0 + T], [], ["ROPE"])
                    dma("sp", SIN[:, 0:T], rope_d[1][:, t0:t0 + T], [], ["ROPE"])
                    qr0 = t0 // GW
                    rmin = nat_rs(qr0, R)
                    rmax = nat_rs(qr0 + 7, R) + 8
                    nrows = rmax - rmin
                    for t in range(2):
                        dma("sp", NKW[:, t, 0:nrows * 64], nks[t * 128:(t + 1) * 128, rmin * 64:rmax * 64],
                            [("nks", x, t) for x in range(rmin // 8, (rmax - 1) // 8 + 1)], [("NKW", t)])
                    ne = nrows // 2
                    no = (nrows - 1) // 2
                    dma("sp", NVWe[:, 0:ne, :],
                        nvs[rmin * 64:rmin * 64 + ne * 128, :].rearrange("(j p) c -> p j c", p=128),
                        [("nvs", x, y) for x in range(rmin // 8, (rmax - 1) // 8 + 1) for y in range(4)], ["NVWe"])
                    dma("sp", NVWo[:, 0:no, :],
                        nvs[(rmin + 1) * 64:(rmin + 1) * 64 + no * 128, :].rearrange("(j p) c -> p j c", p=128),
                        [("nvs", x, y) for x in range(rmin // 8, (rmax - 1) // 8 + 1) for y in range(4)], ["NVWo"])
                for j in range(8):
                    bk, bkk = proj_fm(WB, "WB", j, T)
                    act(GATES[:, j, 0:T], bk[:, 0:T], AF.Silu, [bkk, "PRM"], [("G", j)], bias=pv("bin", l, 12 + j))
                for t in range(2):
                    bk, bkk = proj_fm(WB, "WB", 8 + t, T)
                    for hh in range(2):
                        ps = slice(64 * hh, 64 * hh + 64)
                        ts("dve", NQM[ps, 2 * t + hh, 0:T], bk[ps, 0:T], PRM[ps, P("bin", l, 20 + t):P("bin", l, 20 + t) + 1],
                           None, ALU.add, None, [bkk, "PRM"], [("NQ", 2 * t + hh)])
                for t in range(2):
                    bk, bkk = proj_fm(WB, "WB", 10 + t, T)
                    rms_rope(bk, bkk, pv("bin", l, 22 + t), pv("qn", l), T, job == "S",
                             [(slice(0, 64), GQM[0:64, t, 0:T], [("GQ", t)]),
                              (slice(64, 128), GQM[64:128, t + 2, 0:T], [("GQ", t + 2)])])

                stage(f"{job}{l}b{b}-proj")
                edge_l = (tin == 0)
                edge_r = (tin + CW == L)
                for t in range(2):
                    X = AXW[:, t, :]
                    S2, S4, S8, S16 = FS[:, 0, :], FS[:, 1, :], FS[:, 2, :], FS[:, 3, :]
                    tt("pool", S2[:, 1:W], X[:, 0:W - 1], X[:, 1:W], ALU.add, [("AXW", t)], [("FS", 0)])
                    if t == 0:
                        tt("pool", S4[64:128, 2:W - 1], S2[64:128, 1:W - 2], S2[64:128, 3:W], ALU.add, [("FS", 0)],
                           [("FS", 1)])
                        fin = [(0, S2, ("FS", 0)), (64, S4, ("FS", 1))]
                    else:
                        tt("pool", S4[:, 2:W - 1], S2[:, 1:W - 2], S2[:, 3:W], ALU.add, [("FS", 0)], [("FS", 1)])
                        tt("pool", S8[:, 4:W - 3], S4[:, 2:W - 5], S4[:, 6:W - 1], ALU.add, [("FS", 1)], [("FS", 2)])
                        tt("pool", S16[64:128, 8:W - 7], S8[64:128, 4:W - 11], S8[64:128, 12:W - 3], ALU.add,
                           [("FS", 2)], [("FS", 3)])
                        fin = [(0, S8, ("FS", 2)), (64, S16, ("FS", 3))]
                    for (p0, Sx, sk) in fin:
                        ps = slice(p0, p0 + 64)
                        for ch in range(NCH):
                            wo, yo = ch * CP + PADW, ch * CW
                            stt(YP[ps, t, yo:yo + CW], Sx[ps, wo:wo + CW],
                                PRM[ps, P("invwin", 0, t):P("invwin", 0, t) + 1],
                                X[ps, wo:wo + CW], ALU.mult, ALU.subtract, [sk, ("AXW", t), "PRM"], [("YP", t, p0)])
                            for (flag, c0, nm) in ((edge_l, 0, "edgeL"), (edge_r, CW - 8, "edgeR")):
                                if not flag:
                                    continue
                                E8 = FS[ps, 4, 0:8]
                                o_e = P(nm, 0, t * 8)
                                tt("dve", E8, Sx[ps, wo + c0:wo + c0 + 8], PRM[ps, o_e:o_e + 8], ALU.mult,
                                   [sk, "PRM"], [("FS", 4)])
                                tt("dve", YP[ps, t, yo + c0:yo + c0 + 8], E8, X[ps, wo + c0:wo + c0 + 8], ALU.subtract,
                                   [("FS", 4), ("AXW", t)], [("YP", t, p0)])
                    bk, bkk = bank("mm")
                    mm(bk[:, 0:T], PWBD[:, t, :], YP[:, t, 0:T], True, True, ["PWBD", ("YP", t, 0), ("YP", t, 64)],
                       [bkk])
                    stt(MIX[:, t, 0:T], bk[:, 0:T], pv("pscale", l, t), GATES[:, t, 0:T], ALU.mult, ALU.mult,
                        [bkk, "PRM", ("G", t)], [("MIX", t, 0), ("MIX", t, 1)])

                stage(f"{job}{l}b{b}-mixA")
                ZC = [FS[:, 0, 0:T], FS[:, 1, 0:T]]
                s1, s1k = bank("st")
                s2, s2k = bank("st")
                for t in range(2):
                    bk, bkk = bank("mm")
                    for k in range(31):
                        slot = cnt["dg"] % NDG
                        cnt["dg"] += 1
                        ts("pool", DG[:, slot, :], IDB[:, :], pv("convw", l, k * 2 + t), 0.0, ALU.mult, ALU.add,
                           ["IDB", "PRM"], [("DG", slot)])
                        for ch in range(NCH):
                            mm(bk[:, ch * CW:(ch + 1) * CW], DG[:, slot, :], ZW[:, t, ch * CP + k + 1:ch * CP + k + 1 + CW],
                               k == 0 and ch == 0, k == 30 and ch == NCH - 1, [("DG", slot), ("ZW", t)], [bkk])
                    act(ZC[t], bk[:, 0:T], AF.Identity, [bkk, "PRM"], [("FS", t)], bias=pv("convb", l, t))
                    act(SQ[:, 0:T], bk[:, 0:T], AF.Square, [bkk, "PRM"], ["SQ"], bias=pv("convb", l, t))
                    cp("dve", VBF[:, 0:T], ZC[t], [("FS", t)], ["VBF"])
                    mm(s1[:, 0:T], ONESB[:, :], VBF[:, 0:T], t == 0, t == 1, ["ONESB", "VBF"], [s1k])
                    mm(s2[:, 0:T], ONESB[:, :], SQ[:, 0:T], t == 0, t == 1, ["ONESB", "SQ"], [s2k])
                MEAN, MSQ, RSTD = FS[:, 2, 0:T], FS[:, 3, 0:T], FS[:, 4, 0:T]

                def ln_stats(n, epscol):
                    act(MEAN, s1[:, 0:T], AF.Identity, [s1k], [("FS", 2)], scale=1.0 / n)
                    act(MSQ, s1[:, 0:T], AF.Square, [s1k], [("FS", 3)], scale=1.0 / n)
                    stt(RSTD, s2[:, 0:T], 1.0 / n, MSQ, ALU.mult, ALU.subtract, [s2k, ("FS", 3)], [("FS", 4)])
                    act(RSTD, RSTD, AF.Ln, [("FS", 4), "EPSV"], [("FS", 4)], bias=EPSV[:, epscol:epscol + 1])
                    act(RSTD, RSTD, AF.Exp, [("FS", 4)], [("FS", 4)], scale=-0.5)

                ln_stats(256.0, 0)
                for t in range(2):
                    tt("dve", ZC[t], ZC[t], MEAN, ALU.subtract, [("FS", t), ("FS", 2)], [("FS", t)])
                    tt("dve", ZC[t], ZC[t], RSTD, ALU.mult, [("FS", t), ("FS", 4)], [("FS", t)])
                    act(ZN[:, t, 0:T], ZC[t], AF.Silu, [("FS", t), "PRM"], [("ZN", t)], bias=pv("clnb", l, t),
                        scale=pv("clng", l, t))
                for ot in range(2):
                    bk, bkk = bank("mm")
                    for t in range(2):
                        mm(bk[:, 0:T], CPW[:, t, ot * 128:(ot + 1) * 128], ZN[:, t, 0:T], t == 0, t == 1,
                           ["CPW", ("ZN", t)], [bkk])
                    tt("dve", MIX[:, 4 + ot, 0:T], bk[:, 0:T], GATES[:, 4 + ot, 0:T], ALU.mult, [bkk, ("G", 4 + ot)],
                       [("MIX", 4 + ot, 0), ("MIX", 4 + ot, 1)])

                stage(f"{job}{l}b{b}-mixC")
                if job == "P":
                    nkt = L // 128
                    for ch in range(NCH):
                        cb = t0 + ch * CW
                        for h in range(4):
                            th, pb = h // 2, 64 * (h % 2)
                            full_attn(CW, nkt,
                                      lambda h=h, ch=ch: (NQM[:, h, ch * CW:(ch + 1) * CW], [("NQ", h)]),
                                      lambda i, th=th, cb=cb: (NKP[:, th, cb + i * 128:cb + (i + 1) * 128], [("NKP", b)]),
                                      lambda i, h=h, cb=cb: (NVP[:, cb // 128 + i, h, :], [("NVP", cb // 128 + i)]),
                                      2 + th, pb, 2 + th, c0=ch * CW)
                else:
                    for h in range(4):
                        th, pb = h // 2, 64 * (h % 2)
                        ob_, obk = bank("o")
                        db_, dbk = bank("st")
                        nrow = T // GW
                        st = {}
                        cst = {}

                        def ctx_qk(i, th=th, h=h):
                            bk, bkk = bank("mm")
                            mm(bk[:, 0:T], NKC[:, th, i * 128:(i + 1) * 128], NQM[:, h, 0:T], True, True,
                               ["NKC", ("NQ", h)], [bkk])
                            slot = cnt["pt"] % NPT
                            cnt["pt"] += 1
                            act(PT[:, slot, 0:T], bk[:, 0:T], AF.Exp, [bkk], [("PT", slot)], scale=HD ** -0.5)
                            cst[i] = slot

                        def ctx_pv(i, h=h, th=th):
                            slot = cst[i]
                            mm(ob_[:, 0:T], NVC[:, i, th * 128:(th + 1) * 128], PT[:, slot, 0:T], i == 0, False,
                               ["NVC", ("PT", slot)], [obk])
                            mm(db_[:, 0:T], ONESB[:, :], PT[:, slot, 0:T], i == 0, False, ["ONESB", ("PT", slot)], [dbk])

                        nctx = PAST // 128
                        for i in range(nctx + 2):
                            if i < nctx:
                                ctx_qk(i)
                            if i >= 2:
                                ctx_pv(i - 2)

                        def nat_qk(r, th=th, h=h):
                            qr = qr0 + r
                            rs = nat_rs(qr, R)
                            off = rs - rmin
                            bk, bkk = bank("mm")
                            for j in range(4):
                                mm(bk[:, j * 64:(j + 1) * 64], NKW[:, th, (off + 2 * j) * 64:(off + 2 * j + 2) * 64],
                                   NQM[:, h, r * 64:(r + 1) * 64], True, True, [("NKW", th), ("NQ", h)], [bkk])
                            slot = cnt["pt"] % NPT
                            cnt["pt"] += 1
                            act(PT[:, slot, 0:256], bk[:, 0:256], AF.Exp, [bkk], [("PT", slot)], scale=HD ** -0.5)
                            e = qr - rs
                            m0 = 7 - e
                            tt("dve", PT[:, slot, 0:256].rearrange("p (j c) -> p j c", c=64),
                               PT[:, slot, 0:256].rearrange("p (j c) -> p j c", c=64),
                               EM[:, h, :].rearrange("p (m c) -> p m c", c=64)[:, m0:m0 + 7:2, :],
                               ALU.mult, [("PT", slot), ("EM", h)], [("PT", slot)])
                            st[r] = (slot, off)

                        def nat_pv(r, h=h, th=th):
                            slot, off = st[r]
                            c0 = r * 64
                            for j in range(4):
                                if off % 2 == 0:
                                    v_ap, vk = NVWe[:, off // 2 + j, th * 128:(th + 1) * 128], "NVWe"
                                else:
                                    v_ap, vk = NVWo[:, (off - 1) // 2 + j, th * 128:(th + 1) * 128], "NVWo"
                                mm(ob_[:, c0:c0 + 64], v_ap, PT[:, slot, j * 64:(j + 1) * 64], False,
                                   (r == nrow - 1 and j == 3), [vk, ("PT", slot)], [obk])
                            for j in range(4):
                                mm(db_[:, c0:c0 + 64], ONESB[:, :], PT[:, slot, j * 64:(j + 1) * 64], False,
                                   (r == nrow - 1 and j == 3), ["ONESB", ("PT", slot)], [dbk])

                        LK = 2
                        for r in range(nrow + LK):
                            if r < nrow:
                                nat_qk(r)
                            if r >= LK:
                                nat_pv(r - LK)
                        attn_finish(ob_, obk, pb, db_, dbk, pb, T, 2 + th, pb, 2 + th)

                stage(f"{job}{l}b{b}-mixB")
                nkt = (L // 128) if job == "P" else NKT_S
                for ch in range(NCH):
                    kbase = (t0 + ch * CW) if job == "P" else 0
                    for h in range(4):
                        kv = h // 2
                        mt, pb = 6 + h // 2, 64 * (h % 2)
                        full_attn(CW, nkt,
                                  lambda h=h, ch=ch: (GQM[:, h, ch * CW:(ch + 1) * CW], [("GQ", h)]),
                                  lambda i, kbase=kbase: (GK[:, kbase + i * 128:kbase + (i + 1) * 128],
                                                          [("GK", x) for x in (list(range(NB)) + ["ctx"])] if job == "S"
                                                          else [("GK", b)]),
                                  lambda i, kv=kv, kbase=kbase: (GV[:, kbase // 128 + i, kv, :],
                                                                 [("GV", kbase // 128 + i)]),
                                  mt, pb, 6 + h // 2, c0=ch * CW)

                stage(f"{job}{l}b{b}-mixD")
                s1, s1k = bank("st")
                s2, s2k = bank("st")
                for ot in range(8):
                    bk, bkk = bank("mm")
                    for kt in range(8):
                        mm(bk[:, 0:T], WO[:, kt, ot * 128:(ot + 1) * 128], MIX[:, kt, 0:T], kt == 0, kt == 7,
                           ["WO", ("MIX", kt, 0), ("MIX", kt, 1)], [bkk])
                    T1 = FS[:, 5 + (ot % 2), 0:T]
                    act(T1, bk[:, 0:T], AF.Identity, [bkk, LVK], [("FS", 5 + (ot % 2))], bias=LV[:, 3, ot:ot + 1],
                        scale=LV[:, 2, ot:ot + 1])
                    stt(XB[:, ot, 0:T], XB[:, ot, 0:T], float(ALPHA), T1, ALU.mult, ALU.add,
                        [("XB", ot), ("FS", 5 + (ot % 2))], [("XB", ot)])
                    act(SQ[:, 0:T], XB[:, ot, 0:T], AF.Square, [("XB", ot)], ["SQ"])
                    cp("dve", VBF[:, 0:T], XB[:, ot, 0:T], [("XB", ot)], ["VBF"])
                    mm(s1[:, 0:T], ONESB[:, :], VBF[:, 0:T], ot == 0, ot == 7, ["ONESB", "VBF"], [s1k])
                    mm(s2[:, 0:T], ONESB[:, :], SQ[:, 0:T], ot == 0, ot == 7, ["ONESB", "SQ"], [s2k])
                if b + 1 < NB:
                    stage_h(job, l, t0 + T, T)
                ln_stats(float(D), 0)
                last = (l == nl - 1)
                dst = (yT_p if job == "P" else yT_s) if last else xs[(job, (l + 1) % 2)]
                for ot in range(8):
                    tt("dve", XB[:, ot, 0:T], XB[:, ot, 0:T], MEAN, ALU.subtract, [("XB", ot), ("FS", 2)],
                       [("XB", ot)])
                    tt("dve", XB[:, ot, 0:T], XB[:, ot, 0:T], RSTD, ALU.mult, [("XB", ot), ("FS", 4)], [("XB", ot)])
                    ts("dve", XB[:, ot, 0:T], XB[:, ot, 0:T], pv("lng", l, ot), pv("lnb", l, ot), ALU.mult, ALU.add,
                       [("XB", ot), "PRM"], [("XB", ot)])
                wkey = okey() if last else ("xs", job, (l + 1) % 2, b)
                dma("sp", dst.rearrange("(kt p) t -> p kt t", p=128)[:, :, t0:t0 + T], XB[:, :, 0:T],
                    [("XB", k) for k in range(8)], [wkey])

    S.stopped = False
    S.add("sp", lambda e: e.nop(), [k for k in S.outkeys if k in S.lastw], [])
    S.emit(nc, stack)
    stack.close()
    return nc


def host_constants(LS, LP):
    ident = np.eye(128, dtype=np.float32)
    ones = np.ones((128, 128), np.float32)
    bd = np.zeros((128, 128), np.float32)
    bd[:64, :64] = 1.0
    bd[64:, 64:] = 1.0
    perm = np.zeros((128, 128), np.float32)
    for m in range(128):
        d = m % 64
        blk = d // 16
        partner = m + 16 if blk % 2 == 0 else m - 16
        perm[partner, m] = 1.0
    consts = np.stack([ident, ones, bd, perm]).astype(np.float32)
    half, nf = HD // 2, HD // 4
    t = np.arange(LS)
    inv = (10000.0 ** (-np.arange(nf, dtype=np.float32) * 2.0 / half)).astype(np.float32)
    cos = np.zeros((128, LS), np.float32)
    sin = np.zeros((128, LS), np.float32)
    for p in range(128):
        d = p % 64
        pos = (t // GW) if d < half else (t % GW)
        dd = d % half
        f = dd % nf
        ang = pos.astype(np.float32) * inv[f]
        cos[p] = np.cos(ang).astype(np.float32)
        s = np.sin(ang).astype(np.float32)
        sin[p] = -s if dd < nf else s
    rope = np.stack([cos, sin]).astype(np.float32)
    col = np.arange(GW)
    cs = np.clip(col - 8, 0, GW - 16)
    col_in = (col[None, :] >= cs[:, None]) & (col[None, :] < cs[:, None] + 16)
    cmask = np.zeros((128, 64), np.float32)
    for krl in range(2):
        cmask[krl * 64:(krl + 1) * 64, :] = col_in.T.astype(np.float32)
    return consts, rope, cmask


def nat_master(nat_bias):
    nl = nat_bias.shape[0]
    krl = np.arange(2)[:, None, None, None]
    kc = np.arange(64)[None, :, None, None]
    m = np.arange(14)[None, None, :, None]
    qc = np.arange(64)[None, None, None, :]
    dr = np.broadcast_to(krl + m, (2, 64, 14, 64))
    dc = np.broadcast_to(np.clip(kc - qc, -15, 15) + 15, (2, 64, 14, 64))
    out = nat_bias[:, :, dr, dc]
    return np.ascontiguousarray(out.reshape(nl, 4, 128, 14 * 64)).astype(np.float32)


def build_params(nl, inp, c_vec, c_ctx, bperm, L_for_edges=None):
    PO, NPAR = param_layout(nl)
    prm = np.zeros((128, NPAR), np.float32)

    def put(name, l, arr):
        o, c = PO[name]
        prm[:, o + l * c:o + l * c + c] = arr.T

    for l in range(nl):
        put("bmod", l, inp["b_mod"][l].reshape(24, 128))
        put("bin", l, bperm[l].reshape(24, 128))
        put("pscale", l, inp["pool_scale"][l].reshape(2, 128))
        put("convw", l, inp["conv_w"][l].reshape(31, 2, 128).reshape(62, 128))
        put("convb", l, inp["conv_b"][l].reshape(2, 128))
        put("clng", l, inp["conv_ln_g"][l].reshape(2, 128))
        put("clnb", l, inp["conv_ln_b"][l].reshape(2, 128))
        put("bout", l, inp["b_out"][l].reshape(8, 128))
        put("lng", l, inp["ln_g"][l].reshape(8, 128))
        put("lnb", l, inp["ln_b"][l].reshape(8, 128))
        put("qn", l, np.tile(inp["q_norm"][l], 2)[None, :])
        put("kn", l, np.tile(inp["k_norm"][l], 2)[None, :])
    o, _ = PO["cond"]
    cc = np.stack([c_ctx.reshape(8, 128), c_vec.reshape(8, 128)], axis=-1)
    prm[:, o:o + 16] = cc.transpose(1, 0, 2).reshape(128, 16)
    wins = [(2, 4), (8, 16)]
    o, _ = PO["invwin"]
    oL, _ = PO["edgeL"]
    oR, _ = PO["edgeR"]
    for t in range(2):
        for half in range(2):
            w = wins[t][half]
            ps = slice(half * 64, half * 64 + 64)
            prm[ps, o + t] = 1.0 / w
            for i in range(8):
                cl = i + w // 2 - max(i - w // 2, 0)
                prm[ps, oL + t * 8 + i] = 1.0 / cl
                cr = min(w, 8 - i + w // 2)
                prm[ps, oR + t * 8 + i] = 1.0 / cr
    return prm


def prepare_inputs(inp, nl=NL, LS=4096, NPS=4, LP=256, n_cores=N_CORES):
    perm = col_perm()
    f = lambda a: np.ascontiguousarray(np.asarray(a, dtype=np.float32))
    inp = {k: f(v) for k, v in inp.items()}
    w_in_p = np.ascontiguousarray(inp["w_in"][:nl][:, :, perm])
    bperm = inp["b_in"][:nl][:, perm]
    b_row = np.ascontiguousarray(bperm[:, 1152:1536].reshape(1, nl * 384))
    consts, rope, cmask = host_constants(LS, LP)
    pool_bd = np.zeros((nl, 2, 128, 128), np.float32)
    for l in range(nl):
        for g in range(4):
            t, hh = g // 2, g % 2
            pool_bd[l, t, hh * 64:(hh + 1) * 64, hh * 64:(hh + 1) * 64] = inp["pool_w"][l, g]
    bm = nat_master(inp["nat_bias"][:nl])
    shared = dict(w_mod=np.ascontiguousarray(inp["w_mod"][:nl]), w_in_p=w_in_p, b_row=b_row,
                  w_out=np.ascontiguousarray(inp["w_out"][:nl]), pool_bd=pool_bd,
                  conv_pw=np.ascontiguousarray(inp["conv_pw"][:nl]), nat_bm=bm, colmask=cmask, rope=rope,
                  consts=consts)
    maps = []
    for i in range(n_cores):
        xp = inp["x_prompt"][i * NPS:(i + 1) * NPS].reshape(NPS * LP, D)
        m = dict(shared)
        m["xT_p"] = np.ascontiguousarray(xp.T)
        m["xT_s"] = np.ascontiguousarray(inp["x_sample"][i].T)
        m["params"] = build_params(nl, inp, inp["c"][i], inp["c_ctx"], bperm)
        m["nkc"] = np.ascontiguousarray(inp["cache_nat_k"][i, :nl].reshape(nl, PAST, 256).transpose(0, 2, 1))
        m["nvc"] = np.ascontiguousarray(inp["cache_nat_v"][i, :nl].reshape(nl, PAST, 256))
        m["gkc"] = np.ascontiguousarray(inp["cache_gqa_k"][i, :nl].reshape(nl, PAST, 128).transpose(0, 2, 1))
        m["gvc"] = np.ascontiguousarray(inp["cache_gqa_v"][i, :nl].reshape(nl, PAST, 128))
        maps.append(m)
    return maps


_NC_CACHE = {}


def kernel(**inputs):
    key = "full"
    if key not in _NC_CACHE:
        _NC_CACHE[key] = build_program()
    nc = _NC_CACHE[key]
    maps = prepare_inputs(inputs)
    res = run_bass_kernel_spmd(nc, maps, core_ids=list(range(N_CORES)))
    rs = res.results
    B, SEQ, DB, DSEQ = 32, 256, 8, 4096
    y_prompt = np.zeros((B, SEQ, D), np.float32)
    y_sample = np.zeros((DB, DSEQ, D), np.float32)
    nk = np.zeros((B, NL, SEQ, 4, HD), np.float32)
    nv = np.zeros((B, NL, SEQ, 4, HD), np.float32)
    gk = np.zeros((B, NL, SEQ, 2, HD), np.float32)
    gv = np.zeros((B, NL, SEQ, 2, HD), np.float32)
    for i, r in enumerate(rs):
        y_prompt[4 * i:4 * i + 4] = np.asarray(r["yT_p"]).T.reshape(4, SEQ, D)
        y_sample[i] = np.asarray(r["yT_s"]).T
        a = np.asarray(r["nkT_o"]).reshape(NL, 256, 4, SEQ)
        nk[4 * i:4 * i + 4] = a.transpose(2, 0, 3, 1).reshape(4, NL, SEQ, 4, HD)
        a = np.asarray(r["gkT_o"]).reshape(NL, 128, 4, SEQ)
        gk[4 * i:4 * i + 4] = a.transpose(2, 0, 3, 1).reshape(4, NL, SEQ, 2, HD)
        a = np.asarray(r["nv_o"]).reshape(NL, 4, SEQ, 256)
        nv[4 * i:4 * i + 4] = a.transpose(1, 0, 2, 3).reshape(4, NL, SEQ, 4, HD)
        a = np.asarray(r["gv_o"]).reshape(NL, 4, SEQ, 128)
        gv[4 * i:4 * i + 4] = a.transpose(1, 0, 2, 3).reshape(4, NL, SEQ, 2, HD)
    return (y_prompt, y_sample, nk, nv, gk, gv)
```

```python
import math
from contextlib import ExitStack

import numpy as np
import concourse.bass as bass
import concourse.mybir as mybir
from concourse.bass_utils import run_bass_kernel_spmd

F32 = mybir.dt.float32
BF16 = mybir.dt.bfloat16
AF = mybir.ActivationFunctionType
ALU = mybir.AluOpType

D = 1024
NL = 4
GW = 64
HD = 64
PAST = 512
LN_EPS = 1e-5
RMS_EPS = 1e-6
ALPHA = (2 * NL) ** 0.25
N_CORES = 8
PADW = 16

ENGS = ("pe", "act", "dve", "pool", "sp")
EPOCH = 12000
NRING = 12


class Op:
    __slots__ = ("eng", "idx", "gid", "fn", "deps", "gdeps", "dma", "sig", "signo", "dmano", "busy", "lat")


HOP_NS = 700.0


class Sched:
    def __init__(self):
        self.all = []
        self.ops = {e: [] for e in ENGS}
        self.lastw = {}
        self.readers = {}
        self.outkeys = []
        self.stopped = False
        self.reorder = True

    def add(self, eng, fn, reads=(), writes=(), dma=False, cost=300.0, lat=None):
        if self.stopped:
            return None
        op = Op()
        op.eng = eng
        op.gid = len(self.all)
        op.idx = -1
        op.fn = fn
        op.dma = dma
        op.sig = False
        op.signo = -1
        op.dmano = -1
        op.busy = float(cost)
        op.lat = float(cost if lat is None else lat)
        deps = set()
        for k in reads:
            lw = self.lastw.get(k)
            if lw is not None:
                deps.add(lw)
            if isinstance(k, tuple) and k[0] == "bk":
                for r in self.readers.get(k, ()):
                    if self.all[r].eng != eng:
                        deps.add(r)
        for k in writes:
            lw = self.lastw.get(k)
            if lw is not None:
                deps.add(lw)
            for r in self.readers.get(k, ()):
                deps.add(r)
        deps.discard(op.gid)
        op.gdeps = deps
        self.all.append(op)
        for k in reads:
            self.readers.setdefault(k, []).append(op.gid)
        for k in writes:
            self.lastw[k] = op.gid
            self.readers[k] = []
        return op

    def schedule(self):
        import heapq
        n = len(self.all)
        if not self.reorder:
            for op in self.all:
                op.idx = len(self.ops[op.eng])
                self.ops[op.eng].append(op)
        else:
            succ = [[] for _ in range(n)]
            indeg = [0] * n
            for op in self.all:
                indeg[op.gid] = len(op.gdeps)
                for d in op.gdeps:
                    succ[d].append(op.gid)
            bl = [0.0] * n
            for op in reversed(self.all):
                m = 0.0
                for s_ in succ[op.gid]:
                    if bl[s_] > m:
                        m = bl[s_]
                bl[op.gid] = op.lat + m
            finish = [0.0] * n
            ready_t = [0.0] * n
            future = {e: [] for e in ENGS}
            avail = {e: [] for e in ENGS}
            free = {e: 0.0 for e in ENGS}
            for op in self.all:
                if indeg[op.gid] == 0:
                    heapq.heappush(future[op.eng], (0.0, op.gid))
            done = 0
            while done < n:
                best = None
                for e in ENGS:
                    f, a = future[e], avail[e]
                    while f and f[0][0] <= free[e]:
                        g_ = heapq.heappop(f)[1]
                        heapq.heappush(a, (-bl[g_], g_))
                    if a:
                        cand = (free[e], a[0][1], e, True)
                    elif f:
                        cand = (f[0][0], f[0][1], e, False)
                    else:
                        continue
                    if best is None or cand[:2] < best[:2]:
                        best = cand
                assert best is not None, "scheduler: dependency cycle"
                st, g, e, from_avail = best
                if from_avail:
                    heapq.heappop(avail[e])
                else:
                    heapq.heappop(future[e])
                op = self.all[g]
                op.idx = len(self.ops[e])
                self.ops[e].append(op)
                free[e] = st + op.busy
                finish[g] = st + op.lat
                done += 1
                for s_ in succ[g]:
                    t = finish[g] + (HOP_NS if self.all[s_].eng != e else 250.0)
                    if t > ready_t[s_]:
                        ready_t[s_] = t
                    indeg[s_] -= 1
                    if indeg[s_] == 0:
                        heapq.heappush(future[self.all[s_].eng], (ready_t[s_], s_))
            self.model_ns = max(finish) if n else 0.0
        for e in ENGS:
            k = 0
            for op in self.ops[e]:
                if op.dma:
                    op.dmano = k
                    k += 1
        for op in self.all:
            op.deps = {(self.all[d].eng, self.all[d].idx) for d in op.gdeps
                       if not (op.eng == "pe" and self.all[d].eng == "pe")}

    def emit(self, nc, stack):
        self.schedule()
        ndma = {e: sum(1 for o in self.ops[e] if o.dma) for e in ENGS}
        for e in ENGS:
            for op in self.ops[e]:
                for (se, si) in op.deps:
                    so = self.ops[se][si]
                    if not so.dma:
                        so.sig = True
        esems = {}
        for e in ENGS:
            n = 0
            for op in self.ops[e]:
                if op.sig:
                    op.signo = n
                    n += 1
            esems[e] = [stack.enter_context(nc.semaphore(f"s_{e}_{i}")) for i in range(n // EPOCH + 1)]
        rsems = {}
        for e in ENGS:
            if ndma[e]:
                rsems[e] = [stack.enter_context(nc.semaphore(f"r_{e}_{i}")) for i in range(NRING)]
        block = stack.enter_context(nc.Block())
        deco = {"pe": block.tensor, "act": block.scalar, "dve": block.vector, "pool": block.gpsimd, "sp": block.sync}

        def resolve(se, si):
            so = self.ops[se][si]
            if so.dma:
                return (se, "r", so.dmano % NRING), rsems[se][so.dmano % NRING], 16 * (so.dmano // NRING + 1)
            return (se, "s", so.signo // EPOCH), esems[se][so.signo // EPOCH], so.signo % EPOCH + 1

        def body_for(e):
            def body(eng):
                waited = {}

                def wait(key, sem, val):
                    if waited.get(key, 0) >= val:
                        return
                    waited[key] = val
                    eng.wait_ge(sem, val)

                for op in self.ops[e]:
                    need = {}
                    for (se, si) in op.deps:
                        k_, sem_, v_ = resolve(se, si)
                        if k_ not in need or need[k_][1] < v_:
                            need[k_] = (sem_, v_)
                    for k_ in sorted(need):
                        wait(k_, need[k_][0], need[k_][1])
                    if op.dma and op.dmano >= NRING:
                        slot = op.dmano % NRING
                        wait((e, "r", slot), rsems[e][slot], 16 * (op.dmano // NRING))
                    ins = op.fn(eng)
                    if op.dma:
                        ins.then_inc(rsems[e][op.dmano % NRING], 16)
                    elif op.sig:
                        ins.then_inc(esems[e][op.signo // EPOCH], 1)
            return body

        for e in ENGS:
            deco[e](body_for(e))


def param_layout(nl):
    off = {}
    n = 0
    for name, cnt in (("bmod", 24), ("bin", 24), ("pscale", 2), ("convw", 62), ("convb", 2), ("clng", 2),
                      ("clnb", 2), ("bout", 8), ("lng", 8), ("lnb", 8), ("qn", 1), ("kn", 1)):
        off[name] = (n, cnt)
        n += cnt * nl
    for name, cnt in (("cond", 16), ("invwin", 2), ("edgeL", 16), ("edgeR", 16)):
        off[name] = (n, cnt)
        n += cnt
    return off, n


_C = dict(a_x=0, a_g=256, nq=512, nk=768, nv=1024, n_g=1280, c_v=1536, c_glu=1792, c_g=2048, gq=2304, gk=2560,
          gv=2688, g_g=2816)


def col_perm():
    r = lambda a, n: list(range(a, a + n))
    A = r(_C["a_x"], 256) + r(_C["nk"], 256) + r(_C["c_v"], 256) + r(_C["c_glu"], 256) + r(_C["gk"], 128) \
        + r(_C["nv"], 256) + r(_C["gv"], 128)
    gq = []
    for h in (0, 2, 1, 3):
        gq += r(_C["gq"] + 64 * h, 64)
    B = r(_C["a_g"], 256) + r(_C["n_g"], 256) + r(_C["c_g"], 256) + r(_C["g_g"], 256) + r(_C["nq"], 256) + gq
    return np.array(A + B, dtype=np.int64)


def nat_rs(qr, R):
    return min(max(qr - 4, 0), R - 8)


class _StopBuild(Exception):
    pass


def build_program(nl=NL, LS=4096, NPS=4, LP=256, dbg=False, limit=None):
    nc = bass.Bass("TRN2", target_bir_lowering=False)
    S = Sched()
    PO, NPAR = param_layout(nl)
    R = LS // GW
    TP = LP * NPS
    NKT_S = LS // 128 + PAST // 128
    stack = ExitStack()

    def din(name, shape, dt=F32):
        return nc.dram_tensor(name, list(shape), dt, kind="ExternalInput").ap()

    def dout(name, shape, dt=F32):
        return nc.dram_tensor(name, list(shape), dt, kind="ExternalOutput").ap()

    def dscr(name, shape, dt):
        return nc.dram_tensor(name, list(shape), dt).ap()

    xT_p = din("xT_p", [D, TP])
    xT_s = din("xT_s", [D, LS])
    params_d = din("params", [128, NPAR])
    wmod_d = din("w_mod", [nl, D, 3 * D])
    win_d = din("w_in_p", [nl, D, 3 * D])
    brow_d = din("b_row", [1, nl * 384])
    wout_d = din("w_out", [nl, D, D])
    pwbd_d = din("pool_bd", [nl, 2, 128, 128])
    cpw_d = din("conv_pw", [nl, 256, 256])
    bm_d = din("nat_bm", [nl, 4, 128, 14 * 64])
    cmask_d = din("colmask", [128, 64])
    nkc_d = din("nkc", [nl, 256, PAST])
    nvc_d = din("nvc", [nl, PAST, 256])
    gkc_d = din("gkc", [nl, 128, PAST])
    gvc_d = din("gvc", [nl, PAST, 128])
    rope_d = din("rope", [2, 128, LS])
    consts_d = din("consts", [4, 128, 128])

    yT_p = dout("yT_p", [D, TP])
    yT_s = dout("yT_s", [D, LS])
    nkT_o = dout("nkT_o", [nl, 256, TP])
    gkT_o = dout("gkT_o", [nl, 128, TP])
    nv_o = dout("nv_o", [nl, TP, 256])
    gv_o = dout("gv_o", [nl, TP, 128])
    dbg_out = {}

    xs = {("P", 0): dscr("xsP0", [D, TP], F32), ("P", 1): dscr("xsP1", [D, TP], F32),
          ("S", 0): dscr("xsS0", [D, LS], F32), ("S", 1): dscr("xsS1", [D, LS], F32)}
    SEQW = {"P": LP + 2 * PADW, "S": LS + 2 * PADW}
    axs = {"P": dscr("axP", [256, NPS * SEQW["P"]], BF16), "S": dscr("axS", [256, SEQW["S"]], BF16)}
    zs = {"P": dscr("zP", [256, NPS * SEQW["P"]], BF16), "S": dscr("zS", [256, SEQW["S"]], BF16)}
    nks = dscr("nkS", [256, LS], BF16)
    nvs = dscr("nvS", [LS, 256], BF16)

    def sb(name, shape, dt):
        return stack.enter_context(nc.sbuf_tensor(name, list(shape), dt))

    TM = 512
    WPAD = TM + 4 * PADW
    GK = sb("GK", [128, max(LS + PAST, TP)], BF16)
    GV = sb("GV", [128, max(NKT_S, TP // 128), 2, 128], BF16)
    WA = sb("WA", [128, 8, 1536], BF16)
    WB = sb("WB", [128, 8, 1536], BF16)
    WO = sb("WO", [128, 8, D], BF16)
    CPW = sb("CPW", [128, 2, 256], BF16)
    PWBD = sb("PWBD", [128, 2, 128], BF16)
    EM = sb("EM", [128, 4, 14 * 64], BF16)
    NKC = sb("NKC", [128, 2, PAST], BF16)
    n_p = 2 * TP + (TP // 128) * 4 * 128
    NVC_OFF = 5760
    ONES_OFF = NVC_OFF + (PAST // 128) * 256
    n_s = ONES_OFF + 64
    assert n_p <= ONES_OFF
    JR = sb("JR", [128, max(n_p, n_s)], BF16)
    NVC = JR[:, NVC_OFF:ONES_OFF].rearrange("p (i c) -> p i c", c=256)

    NKP = JR[:, 0:2 * TP].rearrange("p (t n) -> p t n", t=2)
    NVP = JR[:, 2 * TP:n_p].rearrange("p (i h c) -> p i h c", h=4, c=128)
    NKW = JR[:, 0:1920].rearrange("p (t n) -> p t n", t=2)
    NVWe = JR[:, 1920:1920 + 2048].rearrange("p (j c) -> p j c", c=256)
    NVWo = JR[:, 3968:3968 + 1792].rearrange("p (j c) -> p j c", c=256)
    IDB = sb("IDB", [128, 128], BF16)
    ONESB = sb("ONESB", [128, 128], BF16)
    BD64 = sb("BD64", [128, 128], BF16)
    PERM = sb("PERM", [128, 128], F32)
    BROW = sb("BROW", [128, 384], BF16)
    CMASK = sb("CMASK", [128, 64], BF16)
    PRM = sb("PRM", [128, NPAR], F32)
    MODT = sb("MODT", [128, nl, 24, 2], F32)
    LV2 = sb("LV", [128, 2, 4, 8], F32)
    SC = sb("SC", [128, 8, 2], F32)
    EPSV = sb("EPSV", [128, 2], F32)
    XB = sb("XB", [128, 8, TM], F32)
    H = sb("H", [128, 8, TM], BF16)
    GATES = sb("GATES", [128, 8, TM], BF16)
    NQM = sb("NQM", [128, 4, TM], BF16)
    GQM = sb("GQM", [128, 4, TM], BF16)
    MIX = sb("MIX", [128, 8, TM], BF16)
    AXW = sb("AXW", [128, 2, WPAD], BF16)
    ZW = sb("ZW", [128, 2, WPAD], BF16)
    NDG = 6
    DG = sb("DG", [128, NDG, 128], BF16)
    NPT = 6
    PT = sb("PT", [128, NPT, TM], BF16)
    COS = sb("COS", [128, TM], F32)
    SIN = sb("SIN", [128, TM], F32)
    NFS = 7
    FS = sb("FS", [128, NFS, WPAD], F32)
    SQ = sb("SQ", [128, TM], BF16)
    VBF = sb("VBF", [128, TM], BF16)
    ZN = sb("ZN", [128, 2, TM], BF16)
    YP = sb("YP", [128, 2, TM], BF16)
    ZERO = sb("ZERO", [128, 2 * PADW], BF16)
    DUMMY = sb("DUMMY", [128, 8], F32)

    G32 = GATES[:, :, :].rearrange("p a w -> p (a w)").bitcast(F32).rearrange("p (s w) -> p s w", w=TM)
    M32 = MIX[:, :, :].rearrange("p a w -> p (a w)").bitcast(F32).rearrange("p (s w) -> p s w", w=TM)
    banks = [stack.enter_context(nc.psum_tensor(f"bk{i}", [128, 512], F32)) for i in range(8)]

    def P(name, l=0, j=0):
        o, c = PO[name]
        return o + l * c + j

    def pv(name, l=0, j=0, n=1):
        o = P(name, l, j)
        return PRM[:, o:o + n]

    rot = {"mm": [0, 1, 2, 3], "o": [4, 5], "st": [6, 7]}
    rpos = {k: 0 for k in rot}

    def bank(role):
        i = rot[role][rpos[role] % len(rot[role])]
        rpos[role] += 1
        return banks[i], ("bk", i)

    cnt = {"pt": 0, "dg": 0}

    stg_n = [0]

    def stage(name):
        stg_n[0] += 1
        if limit is not None and stg_n[0] >= limit and not S.stopped:
            print("STOP at stage", stg_n[0], name)
            S.stopped = True

    def fsz(ap):
        n = 1
        for d in ap.shape[1:]:
            n *= d
        return n

    def mm(out, lhsT, rhs, start, stop, reads, writes):
        n = fsz(out)
        c = max(n / 2.4, 90.0) * (4.0 if rhs.dtype == F32 else 1.0)
        S.add("pe", lambda e: e.matmul(out, lhsT=lhsT, rhs=rhs, start=start, stop=stop), reads, writes, cost=c,
              lat=c + 160.0)

    def act(out, in_, func, reads, writes, bias=None, scale=None):
        kw = {}
        if bias is not None:
            kw["bias"] = bias
        if scale is not None:
            kw["scale"] = scale
        S.add("act", lambda e: e.activation(out=out, in_=in_, func=func, **kw), reads, writes,
              cost=200.0 + 0.75 * fsz(out))

    def ecost(eng, kind, n):
        if eng == "pool":
            return {"tt": 150 + 1.7 * n, "ts": 230 + 1.4 * n, "cp": 150 + 2.5 * n}[kind]
        return {"tt": 100 + 0.7 * n, "ts": 100 + 0.6 * n, "cp": 100 + 0.5 * n, "stt": 120 + 1.1 * n}[kind]

    def tt(eng, out, in0, in1, op, reads, writes):
        S.add(eng, lambda e: e.tensor_tensor(out=out, in0=in0, in1=in1, op=op), reads, writes,
              cost=ecost(eng, "tt", fsz(out)))

    def ts(eng, out, in0, s1, s2, op0, op1, reads, writes):
        c = ecost(eng, "ts", fsz(out))
        if op1 is None:
            S.add(eng, lambda e: e.tensor_scalar(out=out, in0=in0, scalar1=s1, scalar2=None, op0=op0), reads, writes,
                  cost=c)
        else:
            S.add(eng, lambda e: e.tensor_scalar(out=out, in0=in0, scalar1=s1, scalar2=s2, op0=op0, op1=op1),
                  reads, writes, cost=c)

    def stt(out, in0, scalar, in1, op0, op1, reads, writes):
        S.add("dve", lambda e: e.scalar_tensor_tensor(out=out, in0=in0, scalar=scalar, in1=in1, op0=op0, op1=op1),
              reads, writes, cost=ecost("dve", "stt", fsz(out)))

    def cp(eng, out, in_, reads, writes):
        if eng == "act":
            S.add(eng, lambda e: e.copy(out=out, in_=in_), reads, writes, cost=200.0 + 0.75 * fsz(out))
        else:
            S.add(eng, lambda e: e.tensor_copy(out=out, in_=in_), reads, writes, cost=ecost(eng, "cp", fsz(out)))

    def dma(q, out, in_, reads, writes):
        nbytes = fsz(out) * out.shape[0] * (4 if out.dtype == F32 else 2)
        S.add(q, lambda e: e.dma_start(out=out, in_=in_), reads, writes, dma=True,
              cost=(150.0 if q == "sp" else 1500.0), lat=2500.0 + nbytes / 120.0)

    def okey():
        k = ("out", len(S.outkeys))
        S.outkeys.append(k)
        return k

    def dbg_dump(name, ap, shape, key):
        if not dbg:
            return
        d = dout("dbg_" + name, shape, ap.dtype)
        dbg_out[name] = d
        dma("sp", d, ap, [key] if not isinstance(key, list) else key, [okey()])

    dma("sp", PRM[:, :], params_d[:, :], [], ["PRM"])
    dma("pool", IDB[:, :], consts_d[0], [], ["IDB"])
    dma("pool", ONESB[:, :], consts_d[1], [], ["ONESB"])
    dma("pool", BD64[:, :], consts_d[2], [], ["BD64"])
    dma("sp", PERM[:, :], consts_d[3], [], ["PERM"])
    dma("pool", CMASK[:, :], cmask_d[:, :], [], ["CMASK"])
    S.add("dve", lambda e: e.memset(ZERO[:, :], 0.0), [], ["ZERO"])
    S.add("dve", lambda e: e.memset(EPSV[:, 0:1], LN_EPS), [], ["EPSV"])
    S.add("dve", lambda e: e.memset(EPSV[:, 1:2], RMS_EPS), [], ["EPSV"])
    S.add("pool", lambda e: e.memset(GV[:, :, :, 64:128], 1.0), [], [("GV", i) for i in range(GV.shape[1])])
    S.add("pool", lambda e: e.memset(NVP[:, :, :, 64:128], 1.0), [], [("NVP", i) for i in range(TP // 128)])
    S.add("pool", lambda e: e.memset(BROW[:, :], 0.0), [], ["BROW"])
    S.add("pool", lambda e: e.memset(NQM[:, :, :], 0.0), [], [("NQ", h) for h in range(4)])
    S.add("pool", lambda e: e.memset(GQM[:, :, :], 0.0), [], [("GQ", h) for h in range(4)])
    for job, nseq in (("P", NPS), ("S", 1)):
        L = LP if job == "P" else LS
        for s in range(nseq):
            for scr in (axs[job], zs[job]):
                for t in range(2):
                    b0 = s * SEQW[job]
                    dma("sp", scr[t * 128:(t + 1) * 128, b0:b0 + PADW], ZERO[:, 0:PADW], ["ZERO"],
                        [("pad", job, s, id(scr), t, 0)])
                    dma("sp", scr[t * 128:(t + 1) * 128, b0 + PADW + L:b0 + 2 * PADW + L], ZERO[:, 0:PADW],
                        ["ZERO"], [("pad", job, s, id(scr), t, 1)])

    stage("prologue-loads")
    o_c, _ = PO["cond"]
    act(SC[:, :, :].rearrange("p k j -> p (k j)"), PRM[:, o_c:o_c + 16], AF.Silu, ["PRM"], ["SC"])
    idle_tiles = GV.shape[1] - TP // 128
    bgw = 128 if idle_tiles >= 8 else 0
    NBG = min(3, idle_tiles // 8)
    ob, _ = PO["bmod"]

    def mod_layer(l, bg):
        cw = bgw if bg else 512
        nch = 3072 // cw
        for ch in range(nch):
            if bg:
                r0 = TP // 128 + 8 * ((l * nch + ch) % NBG)
                WM = GV[:, r0:r0 + 8, :, :].rearrange("p i k c -> p (i k c)").bitcast(F32).rearrange(
                    "p (k w) -> p k w", w=cw)
                wkeys = [("GV", r0 + i) for i in range(8)]
            else:
                par = ch % 2
                WM = XB
                wkeys = [("XB", k) for k in range(8)] if par == 0 else (
                    [("G", j) for j in range(8)] + [("MIX", j, hh) for j in range(8) for hh in range(2)])
            src_w = wmod_d[l].rearrange("(kt p) n -> p kt n", p=128)[:, :, ch * cw:(ch + 1) * cw]
            if (not bg) and par == 1:
                dma("sp", G32[:, :, :], src_w[:, 0:4, :], [], wkeys)
                dma("sp", M32[:, :, :], src_w[:, 4:8, :], [], wkeys)
                wsel = lambda kt: (G32 if kt < 4 else M32)[:, kt % 4, :]
            else:
                dma("sp", WM[:, :, :], src_w, [], wkeys)
                wsel = lambda kt, WM=WM: WM[:, kt, :]
            bk, bkk = bank("st")
            ng = cw // 128
            for c4 in range(ng):
                for kt in range(8):
                    mm(bk[:, c4 * 2:c4 * 2 + 2], wsel(kt)[:, c4 * 128:(c4 + 1) * 128], SC[:, kt, :], kt == 0, kt == 7,
                       wkeys + ["SC"], [bkk])
            ct0 = ch * ng
            cp("dve", MODT[:, l, ct0:ct0 + ng, :].rearrange("p c j -> p (c j)"), bk[:, 0:2 * ng], [bkk],
               [("MODT", l)])
        for j in range(2):
            tt("dve", MODT[:, l, :, j], MODT[:, l, :, j], PRM[:, ob + l * 24:ob + (l + 1) * 24], ALU.add,
               [("MODT", l), "PRM"], [("MODT", l)])

    mod_layer(0, False)
    for l in range(1, nl):
        mod_layer(l, bgw > 0)
    if bgw > 0 and nl > 1:
        nt = 8 * NBG
        S.add("pool", lambda e: e.memset(GV[:, TP // 128:TP // 128 + nt, :, 64:128], 1.0), [],
              [("GV", TP // 128 + i) for i in range(nt)], cost=1500.0)

    stage("prologue-mod")
    def x_src(job, l):
        if l == 0:
            return xT_p if job == "P" else xT_s
        return xs[(job, l % 2)]

    def load_x(job, l, t0, T):
        dma("sp", XB[:, :, 0:T], x_src(job, l).rearrange("(kt p) t -> p kt t", p=128)[:, :, t0:t0 + T],
            [("xs", job, l % 2, t0 // T)] if l > 0 else [], [("XB", k) for k in range(8)])

    HK = {"H": lambda kt: [("H", kt)], "MIX": lambda kt: [("MIX", kt, 0), ("MIX", kt, 1)]}
    HB = {"H": H, "MIX": MIX}

    def h_from_xb(T, hb):
        for kt in range(8):
            ts("pool", HB[hb][:, kt, 0:T], XB[:, kt, 0:T], LV[:, 0, kt:kt + 1], LV[:, 1, kt:kt + 1], ALU.mult, ALU.add,
               [("XB", kt), LVK], HK[hb](kt))

    def stage_h(job, l, t0, T):
        src = x_src(job, l).rearrange("(kt p) t -> p kt t", p=128)
        rk = [("xs", job, l % 2, t0 // T)] if l > 0 else []
        gk = [("G", j) for j in range(8)]
        mk = [("MIX", j, hh) for j in range(8) for hh in range(2)]
        dma("sp", G32[:, :, 0:T], src[:, 0:4, t0:t0 + T], rk, gk)
        dma("sp", M32[:, :, 0:T], src[:, 4:8, t0:t0 + T], rk, mk)
        for kt in range(8):
            st_, sk = (G32, gk) if kt < 4 else (M32, mk)
            ts("pool", H[:, kt, 0:T], st_[:, kt % 4, 0:T], LV[:, 0, kt:kt + 1], LV[:, 1, kt:kt + 1], ALU.mult, ALU.add,
               sk + [LVK], [("H", kt)])

    def proj_fm(W, wkey, ct, T, hb="H"):
        bk, bkk = bank("mm")
        for kt in range(8):
            mm(bk[:, 0:T], W[:, kt, ct * 128:(ct + 1) * 128], HB[hb][:, kt, 0:T], kt == 0, kt == 7,
               [wkey] + HK[hb](kt), [bkk])
        return bk, bkk

    def rms_rope(bk, bkk, bias_ap, gain_ap, T, rope, dst, fout=None):
        G0, G1, R1 = FS[:, 0, 0:T], FS[:, 1, 0:T], FS[:, 2, 0:T]
        ts("dve", G0, bk[:, 0:T], bias_ap, None, ALU.add, None, [bkk, "PRM"], [("FS", 0)])
        stage("rr-ts")
        act(SQ[:, 0:T], bk[:, 0:T], AF.Square, [bkk, "PRM"], ["SQ"], bias=bias_ap)
        stage("rr-square")
        b2, b2k = bank("st")
        mm(b2[:, 0:T], BD64[:, :], SQ[:, 0:T], True, True, ["BD64", "SQ"], [b2k])
        act(R1, b2[:, 0:T], AF.Ln, [b2k, "EPSV"], [("FS", 2)], bias=EPSV[:, 1:2], scale=1.0 / HD)
        stage("rr-ln")
        act(R1, R1, AF.Exp, [("FS", 2)], [("FS", 2)], scale=-0.5)
        stage("rr-exp")
        stt(G1, G0, gain_ap, R1, ALU.mult, ALU.mult, [("FS", 0), ("FS", 2), "PRM"], [("FS", 1)])
        stage("rr-stt")
        if fout is not None:
            dma("sp", fout, G1, [("FS", 1)], [okey()])
        if not rope:
            for (ps, d_ap, dk) in dst:
                cp("pool", d_ap, G1[ps, :], [("FS", 1)], dk)
            return
        b3, b3k = bank("st")
        mm(b3[:, 0:T], PERM[:, :], G1, True, True, ["PERM", ("FS", 1)], [b3k])
        R2, R3 = FS[:, 3, 0:T], FS[:, 4, 0:T]
        tt("dve", R2, G1, COS[:, 0:T], ALU.mult, [("FS", 1), "ROPE"], [("FS", 3)])
        tt("dve", R3, b3[:, 0:T], SIN[:, 0:T], ALU.mult, [b3k, "ROPE"], [("FS", 4)])
        for (ps, d_ap, dk) in dst:
            tt("pool", d_ap, R2[ps, :], R3[ps, :], ALU.add, [("FS", 3), ("FS", 4)], dk)

    def attn_finish(ob_, obk, po, db_, dbk, pd, T, mt, pb, gate_j, c0=0):
        RC, TMP = FS[:, 5, 0:T], FS[:, 6, 0:T]
        qo = slice(po, po + 64)
        qb = slice(pb, pb + 64)
        hk = "lo" if po == 0 else "hi"
        act(RC[qo, :], db_[pd:pd + 64, 0:T], AF.Ln, [dbk], [("FS", 5, hk)])
        act(RC[qo, :], RC[qo, :], AF.Exp, [("FS", 5, hk)], [("FS", 5, hk)], scale=-1.0)
        tt("dve", TMP[qo, :], ob_[qo, 0:T], RC[qo, :], ALU.mult, [obk, ("FS", 5, hk)], [("FS", 6, hk)])
        if po != pb:
            hk2 = "lo" if pb == 0 else "hi"
            cp("act", TMP[qb, :], TMP[qo, :], [("FS", 6, hk)], [("FS", 6, hk2)])
            hk = hk2
        tt("dve", MIX[qb, mt, c0:c0 + T], TMP[qb, :], GATES[qb, gate_j, c0:c0 + T], ALU.mult,
           [("FS", 6, hk), ("G", gate_j)], [("MIX", mt, pb // 64)])

    def full_attn(T, nkt, qf, kf, vf, mt, pb, gate_j, c0=0):
        ob_, obk = bank("o")
        q_ap, qk = qf()
        sb_ = {}

        def qk_exp(i):
            bk, bkk = bank("mm")
            k_ap, kk = kf(i)
            mm(bk[:, 0:T], k_ap, q_ap, True, True, qk + kk, [bkk])
            slot = cnt["pt"] % NPT
            cnt["pt"] += 1
            act(PT[:, slot, 0:T], bk[:, 0:T], AF.Exp, [bkk], [("PT", slot)], scale=HD ** -0.5)
            sb_[i] = slot

        def pv_(i):
            v_ap, vk = vf(i)
            slot = sb_[i]
            mm(ob_[:, 0:T], v_ap, PT[:, slot, 0:T], i == 0, i == nkt - 1, vk + [("PT", slot)], [obk])

        LOOK = 2
        for i in range(nkt + LOOK):
            if i < nkt:
                qk_exp(i)
            if i >= LOOK:
                pv_(i - LOOK)
        attn_finish(ob_, obk, 0, ob_, obk, 64, T, mt, pb, gate_j, c0)

    assert NPS % 2 == 0 and LP == 256
    jobs = [("P", 2 * LP, NPS // 2, LP), ("S", 512, LS // 512, LS)]
    seq_jl = [(jb[0], l) for jb in jobs for l in range(nl)]

    def load_weights_A(l):
        dma("pool", WA[:, :, :], win_d[l].rearrange("(kt p) n -> p kt n", p=128)[:, :, 0:1536], [], ["WA"])
        dma("pool", BROW[0:1, :], brow_d[:, l * 384:(l + 1) * 384], [], ["BROW"])

    def load_weights_B(l):
        dma("pool", WB[:, :, :], win_d[l].rearrange("(kt p) n -> p kt n", p=128)[:, :, 1536:3072], [], ["WB"])
        dma("pool", WO[:, :, :], wout_d[l].rearrange("(kt p) n -> p kt n", p=128), [], ["WO"])
        dma("pool", CPW[:, :, :], cpw_d[l].rearrange("(t p) n -> p t n", p=128), [], ["CPW"])
        dma("pool", PWBD[:, :, :], pwbd_d[l].rearrange("t p n -> p t n"), [], ["PWBD"])

    load_weights_A(0)
    for (job, T, NB, L) in jobs:
        jc = 0 if job == "P" else 1
        if job == "S":
            S.add("pool", lambda e: e.memset(DUMMY[:, 0:1], 0.0), [],
                  [("NKP", x) for x in range(NPS)] + [("NVP", x) for x in range(TP // 128)]
                  + [("NKW", 0), ("NKW", 1), "NVWe", "NVWo", "NVC"])
        for l in range(nl):
            lvp = seq_jl.index((job, l)) % 2
            LV = LV2[:, lvp]
            LVK = ("LV", lvp)
            ts("dve", LV[:, 0, :], MODT[:, l, 8:16, jc], 1.0, None, ALU.add, None, [("MODT", l)], [LVK])
            cp("dve", LV[:, 1, :], MODT[:, l, 0:8, jc], [("MODT", l)], [LVK])
            cp("dve", LV[:, 2, :], MODT[:, l, 16:24, jc], [("MODT", l)], [LVK])
            tt("dve", LV[:, 3, :], MODT[:, l, 16:24, jc], PRM[:, P("bout", l):P("bout", l) + 8], ALU.mult,
               [("MODT", l), "PRM"], [LVK])
            load_weights_B(l)
            if job == "S":
                dma("pool", NKC[:, :, :], nkc_d[l].rearrange("(t p) s -> p t s", p=128), [], ["NKC"])
                dma("pool", NVC[:, :, :], nvc_d[l].rearrange("(i p) c -> p i c", p=128), [], ["NVC"])
                dma("pool", GK[:, LS:LS + PAST], gkc_d[l], [], [("GK", "ctx")])
                for i in range(PAST // 128):
                    dma("pool", GV[:, LS // 128 + i, :, 0:64],
                        gvc_d[l][i * 128:(i + 1) * 128, :].rearrange("p (k d) -> p k d", k=2), [],
                        [("GV", LS // 128 + i)])
                for h in range(4):
                    F = M32.rearrange("p s w -> p (s w)")[:, 0:896]
                    mk_ = [("MIX", j, hh) for j in range(8) for hh in range(2)]
                    dma("sp", F, bm_d[l, h], [], mk_)
                    act(F, F, AF.Exp, mk_, mk_)
                    tt("dve", EM[:, h, :].rearrange("p (m c) -> p m c", c=64), F.rearrange("p (m c) -> p m c", c=64),
                       CMASK[:, :].unsqueeze(1).to_broadcast([128, 14, 64]), ALU.mult,
                       mk_ + ["CMASK"], [("EM", h)])

            stage(f"{job}{l}-setup")
            for b in range(NB):
                t0 = b * T
                CW = L if job == "P" else T
                NCH = T // CW
                sbase = (b * NCH * SEQW["P"]) if job == "P" else 0
                tin = 0 if job == "P" else t0

                def scr_dst(scr, t):
                    if NCH == 1:
                        return scr[t * 128:(t + 1) * 128, sbase + PADW + tin:sbase + PADW + tin + T]
                    return scr[t * 128:(t + 1) * 128, sbase:sbase + NCH * (CW + 2 * PADW)].rearrange(
                        "p (s w) -> p s w", w=CW + 2 * PADW)[:, :, PADW:PADW + CW]

                def chunked(ap):
                    return ap if NCH == 1 else ap.rearrange("p (s w) -> p s w", w=CW)
                hb = "H" if b % 2 == 0 else "MIX"
                if b == 0:
                    stage_h(job, l, t0, T)
                else:
                    load_x(job, l, t0, T)
                    h_from_xb(T, hb)
                if job == "S":
                    dma("sp", COS[:, 0:T], rope_d[0][:, t0:t0 + T], [], ["ROPE"])
                    dma("sp", SIN[:, 0:T], rope_d[1][:, t0:t0 + T], [], ["ROPE"])
                stage(f"{job}{l}A{b}-loadxh")
                for t in range(2):
                    bk, bkk = proj_fm(WA, "WA", t, T, hb)
                    ts("dve", GATES[:, t, 0:T], bk[:, 0:T], pv("bin", l, t), None, ALU.add, None, [bkk, "PRM"],
                       [("G", t)])
                    dma("sp", scr_dst(axs[job], t), chunked(GATES[:, t, 0:T]), [("G", t)], [("ax", job, b, t)])
                stage(f"{job}{l}A{b}-ax")
                for t in range(2):
                    bk, bkk = proj_fm(WA, "WA", 2 + t, T, hb)
                    if job == "S":
                        ts("dve", GATES[:, 4 + t, 0:T], bk[:, 0:T], pv("bin", l, 2 + t), None, ALU.add, None,
                           [bkk, "PRM"], [("G", 4 + t)])
                        dma("sp", nks[t * 128:(t + 1) * 128, t0:t0 + T], GATES[:, 4 + t, 0:T], [("G", 4 + t)],
                            [("nks", b, t)])
                    else:
                        F = FS[:, 5 + t, 0:T]
                        ts("dve", F, bk[:, 0:T], pv("bin", l, 2 + t), None, ALU.add, None, [bkk, "PRM"],
                           [("FS", 5 + t)])
                        dma("sp", nkT_o[l, t * 128:(t + 1) * 128, t0:t0 + T], F, [("FS", 5 + t)], [okey()])
                        cp("pool", NKP[:, t, t0:t0 + T], F, [("FS", 5 + t)], [("NKP", b)])
                stage(f"{job}{l}A{b}-nk")
                for t in range(2):
                    bv, bvk = proj_fm(WA, "WA", 4 + t, T, hb)
                    bg, bgk = proj_fm(WA, "WA", 6 + t, T, hb)
                    SG = FS[:, 5 + t, 0:T]
                    act(SG, bg[:, 0:T], AF.Sigmoid, [bgk, "PRM"], [("FS", 5 + t)], bias=pv("bin", l, 6 + t))
                    stt(GATES[:, 2 + t, 0:T], bv[:, 0:T], pv("bin", l, 4 + t), SG, ALU.add, ALU.mult,
                        [bvk, ("FS", 5 + t), "PRM"], [("G", 2 + t)])
                    dma("sp", scr_dst(zs[job], t), chunked(GATES[:, 2 + t, 0:T]), [("G", 2 + t)], [("z", job, b, t)])
                stage(f"{job}{l}A{b}-z")
                bk, bkk = proj_fm(WA, "WA", 8, T, hb)
                rms_rope(bk, bkk, pv("bin", l, 8), pv("kn", l), T, job == "S",
                         [(slice(0, 128), GK[:, t0:t0 + T], [("GK", b)])],
                         fout=(gkT_o[l, :, t0:t0 + T] if job == "P" else None))
                stage(f"{job}{l}A{b}-gk")
                for tt_ in range(T // 128):
                    bk, bkk = bank("o")
                    mm(bk[:, 0:384], ONESB[:, :], BROW[:, :], True, False,
                       ["ONESB", "BROW"], [bkk])
                    for kt in range(8):
                        mm(bk[:, 0:384], HB[hb][:, kt, tt_ * 128:(tt_ + 1) * 128], WA[:, kt, 1152:1536], False, kt == 7,
                           HK[hb](kt) + ["WA"], [bkk])
                    gtile = (t0 + tt_ * 128) // 128
                    if job == "S":
                        stg = GATES[:, 6:8, :].rearrange("p a w -> p (a w)")[:, tt_ * 256:(tt_ + 1) * 256]
                        cp("act", stg, bk[:, 0:256], [bkk], [("G", 6 + tt_ // 2)])
                        dma("sp", nvs[t0 + tt_ * 128:t0 + (tt_ + 1) * 128, :], stg, [("G", 6 + tt_ // 2)],
                            [("nvs", b, tt_)])
                        cp("dve", GV[:, gtile, :, 0:64], bk[:, 256:384].rearrange("p (k d) -> p k d", k=2), [bkk],
                           [("GV", gtile)])
                    else:
                        F = FS[:, 3 + (tt_ % 2), 0:384]
                        cp("act", F, bk[:, 0:384], [bkk], [("FS", 3 + (tt_ % 2))])
                        dma("sp", nv_o[l, t0 + tt_ * 128:t0 + (tt_ + 1) * 128, :], F[:, 0:256],
                            [("FS", 3 + (tt_ % 2))], [okey()])
                        dma("sp", gv_o[l, t0 + tt_ * 128:t0 + (tt_ + 1) * 128, :], F[:, 256:384],
                            [("FS", 3 + (tt_ % 2))], [okey()])
                        cp("pool", NVP[:, gtile, :, 0:64], F[:, 0:256].rearrange("p (h d) -> p h d", h=4),
                           [("FS", 3 + (tt_ % 2))], [("NVP", gtile)])
                        cp("pool", GV[:, gtile, :, 0:64], F[:, 256:384].rearrange("p (k d) -> p k d", k=2),
                           [("FS", 3 + (tt_ % 2))], [("GV", gtile)])

            stage(f"{job}{l}-phaseA")
            nxt = seq_jl.index((job, l)) + 1
            if nxt < len(seq_jl):
                load_weights_A(seq_jl[nxt][1])

            for b in range(NB):
                t0 = b * T
                CW = L if job == "P" else T
                NCH = T // CW
                CP = CW + 2 * PADW
                sbase = (b * NCH * SEQW["P"]) if job == "P" else 0
                tin = 0 if job == "P" else t0
                if b == 0:
                    stage_h(job, l, t0, T)
                load_x(job, l, t0, T)
                W = NCH * CP
                for t in range(2):
                    nb_ = [b] if job == "P" else [x for x in (b - 1, b, b + 1) if 0 <= x < NB]
                    sqs = [b * NCH + x for x in range(NCH)] if job == "P" else [0]
                    dma("sp", AXW[:, t, 0:W], axs[job][t * 128:(t + 1) * 128, sbase + tin:sbase + tin + W],
                        [("ax", job, x, t) for x in nb_]
                        + [("pad", job, sq_, id(axs[job]), t, sd) for sd in (0, 1) for sq_ in sqs], [("AXW", t)])
                    dma("sp", ZW[:, t, 0:W], zs[job][t * 128:(t + 1) * 128, sbase + tin:sbase + tin + W],
                        [("z", job, x, t) for x in nb_]
                        + [("pad", job, sq_, id(zs[job]), t, sd) for sd in (0, 1) for sq_ in sqs], [("ZW", t)])
                if job == "S":
                    dma("sp", COS[:, 0:T], rope_d[0][:, t0:t0 + T], [], ["ROPE"])
                    dma("sp", SIN[:, 0:T], rope_d[1][:, t0:t0 + T], [], ["ROPE"])
                    qr0 = t0 // GW
                    rmin = nat_rs(qr0, R)
                    rmax = nat_rs(qr0 + 7, R) + 8
                    nrows = rmax - rmin
                    for t in range(2):
                        dma("sp", NKW[:, t, 0:nrows * 64], nks[t * 128:(t + 1) * 128, rmin * 64:rmax * 64],
                            [("nks", x, t) for x in range(rmin // 8, (rmax - 1) // 8 + 1)], [("NKW", t)])
                    ne = nrows // 2
                    no = (nrows - 1) // 2
                    dma("sp", NVWe[:, 0:ne, :],
                        nvs[rmin * 64:rmin * 64 + ne * 128, :].rearrange("(j p) c -> p j c", p=128),
                        [("nvs", x, y) for x in range(rmin // 8, (rmax - 1) // 8 + 1) for y in range(4)], ["NVWe"])
                    dma("sp", NVWo[:, 0:no, :],
                        nvs[(rmin + 1) * 64:(rmin + 1) * 64 + no * 128, :].rearrange("(j p) c -> p j c", p=128),
                        [("nvs", x, y) for x in range(rmin // 8, (rmax - 1) // 8 + 1) for y in range(4)], ["NVWo"])
                for j in range(8):
                    bk, bkk = proj_fm(WB, "WB", j, T)
                    act(GATES[:, j, 0:T], bk[:, 0:T], AF.Silu, [bkk, "PRM"], [("G", j)], bias=pv("bin", l, 12 + j))
                for t in range(2):
                    bk, bkk = proj_fm(WB, "WB", 8 + t, T)
                    for hh in range(2):
                        ps = slice(64 * hh, 64 * hh + 64)
                        ts("dve", NQM[ps, 2 * t + hh, 0:T], bk[ps, 0:T], PRM[ps, P("bin", l, 20 + t):P("bin", l, 20 + t) + 1],
                           None, ALU.add, None, [bkk, "PRM"], [("NQ", 2 * t + hh)])
                for t in range(2):
                    bk, bkk = proj_fm(WB, "WB", 10 + t, T)
                    rms_rope(bk, bkk, pv("bin", l, 22 + t), pv("qn", l), T, job == "S",
                             [(slice(0, 64), GQM[0:64, t, 0:T], [("GQ", t)]),
                              (slice(64, 128), GQM[64:128, t + 2, 0:T], [("GQ", t + 2)])])

                stage(f"{job}{l}b{b}-proj")
                edge_l = (tin == 0)
                edge_r = (tin + CW == L)
                for t in range(2):
                    X = AXW[:, t, :]
                    S2, S4, S8, S16 = FS[:, 0, :], FS[:, 1, :], FS[:, 2, :], FS[:, 3, :]
                    tt("pool", S2[:, 1:W], X[:, 0:W - 1], X[:, 1:W], ALU.add, [("AXW", t)], [("FS", 0)])
                    if t == 0:
                        tt("pool", S4[64:128, 2:W - 1], S2[64:128, 1:W - 2], S2[64:128, 3:W], ALU.add, [("FS", 0)],
                           [("FS", 1)])
                        fin = [(0, S2, ("FS", 0)), (64, S4, ("FS", 1))]
                    else:
                        tt("pool", S4[:, 2:W - 1], S2[:, 1:W - 2], S2[:, 3:W], ALU.add, [("FS", 0)], [("FS", 1)])
                        tt("pool", S8[:, 4:W - 3], S4[:, 2:W - 5], S4[:, 6:W - 1], ALU.add, [("FS", 1)], [("FS", 2)])
                        tt("pool", S16[64:128, 8:W - 7], S8[64:128, 4:W - 11], S8[64:128, 12:W - 3], ALU.add,
                           [("FS", 2)], [("FS", 3)])
                        fin = [(0, S8, ("FS", 2)), (64, S16, ("FS", 3))]
                    for (p0, Sx, sk) in fin:
                        ps = slice(p0, p0 + 64)
                        for ch in range(NCH):
                            wo, yo = ch * CP + PADW, ch * CW
                            stt(YP[ps, t, yo:yo + CW], Sx[ps, wo:wo + CW],
                                PRM[ps, P("invwin", 0, t):P("invwin", 0, t) + 1],
                                X[ps, wo:wo + CW], ALU.mult, ALU.subtract, [sk, ("AXW", t), "PRM"], [("YP", t, p0)])
                            for (flag, c0, nm) in ((edge_l, 0, "edgeL"), (edge_r, CW - 8, "edgeR")):
                                if not flag:
                                    continue
                                E8 = FS[ps, 4, 0:8]
                                o_e = P(nm, 0, t * 8)
                                tt("dve", E8, Sx[ps, wo + c0:wo + c0 + 8], PRM[ps, o_e:o_e + 8], ALU.mult,
                                   [sk, "PRM"], [("FS", 4)])
                                tt("dve", YP[ps, t, yo + c0:yo + c0 + 8], E8, X[ps, wo + c0:wo + c0 + 8], ALU.subtract,
                                   [("FS", 4), ("AXW", t)], [("YP", t, p0)])
                    bk, bkk = bank("mm")
                    mm(bk[:, 0:T], PWBD[:, t, :], YP[:, t, 0:T], True, True, ["PWBD", ("YP", t, 0), ("YP", t, 64)],
                       [bkk])
                    stt(MIX[:, t, 0:T], bk[:, 0:T], pv("pscale", l, t), GATES[:, t, 0:T], ALU.mult, ALU.mult,
                        [bkk, "PRM", ("G", t)], [("MIX", t, 0), ("MIX", t, 1)])

                stage(f"{job}{l}b{b}-mixA")
                ZC = [FS[:, 0, 0:T], FS[:, 1, 0:T]]
                s1, s1k = bank("st")
                s2, s2k = bank("st")
                for t in range(2):
                    bk, bkk = bank("mm")
                    for k in range(31):
                        slot = cnt["dg"] % NDG
                        cnt["dg"] += 1
                        ts("pool", DG[:, slot, :], IDB[:, :], pv("convw", l, k * 2 + t), 0.0, ALU.mult, ALU.add,
                           ["IDB", "PRM"], [("DG", slot)])
                        for ch in range(NCH):
                            mm(bk[:, ch * CW:(ch + 1) * CW], DG[:, slot, :], ZW[:, t, ch * CP + k + 1:ch * CP + k + 1 + CW],
                               k == 0 and ch == 0, k == 30 and ch == NCH - 1, [("DG", slot), ("ZW", t)], [bkk])
                    act(ZC[t], bk[:, 0:T], AF.Identity, [bkk, "PRM"], [("FS", t)], bias=pv("convb", l, t))
                    act(SQ[:, 0:T], bk[:, 0:T], AF.Square, [bkk, "PRM"], ["SQ"], bias=pv("convb", l, t))
                    cp("dve", VBF[:, 0:T], ZC[t], [("FS", t)], ["VBF"])
                    mm(s1[:, 0:T], ONESB[:, :], VBF[:, 0:T], t == 0, t == 1, ["ONESB", "VBF"], [s1k])
                    mm(s2[:, 0:T], ONESB[:, :], SQ[:, 0:T], t == 0, t == 1, ["ONESB", "SQ"], [s2k])
                MEAN, MSQ, RSTD = FS[:, 2, 0:T], FS[:, 3, 0:T], FS[:, 4, 0:T]

                def ln_stats(n, epscol):
                    act(MEAN, s1[:, 0:T], AF.Identity, [s1k], [("FS", 2)], scale=1.0 / n)
                    act(MSQ, s1[:, 0:T], AF.Square, [s1k], [("FS", 3)], scale=1.0 / n)
                    stt(RSTD, s2[:, 0:T], 1.0 / n, MSQ, ALU.mult, ALU.subtract, [s2k, ("FS", 3)], [("FS", 4)])
                    act(RSTD, RSTD, AF.Ln, [("FS", 4), "EPSV"], [("FS", 4)], bias=EPSV[:, epscol:epscol + 1])
                    act(RSTD, RSTD, AF.Exp, [("FS", 4)], [("FS", 4)], scale=-0.5)

                ln_stats(256.0, 0)
                for t in range(2):
                    tt("dve", ZC[t], ZC[t], MEAN, ALU.subtract, [("FS", t), ("FS", 2)], [("FS", t)])
                    tt("dve", ZC[t], ZC[t], RSTD, ALU.mult, [("FS", t), ("FS", 4)], [("FS", t)])
                    act(ZN[:, t, 0:T], ZC[t], AF.Silu, [("FS", t), "PRM"], [("ZN", t)], bias=pv("clnb", l, t),
                        scale=pv("clng", l, t))
                for ot in range(2):
                    bk, bkk = bank("mm")
                    for t in range(2):
                        mm(bk[:, 0:T], CPW[:, t, ot * 128:(ot + 1) * 128], ZN[:, t, 0:T], t == 0, t == 1,
                           ["CPW", ("ZN", t)], [bkk])
                    tt("dve", MIX[:, 4 + ot, 0:T], bk[:, 0:T], GATES[:, 4 + ot, 0:T], ALU.mult, [bkk, ("G", 4 + ot)],
                       [("MIX", 4 + ot, 0), ("MIX", 4 + ot, 1)])

                stage(f"{job}{l}b{b}-mixC")
                if job == "P":
                    nkt = L // 128
                    for ch in range(NCH):
                        cb = t0 + ch * CW
                        for h in range(4):
                            th, pb = h // 2, 64 * (h % 2)
                            full_attn(CW, nkt,
                                      lambda h=h, ch=ch: (NQM[:, h, ch * CW:(ch + 1) * CW], [("NQ", h)]),
                                      lambda i, th=th, cb=cb: (NKP[:, th, cb + i * 128:cb + (i + 1) * 128], [("NKP", b)]),
                                      lambda i, h=h, cb=cb: (NVP[:, cb // 128 + i, h, :], [("NVP", cb // 128 + i)]),
                                      2 + th, pb, 2 + th, c0=ch * CW)
                else:
                    for h in range(4):
                        th, pb = h // 2, 64 * (h % 2)
                        ob_, obk = bank("o")
                        db_, dbk = bank("st")
                        nrow = T // GW
                        st = {}
                        cst = {}

                        def ctx_qk(i, th=th, h=h):
                            bk, bkk = bank("mm")
                            mm(bk[:, 0:T], NKC[:, th, i * 128:(i + 1) * 128], NQM[:, h, 0:T], True, True,
                               ["NKC", ("NQ", h)], [bkk])
                            slot = cnt["pt"] % NPT
                            cnt["pt"] += 1
                            act(PT[:, slot, 0:T], bk[:, 0:T], AF.Exp, [bkk], [("PT", slot)], scale=HD ** -0.5)
                            cst[i] = slot

                        def ctx_pv(i, h=h, th=th):
                            slot = cst[i]
                            mm(ob_[:, 0:T], NVC[:, i, th * 128:(th + 1) * 128], PT[:, slot, 0:T], i == 0, False,
                               ["NVC", ("PT", slot)], [obk])
                            mm(db_[:, 0:T], ONESB[:, :], PT[:, slot, 0:T], i == 0, False, ["ONESB", ("PT", slot)], [dbk])

                        nctx = PAST // 128
                        for i in range(nctx + 2):
                            if i < nctx:
                                ctx_qk(i)
                            if i >= 2:
                                ctx_pv(i - 2)

                        def nat_qk(r, th=th, h=h):
                            qr = qr0 + r
                            rs = nat_rs(qr, R)
                            off = rs - rmin
                            bk, bkk = bank("mm")
                            for j in range(4):
                                mm(bk[:, j * 64:(j + 1) * 64], NKW[:, th, (off + 2 * j) * 64:(off + 2 * j + 2) * 64],
                                   NQM[:, h, r * 64:(r + 1) * 64], True, True, [("NKW", th), ("NQ", h)], [bkk])
                            slot = cnt["pt"] % NPT
                            cnt["pt"] += 1
                            act(PT[:, slot, 0:256], bk[:, 0:256], AF.Exp, [bkk], [("PT", slot)], scale=HD ** -0.5)
                            e = qr - rs
                            m0 = 7 - e
                            tt("dve", PT[:, slot, 0:256].rearrange("p (j c) -> p j c", c=64),
                               PT[:, slot, 0:256].rearrange("p (j c) -> p j c", c=64),
                               EM[:, h, :].rearrange("p (m c) -> p m c", c=64)[:, m0:m0 + 7:2, :],
                               ALU.mult, [("PT", slot), ("EM", h)], [("PT", slot)])
                            st[r] = (slot, off)

                        def nat_pv(r, h=h, th=th):
                            slot, off = st[r]
                            c0 = r * 64
                            for j in range(4):
                                if off % 2 == 0:
                                    v_ap, vk = NVWe[:, off // 2 + j, th * 128:(th + 1) * 128], "NVWe"
                                else:
                                    v_ap, vk = NVWo[:, (off - 1) // 2 + j, th * 128:(th + 1) * 128], "NVWo"
                                mm(ob_[:, c0:c0 + 64], v_ap, PT[:, slot, j * 64:(j + 1) * 64], False,
                                   (r == nrow - 1 and j == 3), [vk, ("PT", slot)], [obk])
                            for j in range(4):
                                mm(db_[:, c0:c0 + 64], ONESB[:, :], PT[:, slot, j * 64:(j + 1) * 64], False,
                                   (r == nrow - 1 and j == 3), ["ONESB", ("PT", slot)], [dbk])

                        LK = 2
                        for r in range(nrow + LK):
                            if r < nrow:
                                nat_qk(r)
                            if r >= LK:
                                nat_pv(r - LK)
                        attn_finish(ob_, obk, pb, db_, dbk, pb, T, 2 + th, pb, 2 + th)

                stage(f"{job}{l}b{b}-mixB")
                nkt = (L // 128) if job == "P" else NKT_S
                for ch in range(NCH):
                    kbase = (t0 + ch * CW) if job == "P" else 0
                    for h in range(4):
                        kv = h // 2
                        mt, pb = 6 + h // 2, 64 * (h % 2)
                        full_attn(CW, nkt,
                                  lambda h=h, ch=ch: (GQM[:, h, ch * CW:(ch + 1) * CW], [("GQ", h)]),
                                  lambda i, kbase=kbase: (GK[:, kbase + i * 128:kbase + (i + 1) * 128],
                                                          [("GK", x) for x in (list(range(NB)) + ["ctx"])] if job == "S"
                                                          else [("GK", b)]),
                                  lambda i, kv=kv, kbase=kbase: (GV[:, kbase // 128 + i, kv, :],
                                                                 [("GV", kbase // 128 + i)]),
                                  mt, pb, 6 + h // 2, c0=ch * CW)

                stage(f"{job}{l}b{b}-mixD")
                s1, s1k = bank("st")
                s2, s2k = bank("st")
                for ot in range(8):
                    bk, bkk = bank("mm")
                    for kt in range(8):
                        mm(bk[:, 0:T], WO[:, kt, ot * 128:(ot + 1) * 128], MIX[:, kt, 0:T], kt == 0, kt == 7,
                           ["WO", ("MIX", kt, 0), ("MIX", kt, 1)], [bkk])
                    T1 = FS[:, 5 + (ot % 2), 0:T]
                    act(T1, bk[:, 0:T], AF.Identity, [bkk, LVK], [("FS", 5 + (ot % 2))], bias=LV[:, 3, ot:ot + 1],
                        scale=LV[:, 2, ot:ot + 1])
                    stt(XB[:, ot, 0:T], XB[:, ot, 0:T], float(ALPHA), T1, ALU.mult, ALU.add,
                        [("XB", ot), ("FS", 5 + (ot % 2))], [("XB", ot)])
                    act(SQ[:, 0:T], XB[:, ot, 0:T], AF.Square, [("XB", ot)], ["SQ"])
                    cp("dve", VBF[:, 0:T], XB[:, ot, 0:T], [("XB", ot)], ["VBF"])
                    mm(s1[:, 0:T], ONESB[:, :], VBF[:, 0:T], ot == 0, ot == 7, ["ONESB", "VBF"], [s1k])
                    mm(s2[:, 0:T], ONESB[:, :], SQ[:, 0:T], ot == 0, ot == 7, ["ONESB", "SQ"], [s2k])
                if b + 1 < NB:
                    stage_h(job, l, t0 + T, T)
                ln_stats(float(D), 0)
                last = (l == nl - 1)
                dst = (yT_p if job == "P" else yT_s) if last else xs[(job, (l + 1) % 2)]
                for ot in range(8):
                    tt("dve", XB[:, ot, 0:T], XB[:, ot, 0:T], MEAN, ALU.subtract, [("XB", ot), ("FS", 2)],
                       [("XB", ot)])
                    tt("dve", XB[:, ot, 0:T], XB[:, ot, 0:T], RSTD, ALU.mult, [("XB", ot), ("FS", 4)], [("XB", ot)])
                    ts("dve", XB[:, ot, 0:T], XB[:, ot, 0:T], pv("lng", l, ot), pv("lnb", l, ot), ALU.mult, ALU.add,
                       [("XB", ot), "PRM"], [("XB", ot)])
                wkey = okey() if last else ("xs", job, (l + 1) % 2, b)
                dma("sp", dst.rearrange("(kt p) t -> p kt t", p=128)[:, :, t0:t0 + T], XB[:, :, 0:T],
                    [("XB", k) for k in range(8)], [wkey])

    S.stopped = False
    S.add("sp", lambda e: e.nop(), [k for k in S.outkeys if k in S.lastw], [])
    S.emit(nc, stack)
    stack.close()
    return nc


def host_constants(LS, LP):
    ident = np.eye(128, dtype=np.float32)
    ones = np.ones((128, 128), np.float32)
    bd = np.zeros((128, 128), np.float32)
    bd[:64, :64] = 1.0
    bd[64:, 64:] = 1.0
    perm = np.zeros((128, 128), np.float32)
    for m in range(128):
        d = m % 64
        blk = d // 16
        partner = m + 16 if blk % 2 == 0 else m - 16
        perm[partner, m] = 1.0
    consts = np.stack([ident, ones, bd, perm]).astype(np.float32)
    half, nf = HD // 2, HD // 4
    t = np.arange(LS)
    inv = (10000.0 ** (-np.arange(nf, dtype=np.float32) * 2.0 / half)).astype(np.float32)
    cos = np.zeros((128, LS), np.float32)
    sin = np.zeros((128, LS), np.float32)
    for p in range(128):
        d = p % 64
        pos = (t // GW) if d < half else (t % GW)
        dd = d % half
        f = dd % nf
        ang = pos.astype(np.float32) * inv[f]
        cos[p] = np.cos(ang).astype(np.float32)
        s = np.sin(ang).astype(np.float32)
        sin[p] = -s if dd < nf else s
    rope = np.stack([cos, sin]).astype(np.float32)
    col = np.arange(GW)
    cs = np.clip(col - 8, 0, GW - 16)
    col_in = (col[None, :] >= cs[:, None]) & (col[None, :] < cs[:, None] + 16)
    cmask = np.zeros((128, 64), np.float32)
    for krl in range(2):
        cmask[krl * 64:(krl + 1) * 64, :] = col_in.T.astype(np.float32)
    return consts, rope, cmask


def nat_master(nat_bias):
    nl = nat_bias.shape[0]
    krl = np.arange(2)[:, None, None, None]
    kc = np.arange(64)[None, :, None, None]
    m = np.arange(14)[None, None, :, None]
    qc = np.arange(64)[None, None, None, :]
    dr = np.broadcast_to(krl + m, (2, 64, 14, 64))
    dc = np.broadcast_to(np.clip(kc - qc, -15, 15) + 15, (2, 64, 14, 64))
    out = nat_bias[:, :, dr, dc]
    return np.ascontiguousarray(out.reshape(nl, 4, 128, 14 * 64)).astype(np.float32)


def build_params(nl, inp, c_vec, c_ctx, bperm, L_for_edges=None):
    PO, NPAR = param_layout(nl)
    prm = np.zeros((128, NPAR), np.float32)

    def put(name, l, arr):
        o, c = PO[name]
        prm[:, o + l * c:o + l * c + c] = arr.T

    for l in range(nl):
        put("bmod", l, inp["b_mod"][l].reshape(24, 128))
        put("bin", l, bperm[l].reshape(24, 128))
        put("pscale", l, inp["pool_scale"][l].reshape(2, 128))
        put("convw", l, inp["conv_w"][l].reshape(31, 2, 128).reshape(62, 128))
        put("convb", l, inp["conv_b"][l].reshape(2, 128))
        put("clng", l, inp["conv_ln_g"][l].reshape(2, 128))
        put("clnb", l, inp["conv_ln_b"][l].reshape(2, 128))
        put("bout", l, inp["b_out"][l].reshape(8, 128))
        put("lng", l, inp["ln_g"][l].reshape(8, 128))
        put("lnb", l, inp["ln_b"][l].reshape(8, 128))
        put("qn", l, np.tile(inp["q_norm"][l], 2)[None, :])
        put("kn", l, np.tile(inp["k_norm"][l], 2)[None, :])
    o, _ = PO["cond"]
    cc = np.stack([c_ctx.reshape(8, 128), c_vec.reshape(8, 128)], axis=-1)
    prm[:, o:o + 16] = cc.transpose(1, 0, 2).reshape(128, 16)
    wins = [(2, 4), (8, 16)]
    o, _ = PO["invwin"]
    oL, _ = PO["edgeL"]
    oR, _ = PO["edgeR"]
    for t in range(2):
        for half in range(2):
            w = wins[t][half]
            ps = slice(half * 64, half * 64 + 64)
            prm[ps, o + t] = 1.0 / w
            for i in range(8):
                cl = i + w // 2 - max(i - w // 2, 0)
                prm[ps, oL + t * 8 + i] = 1.0 / cl
                cr = min(w, 8 - i + w // 2)
                prm[ps, oR + t * 8 + i] = 1.0 / cr
    return prm


def prepare_inputs(inp, nl=NL, LS=4096, NPS=4, LP=256, n_cores=N_CORES):
    perm = col_perm()
    f = lambda a: np.ascontiguousarray(np.asarray(a, dtype=np.float32))
    inp = {k: f(v) for k, v in inp.items()}
    w_in_p = np.ascontiguousarray(inp["w_in"][:nl][:, :, perm])
    bperm = inp["b_in"][:nl][:, perm]
    b_row = np.ascontiguousarray(bperm[:, 1152:1536].reshape(1, nl * 384))
    consts, rope, cmask = host_constants(LS, LP)
    pool_bd = np.zeros((nl, 2, 128, 128), np.float32)
    for l in range(nl):
        for g in range(4):
            t, hh = g // 2, g % 2
            pool_bd[l, t, hh * 64:(hh + 1) * 64, hh * 64:(hh + 1) * 64] = inp["pool_w"][l, g]
    bm = nat_master(inp["nat_bias"][:nl])
    shared = dict(w_mod=np.ascontiguousarray(inp["w_mod"][:nl]), w_in_p=w_in_p, b_row=b_row,
                  w_out=np.ascontiguousarray(inp["w_out"][:nl]), pool_bd=pool_bd,
                  conv_pw=np.ascontiguousarray(inp["conv_pw"][:nl]), nat_bm=bm, colmask=cmask, rope=rope,
                  consts=consts)
    maps = []
    for i in range(n_cores):
        xp = inp["x_prompt"][i * NPS:(i + 1) * NPS].reshape(NPS * LP, D)
        m = dict(shared)
        m["xT_p"] = np.ascontiguousarray(xp.T)
        m["xT_s"] = np.ascontiguousarray(inp["x_sample"][i].T)
        m["params"] = build_params(nl, inp, inp["c"][i], inp["c_ctx"], bperm)
        m["nkc"] = np.ascontiguousarray(inp["cache_nat_k"][i, :nl].reshape(nl, PAST, 256).transpose(0, 2, 1))
        m["nvc"] = np.ascontiguousarray(inp["cache_nat_v"][i, :nl].reshape(nl, PAST, 256))
        m["gkc"] = np.ascontiguousarray(inp["cache_gqa_k"][i, :nl].reshape(nl, PAST, 128).transpose(0, 2, 1))
        m["gvc"] = np.ascontiguousarray(inp["cache_gqa_v"][i, :nl].reshape(nl, PAST, 128))
        maps.append(m)
    return maps


_NC_CACHE = {}


def kernel(**inputs):
    key = "full"
    if key not in _NC_CACHE:
        _NC_CACHE[key] = build_program()
    nc = _NC_CACHE[key]
    maps = prepare_inputs(inputs)
    res = run_bass_kernel_spmd(nc, maps, core_ids=list(range(N_CORES)))
    rs = res.results
    B, SEQ, DB, DSEQ = 32, 256, 8, 4096
    y_prompt = np.zeros((B, SEQ, D), np.float32)
    y_sample = np.zeros((DB, DSEQ, D), np.float32)
    nk = np.zeros((B, NL, SEQ, 4, HD), np.float32)
    nv = np.zeros((B, NL, SEQ, 4, HD), np.float32)
    gk = np.zeros((B, NL, SEQ, 2, HD), np.float32)
    gv = np.zeros((B, NL, SEQ, 2, HD), np.float32)
    for i, r in enumerate(rs):
        y_prompt[4 * i:4 * i + 4] = np.asarray(r["yT_p"]).T.reshape(4, SEQ, D)
        y_sample[i] = np.asarray(r["yT_s"]).T
        a = np.asarray(r["nkT_o"]).reshape(NL, 256, 4, SEQ)
        nk[4 * i:4 * i + 4] = a.transpose(2, 0, 3, 1).reshape(4, NL, SEQ, 4, HD)
        a = np.asarray(r["gkT_o"]).reshape(NL, 128, 4, SEQ)
        gk[4 * i:4 * i + 4] = a.transpose(2, 0, 3, 1).reshape(4, NL, SEQ, 2, HD)
        a = np.asarray(r["nv_o"]).reshape(NL, 4, SEQ, 256)
        nv[4 * i:4 * i + 4] = a.transpose(1, 0, 2, 3).reshape(4, NL, SEQ, 4, HD)
        a = np.asarray(r["gv_o"]).reshape(NL, 4, SEQ, 128)
        gv[4 * i:4 * i + 4] = a.transpose(1, 0, 2, 3).reshape(4, NL, SEQ, 2, HD)
    return (y_prompt, y_sample, nk, nv, gk, gv)
```

```python
import math
from contextlib import ExitStack

import numpy as np
import concourse.bass as bass
import concourse.mybir as mybir
from concourse.bass_utils import run_bass_kernel_spmd

F32 = mybir.dt.float32
BF16 = mybir.dt.bfloat16
AF = mybir.ActivationFunctionType
ALU = mybir.AluOpType

D = 1024
NL = 4
GW = 64
HD = 64
PAST = 512
LN_EPS = 1e-5
RMS_EPS = 1e-6
ALPHA = (2 * NL) ** 0.25
N_CORES = 8
PADW = 16

ENGS = ("pe", "act", "dve", "pool", "sp")
EPOCH = 12000
NRING = 12


class Op:
    __slots__ = ("eng", "idx", "gid", "fn", "deps", "gdeps", "dma", "sig", "signo", "dmano", "busy", "lat")


HOP_NS = 700.0


class Sched:
    def __init__(self):
        self.all = []
        self.ops = {e: [] for e in ENGS}
        self.lastw = {}
        self.readers = {}
        self.outkeys = []
        self.stopped = False
        self.reorder = True

    def add(self, eng, fn, reads=(), writes=(), dma=False, cost=300.0, lat=None):
        if self.stopped:
            return None
        op = Op()
        op.eng = eng
        op.gid = len(self.all)
        op.idx = -1
        op.fn = fn
        op.dma = dma
        op.sig = False
        op.signo = -1
        op.dmano = -1
        op.busy = float(cost)
        op.lat = float(cost if lat is None else lat)
        deps = set()
        for k in reads:
            lw = self.lastw.get(k)
            if lw is not None:
                deps.add(lw)
            if isinstance(k, tuple) and k[0] == "bk":
                for r in self.readers.get(k, ()):
                    if self.all[r].eng != eng:
                        deps.add(r)
        for k in writes:
            lw = self.lastw.get(k)
            if lw is not None:
                deps.add(lw)
            for r in self.readers.get(k, ()):
                deps.add(r)
        deps.discard(op.gid)
        op.gdeps = deps
        self.all.append(op)
        for k in reads:
            self.readers.setdefault(k, []).append(op.gid)
        for k in writes:
            self.lastw[k] = op.gid
            self.readers[k] = []
        return op

    def schedule(self):
        import heapq
        n = len(self.all)
        if not self.reorder:
            for op in self.all:
                op.idx = len(self.ops[op.eng])
                self.ops[op.eng].append(op)
        else:
            succ = [[] for _ in range(n)]
            indeg = [0] * n
            for op in self.all:
                indeg[op.gid] = len(op.gdeps)
                for d in op.gdeps:
                    succ[d].append(op.gid)
            bl = [0.0] * n
            for op in reversed(self.all):
                m = 0.0
                for s_ in succ[op.gid]:
                    if bl[s_] > m:
                        m = bl[s_]
                bl[op.gid] = op.lat + m
            finish = [0.0] * n
            ready_t = [0.0] * n
            future = {e: [] for e in ENGS}
            avail = {e: [] for e in ENGS}
            free = {e: 0.0 for e in ENGS}
            for op in self.all:
                if indeg[op.gid] == 0:
                    heapq.heappush(future[op.eng], (0.0, op.gid))
            done = 0
            while done < n:
                best = None
                for e in ENGS:
                    f, a = future[e], avail[e]
                    while f and f[0][0] <= free[e]:
                        g_ = heapq.heappop(f)[1]
                        heapq.heappush(a, (-bl[g_], g_))
                    if a:
                        cand = (free[e], a[0][1], e, True)
                    elif f:
                        cand = (f[0][0], f[0][1], e, False)
                    else:
                        continue
                    if best is None or cand[:2] < best[:2]:
                        best = cand
                assert best is not None, "scheduler: dependency cycle"
                st, g, e, from_avail = best
                if from_avail:
                    heapq.heappop(avail[e])
                else:
                    heapq.heappop(future[e])
                op = self.all[g]
                op.idx = len(self.ops[e])
                self.ops[e].append(op)
                free[e] = st + op.busy
                finish[g] = st + op.lat
                done += 1
                for s_ in succ[g]:
                    t = finish[g] + (HOP_NS if self.all[s_].eng != e else 250.0)
                    if t > ready_t[s_]:
                        ready_t[s_] = t
                    indeg[s_] -= 1
                    if indeg[s_] == 0:
                        heapq.heappush(future[self.all[s_].eng], (ready_t[s_], s_))
            self.model_ns = max(finish) if n else 0.0
        for e in ENGS:
            k = 0
            for op in self.ops[e]:
                if op.dma:
                    op.dmano = k
                    k += 1
        for op in self.all:
            op.deps = {(self.all[d].eng, self.all[d].idx) for d in op.gdeps
                       if not (op.eng == "pe" and self.all[d].eng == "pe")}

    def emit(self, nc, stack):
        self.schedule()
        ndma = {e: sum(1 for o in self.ops[e] if o.dma) for e in ENGS}
        for e in ENGS:
            for op in self.ops[e]:
                for (se, si) in op.deps:
                    so = self.ops[se][si]
                    if not so.dma:
                        so.sig = True
        esems = {}
        for e in ENGS:
            n = 0
            for op in self.ops[e]:
                if op.sig:
                    op.signo = n
                    n += 1
            esems[e] = [stack.enter_context(nc.semaphore(f"s_{e}_{i}")) for i in range(n // EPOCH + 1)]
        rsems = {}
        for e in ENGS:
            if ndma[e]:
                rsems[e] = [stack.enter_context(nc.semaphore(f"r_{e}_{i}")) for i in range(NRING)]
        block = stack.enter_context(nc.Block())
        deco = {"pe": block.tensor, "act": block.scalar, "dve": block.vector, "pool": block.gpsimd, "sp": block.sync}

        def resolve(se, si):
            so = self.ops[se][si]
            if so.dma:
                return (se, "r", so.dmano % NRING), rsems[se][so.dmano % NRING], 16 * (so.dmano // NRING + 1)
            return (se, "s", so.signo // EPOCH), esems[se][so.signo // EPOCH], so.signo % EPOCH + 1

        def body_for(e):
            def body(eng):
                waited = {}

                def wait(key, sem, val):
                    if waited.get(key, 0) >= val:
                        return
                    waited[key] = val
                    eng.wait_ge(sem, val)

                for op in self.ops[e]:
                    need = {}
                    for (se, si) in op.deps:
                        k_, sem_, v_ = resolve(se, si)
                        if k_ not in need or need[k_][1] < v_:
                            need[k_] = (sem_, v_)
                    for k_ in sorted(need):
                        wait(k_, need[k_][0], need[k_][1])
                    if op.dma and op.dmano >= NRING:
                        slot = op.dmano % NRING
                        wait((e, "r", slot), rsems[e][slot], 16 * (op.dmano // NRING))
                    ins = op.fn(eng)
                    if op.dma:
                        ins.then_inc(rsems[e][op.dmano % NRING], 16)
                    elif op.sig:
                        ins.then_inc(esems[e][op.signo // EPOCH], 1)
            return body

        for e in ENGS:
            deco[e](body_for(e))


def param_layout(nl):
    off = {}
    n = 0
    for name, cnt in (("bmod", 24), ("bin", 24), ("pscale", 2), ("convw", 62), ("convb", 2), ("clng", 2),
                      ("clnb", 2), ("bout", 8), ("lng", 8), ("lnb", 8), ("qn", 1), ("kn", 1)):
        off[name] = (n, cnt)
        n += cnt * nl
    for name, cnt in (("cond", 16), ("invwin", 2), ("edgeL", 16), ("edgeR", 16)):
        off[name] = (n, cnt)
        n += cnt
    return off, n


_C = dict(a_x=0, a_g=256, nq=512, nk=768, nv=1024, n_g=1280, c_v=1536, c_glu=1792, c_g=2048, gq=2304, gk=2560,
          gv=2688, g_g=2816)


def col_perm():
    r = lambda a, n: list(range(a, a + n))
    A = r(_C["a_x"], 256) + r(_C["nk"], 256) + r(_C["c_v"], 256) + r(_C["c_glu"], 256) + r(_C["gk"], 128) \
        + r(_C["nv"], 256) + r(_C["gv"], 128)
    gq = []
    for h in (0, 2, 1, 3):
        gq += r(_C["gq"] + 64 * h, 64)
    B = r(_C["a_g"], 256) + r(_C["n_g"], 256) + r(_C["c_g"], 256) + r(_C["g_g"], 256) + r(_C["nq"], 256) + gq
    return np.array(A + B, dtype=np.int64)


def nat_rs(qr, R):
    return min(max(qr - 4, 0), R - 8)


class _StopBuild(Exception):
    pass


def build_program(nl=NL, LS=4096, NPS=4, LP=256, dbg=False, limit=None):
    nc = bass.Bass("TRN2", target_bir_lowering=False)
    S = Sched()
    PO, NPAR = param_layout(nl)
    R = LS // GW
    TP = LP * NPS
    NKT_S = LS // 128 + PAST // 128
    stack = ExitStack()

    def din(name, shape, dt=F32):
        return nc.dram_tensor(name, list(shape), dt, kind="ExternalInput").ap()

    def dout(name, shape, dt=F32):
        return nc.dram_tensor(name, list(shape), dt, kind="ExternalOutput").ap()

    def dscr(name, shape, dt):
        return nc.dram_tensor(name, list(shape), dt).ap()

    xT_p = din("xT_p", [D, TP])
    xT_s = din("xT_s", [D, LS])
    params_d = din("params", [128, NPAR])
    wmod_d = din("w_mod", [nl, D, 3 * D])
    win_d = din("w_in_p", [nl, D, 3 * D])
    brow_d = din("b_row", [1, nl * 384])
    wout_d = din("w_out", [nl, D, D])
    pwbd_d = din("pool_bd", [nl, 2, 128, 128])
    cpw_d = din("conv_pw", [nl, 256, 256])
    bm_d = din("nat_bm", [nl, 4, 128, 14 * 64])
    cmask_d = din("colmask", [128, 64])
    nkc_d = din("nkc", [nl, 256, PAST])
    nvc_d = din("nvc", [nl, PAST, 256])
    gkc_d = din("gkc", [nl, 128, PAST])
    gvc_d = din("gvc", [nl, PAST, 128])
    rope_d = din("rope", [2, 128, LS])
    consts_d = din("consts", [4, 128, 128])

    yT_p = dout("yT_p", [D, TP])
    yT_s = dout("yT_s", [D, LS])
    nkT_o = dout("nkT_o", [nl, 256, TP])
    gkT_o = dout("gkT_o", [nl, 128, TP])
    nv_o = dout("nv_o", [nl, TP, 256])
    gv_o = dout("gv_o", [nl, TP, 128])
    dbg_out = {}

    xs = {("P", 0): dscr("xsP0", [D, TP], F32), ("P", 1): dscr("xsP1", [D, TP], F32),
          ("S", 0): dscr("xsS0", [D, LS], F32), ("S", 1): dscr("xsS1", [D, LS], F32)}
    SEQW = {"P": LP + 2 * PADW, "S": LS + 2 * PADW}
    axs = {"P": dscr("axP", [256, NPS * SEQW["P"]], BF16), "S": dscr("axS", [256, SEQW["S"]], BF16)}
    zs = {"P": dscr("zP", [256, NPS * SEQW["P"]], BF16), "S": dscr("zS", [256, SEQW["S"]], BF16)}
    nks = dscr("nkS", [256, LS], BF16)
    nvs = dscr("nvS", [LS, 256], BF16)

    def sb(name, shape, dt):
        return stack.enter_context(nc.sbuf_tensor(name, list(shape), dt))

    TM = 512
    WPAD = TM + 4 * PADW
    GK = sb("GK", [128, max(LS + PAST, TP)], BF16)
    GV = sb("GV", [128, max(NKT_S, TP // 128), 2, 128], BF16)
    WA = sb("WA", [128, 8, 1536], BF16)
    WB = sb("WB", [128, 8, 1536], BF16)
    WO = sb("WO", [128, 8, D], BF16)
    CPW = sb("CPW", [128, 2, 256], BF16)
    PWBD = sb("PWBD", [128, 2, 128], BF16)
    EM = sb("EM", [128, 4, 14 * 64], BF16)
    NKC = sb("NKC", [128, 2, PAST], BF16)
    n_p = 2 * TP + (TP // 128) * 4 * 128
    NVC_OFF = 5760
    ONES_OFF = NVC_OFF + (PAST // 128) * 256
    n_s = ONES_OFF + 64
    assert n_p <= ONES_OFF
    JR = sb("JR", [128, max(n_p, n_s)], BF16)
    NVC = JR[:, NVC_OFF:ONES_OFF].rearrange("p (i c) -> p i c", c=256)

    NKP = JR[:, 0:2 * TP].rearrange("p (t n) -> p t n", t=2)
    NVP = JR[:, 2 * TP:n_p].rearrange("p (i h c) -> p i h c", h=4, c=128)
    NKW = JR[:, 0:1920].rearrange("p (t n) -> p t n", t=2)
    NVWe = JR[:, 1920:1920 + 2048].rearrange("p (j c) -> p j c", c=256)
    NVWo = JR[:, 3968:3968 + 1792].rearrange("p (j c) -> p j c", c=256)
    IDB = sb("IDB", [128, 128], BF16)
    ONESB = sb("ONESB", [128, 128], BF16)
    BD64 = sb("BD64", [128, 128], BF16)
    PERM = sb("PERM", [128, 128], F32)
    BROW = sb("BROW", [128, 384], BF16)
    CMASK = sb("CMASK", [128, 64], BF16)
    PRM = sb("PRM", [128, NPAR], F32)
    MODT = sb("MODT", [128, nl, 24, 2], F32)
    LV2 = sb("LV", [128, 2, 4, 8], F32)
    SC = sb("SC", [128, 8, 2], F32)
    EPSV = sb("EPSV", [128, 2], F32)
    XB = sb("XB", [128, 8, TM], F32)
    H = sb("H", [128, 8, TM], BF16)
    GATES = sb("GATES", [128, 8, TM], BF16)
    NQM = sb("NQM", [128, 4, TM], BF16)
    GQM = sb("GQM", [128, 4, TM], BF16)
    MIX = sb("MIX", [128, 8, TM], BF16)
    AXW = sb("AXW", [128, 2, WPAD], BF16)
    ZW = sb("ZW", [128, 2, WPAD], BF16)
    NDG = 6
    DG = sb("DG", [128, NDG, 128], BF16)
    NPT = 6
    PT = sb("PT", [128, NPT, TM], BF16)
    COS = sb("COS", [128, TM], F32)
    SIN = sb("SIN", [128, TM], F32)
    NFS = 7
    FS = sb("FS", [128, NFS, WPAD], F32)
    SQ = sb("SQ", [128, TM], BF16)
    VBF = sb("VBF", [128, TM], BF16)
    ZN = sb("ZN", [128, 2, TM], BF16)
    YP = sb("YP", [128, 2, TM], BF16)
    ZERO = sb("ZERO", [128, 2 * PADW], BF16)
    DUMMY = sb("DUMMY", [128, 8], F32)

    G32 = GATES[:, :, :].rearrange("p a w -> p (a w)").bitcast(F32).rearrange("p (s w) -> p s w", w=TM)
    M32 = MIX[:, :, :].rearrange("p a w -> p (a w)").bitcast(F32).rearrange("p (s w) -> p s w", w=TM)
    banks = [stack.enter_context(nc.psum_tensor(f"bk{i}", [128, 512], F32)) for i in range(8)]

    def P(name, l=0, j=0):
        o, c = PO[name]
        return o + l * c + j

    def pv(name, l=0, j=0, n=1):
        o = P(name, l, j)
        return PRM[:, o:o + n]

    rot = {"mm": [0, 1, 2, 3], "o": [4, 5], "st": [6, 7]}
    rpos = {k: 0 for k in rot}

    def bank(role):
        i = rot[role][rpos[role] % len(rot[role])]
        rpos[role] += 1
        return banks[i], ("bk", i)

    cnt = {"pt": 0, "dg": 0}

    stg_n = [0]

    def stage(name):
        stg_n[0] += 1
        if limit is not None and stg_n[0] >= limit and not S.stopped:
            print("STOP at stage", stg_n[0], name)
            S.stopped = True

    def fsz(ap):
        n = 1
        for d in ap.shape[1:]:
            n *= d
        return n

    def mm(out, lhsT, rhs, start, stop, reads, writes):
        n = fsz(out)
        c = max(n / 2.0, 90.0) * (4.0 if rhs.dtype == F32 else 1.0)
        S.add("pe", lambda e: e.matmul(out, lhsT=lhsT, rhs=rhs, start=start, stop=stop), reads, writes, cost=c,
              lat=c + 160.0)

    def act(out, in_, func, reads, writes, bias=None, scale=None):
        kw = {}
        if bias is not None:
            kw["bias"] = bias
        if scale is not None:
            kw["scale"] = scale
        S.add("act", lambda e: e.activation(out=out, in_=in_, func=func, **kw), reads, writes,
              cost=200.0 + 0.75 * fsz(out))

    def ecost(eng, kind, n):
        if eng == "pool":
            return {"tt": 150 + 1.7 * n, "ts": 230 + 1.4 * n, "cp": 150 + 2.5 * n}[kind]
        return {"tt": 100 + 0.7 * n, "ts": 100 + 0.6 * n, "cp": 100 + 0.5 * n, "stt": 120 + 1.1 * n}[kind]

    def tt(eng, out, in0, in1, op, reads, writes):
        S.add(eng, lambda e: e.tensor_tensor(out=out, in0=in0, in1=in1, op=op), reads, writes,
              cost=ecost(eng, "tt", fsz(out)))

    def ts(eng, out, in0, s1, s2, op0, op1, reads, writes):
        c = ecost(eng, "ts", fsz(out))
        if op1 is None:
            S.add(eng, lambda e: e.tensor_scalar(out=out, in0=in0, scalar1=s1, scalar2=None, op0=op0), reads, writes,
                  cost=c)
        else:
            S.add(eng, lambda e: e.tensor_scalar(out=out, in0=in0, scalar1=s1, scalar2=s2, op0=op0, op1=op1),
                  reads, writes, cost=c)

    def stt(out, in0, scalar, in1, op0, op1, reads, writes):
        S.add("dve", lambda e: e.scalar_tensor_tensor(out=out, in0=in0, scalar=scalar, in1=in1, op0=op0, op1=op1),
              reads, writes, cost=ecost("dve", "stt", fsz(out)))

    def cp(eng, out, in_, reads, writes):
        if eng == "act":
            S.add(eng, lambda e: e.copy(out=out, in_=in_), reads, writes, cost=200.0 + 0.75 * fsz(out))
        else:
            S.add(eng, lambda e: e.tensor_copy(out=out, in_=in_), reads, writes, cost=ecost(eng, "cp", fsz(out)))

    def dma(q, out, in_, reads, writes):
        nbytes = fsz(out) * out.shape[0] * (4 if out.dtype == F32 else 2)
        S.add(q, lambda e: e.dma_start(out=out, in_=in_), reads, writes, dma=True,
              cost=(150.0 if q == "sp" else 1500.0), lat=2500.0 + nbytes / 120.0)

    def okey():
        k = ("out", len(S.outkeys))
        S.outkeys.append(k)
        return k

    def dbg_dump(name, ap, shape, key):
        if not dbg:
            return
        d = dout("dbg_" + name, shape, ap.dtype)
        dbg_out[name] = d
        dma("sp", d, ap, [key] if not isinstance(key, list) else key, [okey()])

    dma("sp", PRM[:, :], params_d[:, :], [], ["PRM"])
    dma("pool", IDB[:, :], consts_d[0], [], ["IDB"])
    dma("pool", ONESB[:, :], consts_d[1], [], ["ONESB"])
    dma("pool", BD64[:, :], consts_d[2], [], ["BD64"])
    dma("sp", PERM[:, :], consts_d[3], [], ["PERM"])
    dma("pool", CMASK[:, :], cmask_d[:, :], [], ["CMASK"])
    S.add("dve", lambda e: e.memset(ZERO[:, :], 0.0), [], ["ZERO"])
    S.add("dve", lambda e: e.memset(EPSV[:, 0:1], LN_EPS), [], ["EPSV"])
    S.add("dve", lambda e: e.memset(EPSV[:, 1:2], RMS_EPS), [], ["EPSV"])
    S.add("pool", lambda e: e.memset(GV[:, :, :, 64:128], 1.0), [], [("GV", i) for i in range(GV.shape[1])])
    S.add("pool", lambda e: e.memset(NVP[:, :, :, 64:128], 1.0), [], [("NVP", i) for i in range(TP // 128)])
    S.add("pool", lambda e: e.memset(BROW[:, :], 0.0), [], ["BROW"])
    S.add("pool", lambda e: e.memset(NQM[:, :, :], 0.0), [], [("NQ", h) for h in range(4)])
    S.add("pool", lambda e: e.memset(GQM[:, :, :], 0.0), [], [("GQ", h) for h in range(4)])
    for job, nseq in (("P", NPS), ("S", 1)):
        L = LP if job == "P" else LS
        for s in range(nseq):
            for scr in (axs[job], zs[job]):
                for t in range(2):
                    b0 = s * SEQW[job]
                    dma("sp", scr[t * 128:(t + 1) * 128, b0:b0 + PADW], ZERO[:, 0:PADW], ["ZERO"],
                        [("pad", job, s, id(scr), t, 0)])
                    dma("sp", scr[t * 128:(t + 1) * 128, b0 + PADW + L:b0 + 2 * PADW + L], ZERO[:, 0:PADW],
                        ["ZERO"], [("pad", job, s, id(scr), t, 1)])

    stage("prologue-loads")
    o_c, _ = PO["cond"]
    act(SC[:, :, :].rearrange("p k j -> p (k j)"), PRM[:, o_c:o_c + 16], AF.Silu, ["PRM"], ["SC"])
    idle_tiles = GV.shape[1] - TP // 128
    bgw = 128 if idle_tiles >= 8 else 0
    NBG = min(3, idle_tiles // 8)
    ob, _ = PO["bmod"]

    def mod_layer(l, bg):
        cw = bgw if bg else 512
        nch = 3072 // cw
        for ch in range(nch):
            if bg:
                r0 = TP // 128 + 8 * ((l * nch + ch) % NBG)
                WM = GV[:, r0:r0 + 8, :, :].rearrange("p i k c -> p (i k c)").bitcast(F32).rearrange(
                    "p (k w) -> p k w", w=cw)
                wkeys = [("GV", r0 + i) for i in range(8)]
            else:
                par = ch % 2
                WM = XB
                wkeys = [("XB", k) for k in range(8)] if par == 0 else (
                    [("G", j) for j in range(8)] + [("MIX", j, hh) for j in range(8) for hh in range(2)])
            src_w = wmod_d[l].rearrange("(kt p) n -> p kt n", p=128)[:, :, ch * cw:(ch + 1) * cw]
            if (not bg) and par == 1:
                dma("sp", G32[:, :, :], src_w[:, 0:4, :], [], wkeys)
                dma("sp", M32[:, :, :], src_w[:, 4:8, :], [], wkeys)
                wsel = lambda kt: (G32 if kt < 4 else M32)[:, kt % 4, :]
            else:
                dma("sp", WM[:, :, :], src_w, [], wkeys)
                wsel = lambda kt, WM=WM: WM[:, kt, :]
            bk, bkk = bank("st")
            ng = cw // 128
            for c4 in range(ng):
                for kt in range(8):
                    mm(bk[:, c4 * 2:c4 * 2 + 2], wsel(kt)[:, c4 * 128:(c4 + 1) * 128], SC[:, kt, :], kt == 0, kt == 7,
                       wkeys + ["SC"], [bkk])
            ct0 = ch * ng
            cp("dve", MODT[:, l, ct0:ct0 + ng, :].rearrange("p c j -> p (c j)"), bk[:, 0:2 * ng], [bkk],
               [("MODT", l)])
        for j in range(2):
            tt("dve", MODT[:, l, :, j], MODT[:, l, :, j], PRM[:, ob + l * 24:ob + (l + 1) * 24], ALU.add,
               [("MODT", l), "PRM"], [("MODT", l)])

    mod_layer(0, False)
    for l in range(1, nl):
        mod_layer(l, bgw > 0)
    if bgw > 0 and nl > 1:
        nt = 8 * NBG
        S.add("pool", lambda e: e.memset(GV[:, TP // 128:TP // 128 + nt, :, 64:128], 1.0), [],
              [("GV", TP // 128 + i) for i in range(nt)], cost=1500.0)

    stage("prologue-mod")
    def x_src(job, l):
        if l == 0:
            return xT_p if job == "P" else xT_s
        return xs[(job, l % 2)]

    def load_x(job, l, t0, T):
        dma("sp", XB[:, :, 0:T], x_src(job, l).rearrange("(kt p) t -> p kt t", p=128)[:, :, t0:t0 + T],
            [("xs", job, l % 2, t0 // T)] if l > 0 else [], [("XB", k) for k in range(8)])

    HK = {"H": lambda kt: [("H", kt)], "MIX": lambda kt: [("MIX", kt, 0), ("MIX", kt, 1)]}
    HB = {"H": H, "MIX": MIX}

    def h_from_xb(T, hb):
        for kt in range(8):
            ts("pool", HB[hb][:, kt, 0:T], XB[:, kt, 0:T], LV[:, 0, kt:kt + 1], LV[:, 1, kt:kt + 1], ALU.mult, ALU.add,
               [("XB", kt), LVK], HK[hb](kt))

    def stage_h(job, l, t0, T):
        src = x_src(job, l).rearrange("(kt p) t -> p kt t", p=128)
        rk = [("xs", job, l % 2, t0 // T)] if l > 0 else []
        gk = [("G", j) for j in range(8)]
        mk = [("MIX", j, hh) for j in range(8) for hh in range(2)]
        dma("sp", G32[:, :, 0:T], src[:, 0:4, t0:t0 + T], rk, gk)
        dma("sp", M32[:, :, 0:T], src[:, 4:8, t0:t0 + T], rk, mk)
        for kt in range(8):
            st_, sk = (G32, gk) if kt < 4 else (M32, mk)
            ts("pool", H[:, kt, 0:T], st_[:, kt % 4, 0:T], LV[:, 0, kt:kt + 1], LV[:, 1, kt:kt + 1], ALU.mult, ALU.add,
               sk + [LVK], [("H", kt)])

    def proj_fm(W, wkey, ct, T, hb="H"):
        bk, bkk = bank("mm")
        for kt in range(8):
            mm(bk[:, 0:T], W[:, kt, ct * 128:(ct + 1) * 128], HB[hb][:, kt, 0:T], kt == 0, kt == 7,
               [wkey] + HK[hb](kt), [bkk])
        return bk, bkk

    def rms_rope(bk, bkk, bias_ap, gain_ap, T, rope, dst, fout=None):
        G0, G1, R1 = FS[:, 0, 0:T], FS[:, 1, 0:T], FS[:, 2, 0:T]
        ts("dve", G0, bk[:, 0:T], bias_ap, None, ALU.add, None, [bkk, "PRM"], [("FS", 0)])
        stage("rr-ts")
        act(SQ[:, 0:T], bk[:, 0:T], AF.Square, [bkk, "PRM"], ["SQ"], bias=bias_ap)
        stage("rr-square")
        b2, b2k = bank("st")
        mm(b2[:, 0:T], BD64[:, :], SQ[:, 0:T], True, True, ["BD64", "SQ"], [b2k])
        act(R1, b2[:, 0:T], AF.Ln, [b2k, "EPSV"], [("FS", 2)], bias=EPSV[:, 1:2], scale=1.0 / HD)
        stage("rr-ln")
        act(R1, R1, AF.Exp, [("FS", 2)], [("FS", 2)], scale=-0.5)
        stage("rr-exp")
        stt(G1, G0, gain_ap, R1, ALU.mult, ALU.mult, [("FS", 0), ("FS", 2), "PRM"], [("FS", 1)])
        stage("rr-stt")
        if fout is not None:
            dma("sp", fout, G1, [("FS", 1)], [okey()])
        if not rope:
            for (ps, d_ap, dk) in dst:
                cp("pool", d_ap, G1[ps, :], [("FS", 1)], dk)
            return
        b3, b3k = bank("st")
        mm(b3[:, 0:T], PERM[:, :], G1, True, True, ["PERM", ("FS", 1)], [b3k])
        R2, R3 = FS[:, 3, 0:T], FS[:, 4, 0:T]
        tt("dve", R2, G1, COS[:, 0:T], ALU.mult, [("FS", 1), "ROPE"], [("FS", 3)])
        tt("dve", R3, b3[:, 0:T], SIN[:, 0:T], ALU.mult, [b3k, "ROPE"], [("FS", 4)])
        for (ps, d_ap, dk) in dst:
            tt("pool", d_ap, R2[ps, :], R3[ps, :], ALU.add, [("FS", 3), ("FS", 4)], dk)

    def attn_finish(ob_, obk, po, db_, dbk, pd, T, mt, pb, gate_j, c0=0):
        RC, TMP = FS[:, 5, 0:T], FS[:, 6, 0:T]
        qo = slice(po, po + 64)
        qb = slice(pb, pb + 64)
        hk = "lo" if po == 0 else "hi"
        act(RC[qo, :], db_[pd:pd + 64, 0:T], AF.Ln, [dbk], [("FS", 5, hk)])
        act(RC[qo, :], RC[qo, :], AF.Exp, [("FS", 5, hk)], [("FS", 5, hk)], scale=-1.0)
        tt("dve", TMP[qo, :], ob_[qo, 0:T], RC[qo, :], ALU.mult, [obk, ("FS", 5, hk)], [("FS", 6, hk)])
        if po != pb:
            hk2 = "lo" if pb == 0 else "hi"
            cp("act", TMP[qb, :], TMP[qo, :], [("FS", 6, hk)], [("FS", 6, hk2)])
            hk = hk2
        tt("dve", MIX[qb, mt, c0:c0 + T], TMP[qb, :], GATES[qb, gate_j, c0:c0 + T], ALU.mult,
           [("FS", 6, hk), ("G", gate_j)], [("MIX", mt, pb // 64)])

    def full_attn(T, nkt, qf, kf, vf, mt, pb, gate_j, c0=0):
        ob_, obk = bank("o")
        q_ap, qk = qf()
        sb_ = {}

        def qk_exp(i):
            bk, bkk = bank("mm")
            k_ap, kk = kf(i)
            mm(bk[:, 0:T], k_ap, q_ap, True, True, qk + kk, [bkk])
            slot = cnt["pt"] % NPT
            cnt["pt"] += 1
            act(PT[:, slot, 0:T], bk[:, 0:T], AF.Exp, [bkk], [("PT", slot)], scale=HD ** -0.5)
            sb_[i] = slot

        def pv_(i):
            v_ap, vk = vf(i)
            slot = sb_[i]
            mm(ob_[:, 0:T], v_ap, PT[:, slot, 0:T], i == 0, i == nkt - 1, vk + [("PT", slot)], [obk])

        LOOK = 2
        for i in range(nkt + LOOK):
            if i < nkt:
                qk_exp(i)
            if i >= LOOK:
                pv_(i - LOOK)
        attn_finish(ob_, obk, 0, ob_, obk, 64, T, mt, pb, gate_j, c0)

    assert NPS % 2 == 0 and LP == 256
    jobs = [("P", 2 * LP, NPS // 2, LP), ("S", 512, LS // 512, LS)]
    seq_jl = [(jb[0], l) for jb in jobs for l in range(nl)]

    def load_weights_A(l):
        dma("pool", WA[:, :, :], win_d[l].rearrange("(kt p) n -> p kt n", p=128)[:, :, 0:1536], [], ["WA"])
        dma("pool", BROW[0:1, :], brow_d[:, l * 384:(l + 1) * 384], [], ["BROW"])

    def load_weights_B(l):
        dma("pool", WB[:, :, :], win_d[l].rearrange("(kt p) n -> p kt n", p=128)[:, :, 1536:3072], [], ["WB"])
        dma("pool", WO[:, :, :], wout_d[l].rearrange("(kt p) n -> p kt n", p=128), [], ["WO"])
        dma("pool", CPW[:, :, :], cpw_d[l].rearrange("(t p) n -> p t n", p=128), [], ["CPW"])
        dma("pool", PWBD[:, :, :], pwbd_d[l].rearrange("t p n -> p t n"), [], ["PWBD"])

    load_weights_A(0)
    for (job, T, NB, L) in jobs:
        jc = 0 if job == "P" else 1
        if job == "S":
            S.add("pool", lambda e: e.memset(DUMMY[:, 0:1], 0.0), [],
                  [("NKP", x) for x in range(NPS)] + [("NVP", x) for x in range(TP // 128)]
                  + [("NKW", 0), ("NKW", 1), "NVWe", "NVWo", "NVC"])
        for l in range(nl):
            lvp = seq_jl.index((job, l)) % 2
            LV = LV2[:, lvp]
            LVK = ("LV", lvp)
            ts("dve", LV[:, 0, :], MODT[:, l, 8:16, jc], 1.0, None, ALU.add, None, [("MODT", l)], [LVK])
            cp("dve", LV[:, 1, :], MODT[:, l, 0:8, jc], [("MODT", l)], [LVK])
            cp("dve", LV[:, 2, :], MODT[:, l, 16:24, jc], [("MODT", l)], [LVK])
            tt("dve", LV[:, 3, :], MODT[:, l, 16:24, jc], PRM[:, P("bout", l):P("bout", l) + 8], ALU.mult,
               [("MODT", l), "PRM"], [LVK])
            load_weights_B(l)
            if job == "S":
                dma("pool", NKC[:, :, :], nkc_d[l].rearrange("(t p) s -> p t s", p=128), [], ["NKC"])
                dma("pool", NVC[:, :, :], nvc_d[l].rearrange("(i p) c -> p i c", p=128), [], ["NVC"])
                dma("pool", GK[:, LS:LS + PAST], gkc_d[l], [], [("GK", "ctx")])
                for i in range(PAST // 128):
                    dma("pool", GV[:, LS // 128 + i, :, 0:64],
                        gvc_d[l][i * 128:(i + 1) * 128, :].rearrange("p (k d) -> p k d", k=2), [],
                        [("GV", LS // 128 + i)])
                for h in range(4):
                    F = M32.rearrange("p s w -> p (s w)")[:, 0:896]
                    mk_ = [("MIX", j, hh) for j in range(8) for hh in range(2)]
                    dma("sp", F, bm_d[l, h], [], mk_)
                    act(F, F, AF.Exp, mk_, mk_)
                    tt("dve", EM[:, h, :].rearrange("p (m c) -> p m c", c=64), F.rearrange("p (m c) -> p m c", c=64),
                       CMASK[:, :].unsqueeze(1).to_broadcast([128, 14, 64]), ALU.mult,
                       mk_ + ["CMASK"], [("EM", h)])

            stage(f"{job}{l}-setup")
            for b in range(NB):
                t0 = b * T
                CW = L if job == "P" else T
                NCH = T // CW
                sbase = (b * NCH * SEQW["P"]) if job == "P" else 0
                tin = 0 if job == "P" else t0

                def scr_dst(scr, t):
                    if NCH == 1:
                        return scr[t * 128:(t + 1) * 128, sbase + PADW + tin:sbase + PADW + tin + T]
                    return scr[t * 128:(t + 1) * 128, sbase:sbase + NCH * (CW + 2 * PADW)].rearrange(
                        "p (s w) -> p s w", w=CW + 2 * PADW)[:, :, PADW:PADW + CW]

                def chunked(ap):
                    return ap if NCH == 1 else ap.rearrange("p (s w) -> p s w", w=CW)
                hb = "H" if b % 2 == 0 else "MIX"
                if b == 0:
                    stage_h(job, l, t0, T)
                else:
                    load_x(job, l, t0, T)
                    h_from_xb(T, hb)
                if job == "S":
                    dma("sp", COS[:, 0:T], rope_d[0][:, t0:t0 + T], [], ["ROPE"])
                    dma("sp", SIN[:, 0:T], rope_d[1][:, t0:t0 + T], [], ["ROPE"])
                stage(f"{job}{l}A{b}-loadxh")
                for t in range(2):
                    bk, bkk = proj_fm(WA, "WA", t, T, hb)
                    ts("dve", GATES[:, t, 0:T], bk[:, 0:T], pv("bin", l, t), None, ALU.add, None, [bkk, "PRM"],
                       [("G", t)])
                    dma("sp", scr_dst(axs[job], t), chunked(GATES[:, t, 0:T]), [("G", t)], [("ax", job, b, t)])
                stage(f"{job}{l}A{b}-ax")
                for t in range(2):
                    bk, bkk = proj_fm(WA, "WA", 2 + t, T, hb)
                    if job == "S":
                        ts("dve", GATES[:, 4 + t, 0:T], bk[:, 0:T], pv("bin", l, 2 + t), None, ALU.add, None,
                           [bkk, "PRM"], [("G", 4 + t)])
                        dma("sp", nks[t * 128:(t + 1) * 128, t0:t0 + T], GATES[:, 4 + t, 0:T], [("G", 4 + t)],
                            [("nks", b, t)])
                    else:
                        F = FS[:, 5 + t, 0:T]
                        ts("dve", F, bk[:, 0:T], pv("bin", l, 2 + t), None, ALU.add, None, [bkk, "PRM"],
                           [("FS", 5 + t)])
                        dma("sp", nkT_o[l, t * 128:(t + 1) * 128, t0:t0 + T], F, [("FS", 5 + t)], [okey()])
                        cp("pool", NKP[:, t, t0:t0 + T], F, [("FS", 5 + t)], [("NKP", b)])
                stage(f"{job}{l}A{b}-nk")
                for t in range(2):
                    bv, bvk = proj_fm(WA, "WA", 4 + t, T, hb)
                    bg, bgk = proj_fm(WA, "WA", 6 + t, T, hb)
                    SG = FS[:, 5 + t, 0:T]
                    act(SG, bg[:, 0:T], AF.Sigmoid, [bgk, "PRM"], [("FS", 5 + t)], bias=pv("bin", l, 6 + t))
                    stt(GATES[:, 2 + t, 0:T], bv[:, 0:T], pv("bin", l, 4 + t), SG, ALU.add, ALU.mult,
                        [bvk, ("FS", 5 + t), "PRM"], [("G", 2 + t)])
                    dma("sp", scr_dst(zs[job], t), chunked(GATES[:, 2 + t, 0:T]), [("G", 2 + t)], [("z", job, b, t)])
                stage(f"{job}{l}A{b}-z")
                bk, bkk = proj_fm(WA, "WA", 8, T, hb)
                rms_rope(bk, bkk, pv("bin", l, 8), pv("kn", l), T, job == "S",
                         [(slice(0, 128), GK[:, t0:t0 + T], [("GK", b)])],
                         fout=(gkT_o[l, :, t0:t0 + T] if job == "P" else None))
                stage(f"{job}{l}A{b}-gk")
                for tt_ in range(T // 128):
                    bk, bkk = bank("o")
                    mm(bk[:, 0:384], ONESB[:, :], BROW[:, :], True, False,
                       ["ONESB", "BROW"], [bkk])
                    for kt in range(8):
                        mm(bk[:, 0:384], HB[hb][:, kt, tt_ * 128:(tt_ + 1) * 128], WA[:, kt, 1152:1536], False, kt == 7,
                           HK[hb](kt) + ["WA"], [bkk])
                    gtile = (t0 + tt_ * 128) // 128
                    if job == "S":
                        stg = GATES[:, 6:8, :].rearrange("p a w -> p (a w)")[:, tt_ * 256:(tt_ + 1) * 256]
                        cp("act", stg, bk[:, 0:256], [bkk], [("G", 6 + tt_ // 2)])
                        dma("sp", nvs[t0 + tt_ * 128:t0 + (tt_ + 1) * 128, :], stg, [("G", 6 + tt_ // 2)],
                            [("nvs", b, tt_)])
                        cp("dve", GV[:, gtile, :, 0:64], bk[:, 256:384].rearrange("p (k d) -> p k d", k=2), [bkk],
                           [("GV", gtile)])
                    else:
                        F = FS[:, 3 + (tt_ % 2), 0:384]
                        cp("act", F, bk[:, 0:384], [bkk], [("FS", 3 + (tt_ % 2))])
                        dma("sp", nv_o[l, t0 + tt_ * 128:t0 + (tt_ + 1) * 128, :], F[:, 0:256],
                            [("FS", 3 + (tt_ % 2))], [okey()])
                        dma("sp", gv_o[l, t0 + tt_ * 128:t0 + (tt_ + 1) * 128, :], F[:, 256:384],
                            [("FS", 3 + (tt_ % 2))], [okey()])
                        cp("pool", NVP[:, gtile, :, 0:64], F[:, 0:256].rearrange("p (h d) -> p h d", h=4),
                           [("FS", 3 + (tt_ % 2))], [("NVP", gtile)])
                        cp("pool", GV[:, gtile, :, 0:64], F[:, 256:384].rearrange("p (k d) -> p k d", k=2),
                           [("FS", 3 + (tt_ % 2))], [("GV", gtile)])

            stage(f"{job}{l}-phaseA")
            nxt = seq_jl.index((job, l)) + 1
            if nxt < len(seq_jl):
                load_weights_A(seq_jl[nxt][1])

            for b in range(NB):
                t0 = b * T
                CW = L if job == "P" else T
                NCH = T // CW
                CP = CW + 2 * PADW
                sbase = (b * NCH * SEQW["P"]) if job == "P" else 0
                tin = 0 if job == "P" else t0
                if b == 0:
                    stage_h(job, l, t0, T)
                load_x(job, l, t0, T)
                W = NCH * CP
                for t in range(2):
                    nb_ = [b] if job == "P" else [x for x in (b - 1, b, b + 1) if 0 <= x < NB]
                    sqs = [b * NCH + x for x in range(NCH)] if job == "P" else [0]
                    dma("sp", AXW[:, t, 0:W], axs[job][t * 128:(t + 1) * 128, sbase + tin:sbase + tin + W],
                        [("ax", job, x, t) for x in nb_]
                        + [("pad", job, sq_, id(axs[job]), t, sd) for sd in (0, 1) for sq_ in sqs], [("AXW", t)])
                    dma("sp", ZW[:, t, 0:W], zs[job][t * 128:(t + 1) * 128, sbase + tin:sbase + tin + W],
                        [("z", job, x, t) for x in nb_]
                        + [("pad", job, sq_, id(zs[job]), t, sd) for sd in (0, 1) for sq_ in sqs], [("ZW", t)])
                if job == "S":
                    dma("sp", COS[:, 0:T], rope_d[0][:, t0:t0 + T], [], ["ROPE"])
                    dma("sp", SIN[:, 0:T], rope_d[1][:, t0:t0 + T], [], ["ROPE"])
                    qr0 = t0 // GW
                    rmin = nat_rs(qr0, R)
                    rmax = nat_rs(qr0 + 7, R) + 8
                    nrows = rmax - rmin
                    for t in range(2):
                        dma("sp", NKW[:, t, 0:nrows * 64], nks[t * 128:(t + 1) * 128, rmin * 64:rmax * 64],
                            [("nks", x, t) for x in range(rmin // 8, (rmax - 1) // 8 + 1)], [("NKW", t)])
                    ne = nrows // 2
                    no = (nrows - 1) // 2
                    dma("sp", NVWe[:, 0:ne, :],
                        nvs[rmin * 64:rmin * 64 + ne * 128, :].rearrange("(j p) c -> p j c", p=128),
                        [("nvs", x, y) for x in range(rmin // 8, (rmax - 1) // 8 + 1) for y in range(4)], ["NVWe"])
                    dma("sp", NVWo[:, 0:no, :],
                        nvs[(rmin + 1) * 64:(rmin + 1) * 64 + no * 128, :].rearrange("(j p) c -> p j c", p=128),
                        [("nvs", x, y) for x in range(rmin // 8, (rmax - 1) // 8 + 1) for y in range(4)], ["NVWo"])
                for j in range(8):
                    bk, bkk = proj_fm(WB, "WB", j, T)
                    act(GATES[:, j, 0:T], bk[:, 0:T], AF.Silu, [bkk, "PRM"], [("G", j)], bias=pv("bin", l, 12 + j))
                for t in range(2):
                    bk, bkk = proj_fm(WB, "WB", 8 + t, T)
                    for hh in range(2):
                        ps = slice(64 * hh, 64 * hh + 64)
                        ts("dve", NQM[ps, 2 * t + hh, 0:T], bk[ps, 0:T], PRM[ps, P("bin", l, 20 + t):P("bin", l, 20 + t) + 1],
                           None, ALU.add, None, [bkk, "PRM"], [("NQ", 2 * t + hh)])
                for t in range(2):
                    bk, bkk = proj_fm(WB, "WB", 10 + t, T)
                    rms_rope(bk, bkk, pv("bin", l, 22 + t), pv("qn", l), T, job == "S",
                             [(slice(0, 64), GQM[0:64, t, 0:T], [("GQ", t)]),
                              (slice(64, 128), GQM[64:128, t + 2, 0:T], [("GQ", t + 2)])])

                stage(f"{job}{l}b{b}-proj")
                edge_l = (tin == 0)
                edge_r = (tin + CW == L)
                for t in range(2):
                    X = AXW[:, t, :]
                    S2, S4, S8, S16 = FS[:, 0, :], FS[:, 1, :], FS[:, 2, :], FS[:, 3, :]
                    tt("pool", S2[:, 1:W], X[:, 0:W - 1], X[:, 1:W], ALU.add, [("AXW", t)], [("FS", 0)])
                    if t == 0:
                        tt("pool", S4[64:128, 2:W - 1], S2[64:128, 1:W - 2], S2[64:128, 3:W], ALU.add, [("FS", 0)],
                           [("FS", 1)])
                        fin = [(0, S2, ("FS", 0)), (64, S4, ("FS", 1))]
                    else:
                        tt("pool", S4[:, 2:W - 1], S2[:, 1:W - 2], S2[:, 3:W], ALU.add, [("FS", 0)], [("FS", 1)])
                        tt("pool", S8[:, 4:W - 3], S4[:, 2:W - 5], S4[:, 6:W - 1], ALU.add, [("FS", 1)], [("FS", 2)])
                        tt("pool", S16[64:128, 8:W - 7], S8[64:128, 4:W - 11], S8[64:128, 12:W - 3], ALU.add,
                           [("FS", 2)], [("FS", 3)])
                        fin = [(0, S8, ("FS", 2)), (64, S16, ("FS", 3))]
                    for (p0, Sx, sk) in fin:
                        ps = slice(p0, p0 + 64)
                        for ch in range(NCH):
                            wo, yo = ch * CP + PADW, ch * CW
                            stt(YP[ps, t, yo:yo + CW], Sx[ps, wo:wo + CW],
                                PRM[ps, P("invwin", 0, t):P("invwin", 0, t) + 1],
                                X[ps, wo:wo + CW], ALU.mult, ALU.subtract, [sk, ("AXW", t), "PRM"], [("YP", t, p0)])
                            for (flag, c0, nm) in ((edge_l, 0, "edgeL"), (edge_r, CW - 8, "edgeR")):
                                if not flag:
                                    continue
                                E8 = FS[ps, 4, 0:8]
                                o_e = P(nm, 0, t * 8)
                                tt("dve", E8, Sx[ps, wo + c0:wo + c0 + 8], PRM[ps, o_e:o_e + 8], ALU.mult,
                                   [sk, "PRM"], [("FS", 4)])
                                tt("dve", YP[ps, t, yo + c0:yo + c0 + 8], E8, X[ps, wo + c0:wo + c0 + 8], ALU.subtract,
                                   [("FS", 4), ("AXW", t)], [("YP", t, p0)])
                    bk, bkk = bank("mm")
                    mm(bk[:, 0:T], PWBD[:, t, :], YP[:, t, 0:T], True, True, ["PWBD", ("YP", t, 0), ("YP", t, 64)],
                       [bkk])
                    stt(MIX[:, t, 0:T], bk[:, 0:T], pv("pscale", l, t), GATES[:, t, 0:T], ALU.mult, ALU.mult,
                        [bkk, "PRM", ("G", t)], [("MIX", t, 0), ("MIX", t, 1)])

                stage(f"{job}{l}b{b}-mixA")
                ZC = [FS[:, 0, 0:T], FS[:, 1, 0:T]]
                s1, s1k = bank("st")
                s2, s2k = bank("st")
                for t in range(2):
                    bk, bkk = bank("mm")
                    for k in range(31):
                        slot = cnt["dg"] % NDG
                        cnt["dg"] += 1
                        ts("pool", DG[:, slot, :], IDB[:, :], pv("convw", l, k * 2 + t), 0.0, ALU.mult, ALU.add,
                           ["IDB", "PRM"], [("DG", slot)])
                        for ch in range(NCH):
                            mm(bk[:, ch * CW:(ch + 1) * CW], DG[:, slot, :], ZW[:, t, ch * CP + k + 1:ch * CP + k + 1 + CW],
                               k == 0 and ch == 0, k == 30 and ch == NCH - 1, [("DG", slot), ("ZW", t)], [bkk])
                    act(ZC[t], bk[:, 0:T], AF.Identity, [bkk, "PRM"], [("FS", t)], bias=pv("convb", l, t))
                    act(SQ[:, 0:T], bk[:, 0:T], AF.Square, [bkk, "PRM"], ["SQ"], bias=pv("convb", l, t))
                    cp("dve", VBF[:, 0:T], ZC[t], [("FS", t)], ["VBF"])
                    mm(s1[:, 0:T], ONESB[:, :], VBF[:, 0:T], t == 0, t == 1, ["ONESB", "VBF"], [s1k])
                    mm(s2[:, 0:T], ONESB[:, :], SQ[:, 0:T], t == 0, t == 1, ["ONESB", "SQ"], [s2k])
                MEAN, MSQ, RSTD = FS[:, 2, 0:T], FS[:, 3, 0:T], FS[:, 4, 0:T]

                def ln_stats(n, epscol):
                    act(MEAN, s1[:, 0:T], AF.Identity, [s1k], [("FS", 2)], scale=1.0 / n)
                    act(MSQ, s1[:, 0:T], AF.Square, [s1k], [("FS", 3)], scale=1.0 / n)
                    stt(RSTD, s2[:, 0:T], 1.0 / n, MSQ, ALU.mult, ALU.subtract, [s2k, ("FS", 3)], [("FS", 4)])
                    act(RSTD, RSTD, AF.Ln, [("FS", 4), "EPSV"], [("FS", 4)], bias=EPSV[:, epscol:epscol + 1])
                    act(RSTD, RSTD, AF.Exp, [("FS", 4)], [("FS", 4)], scale=-0.5)

                ln_stats(256.0, 0)
                for t in range(2):
                    tt("dve", ZC[t], ZC[t], MEAN, ALU.subtract, [("FS", t), ("FS", 2)], [("FS", t)])
                    tt("dve", ZC[t], ZC[t], RSTD, ALU.mult, [("FS", t), ("FS", 4)], [("FS", t)])
                    act(ZN[:, t, 0:T], ZC[t], AF.Silu, [("FS", t), "PRM"], [("ZN", t)], bias=pv("clnb", l, t),
                        scale=pv("clng", l, t))
                for ot in range(2):
                    bk, bkk = bank("mm")
                    for t in range(2):
                        mm(bk[:, 0:T], CPW[:, t, ot * 128:(ot + 1) * 128], ZN[:, t, 0:T], t == 0, t == 1,
                           ["CPW", ("ZN", t)], [bkk])
                    tt("dve", MIX[:, 4 + ot, 0:T], bk[:, 0:T], GATES[:, 4 + ot, 0:T], ALU.mult, [bkk, ("G", 4 + ot)],
                       [("MIX", 4 + ot, 0), ("MIX", 4 + ot, 1)])

                stage(f"{job}{l}b{b}-mixC")
                if job == "P":
                    nkt = L // 128
                    for ch in range(NCH):
                        cb = t0 + ch * CW
                        for h in range(4):
                            th, pb = h // 2, 64 * (h % 2)
                            full_attn(CW, nkt,
                                      lambda h=h, ch=ch: (NQM[:, h, ch * CW:(ch + 1) * CW], [("NQ", h)]),
                                      lambda i, th=th, cb=cb: (NKP[:, th, cb + i * 128:cb + (i + 1) * 128], [("NKP", b)]),
                                      lambda i, h=h, cb=cb: (NVP[:, cb // 128 + i, h, :], [("NVP", cb // 128 + i)]),
                                      2 + th, pb, 2 + th, c0=ch * CW)
                else:
                    for h in range(4):
                        th, pb = h // 2, 64 * (h % 2)
                        ob_, obk = bank("o")
                        db_, dbk = bank("st")
                        nrow = T // GW
                        st = {}
                        cst = {}

                        def ctx_qk(i, th=th, h=h):
                            bk, bkk = bank("mm")
                            mm(bk[:, 0:T], NKC[:, th, i * 128:(i + 1) * 128], NQM[:, h, 0:T], True, True,
                               ["NKC", ("NQ", h)], [bkk])
                            slot = cnt["pt"] % NPT
                            cnt["pt"] += 1
                            act(PT[:, slot, 0:T], bk[:, 0:T], AF.Exp, [bkk], [("PT", slot)], scale=HD ** -0.5)
                            cst[i] = slot

                        def ctx_pv(i, h=h, th=th):
                            slot = cst[i]
                            mm(ob_[:, 0:T], NVC[:, i, th * 128:(th + 1) * 128], PT[:, slot, 0:T], i == 0, False,
                               ["NVC", ("PT", slot)], [obk])
                            mm(db_[:, 0:T], ONESB[:, :], PT[:, slot, 0:T], i == 0, False, ["ONESB", ("PT", slot)], [dbk])

                        nctx = PAST // 128
                        for i in range(nctx + 2):
                            if i < nctx:
                                ctx_qk(i)
                            if i >= 2:
                                ctx_pv(i - 2)

                        def nat_qk(r, th=th, h=h):
                            qr = qr0 + r
                            rs = nat_rs(qr, R)
                            off = rs - rmin
                            bk, bkk = bank("mm")
                            for j in range(4):
                                mm(bk[:, j * 64:(j + 1) * 64], NKW[:, th, (off + 2 * j) * 64:(off + 2 * j + 2) * 64],
                                   NQM[:, h, r * 64:(r + 1) * 64], True, True, [("NKW", th), ("NQ", h)], [bkk])
                            slot = cnt["pt"] % NPT
                            cnt["pt"] += 1
                            act(PT[:, slot, 0:256], bk[:, 0:256], AF.Exp, [bkk], [("PT", slot)], scale=HD ** -0.5)
                            e = qr - rs
                            m0 = 7 - e
                            tt("dve", PT[:, slot, 0:256].rearrange("p (j c) -> p j c", c=64),
                               PT[:, slot, 0:256].rearrange("p (j c) -> p j c", c=64),
                               EM[:, h, :].rearrange("p (m c) -> p m c", c=64)[:, m0:m0 + 7:2, :],
                               ALU.mult, [("PT", slot), ("EM", h)], [("PT", slot)])
                            st[r] = (slot, off)

                        def nat_pv(r, h=h, th=th):
                            slot, off = st[r]
                            c0 = r * 64
                            for j in range(4):
                                if off % 2 == 0:
                                    v_ap, vk = NVWe[:, off // 2 + j, th * 128:(th + 1) * 128], "NVWe"
                                else:
                                    v_ap, vk = NVWo[:, (off - 1) // 2 + j, th * 128:(th + 1) * 128], "NVWo"
                                mm(ob_[:, c0:c0 + 64], v_ap, PT[:, slot, j * 64:(j + 1) * 64], False,
                                   (r == nrow - 1 and j == 3), [vk, ("PT", slot)], [obk])
                            for j in range(4):
                                mm(db_[:, c0:c0 + 64], ONESB[:, :], PT[:, slot, j * 64:(j + 1) * 64], False,
                                   (r == nrow - 1 and j == 3), ["ONESB", ("PT", slot)], [dbk])

                        LK = 2
                        for r in range(nrow + LK):
                            if r < nrow:
                                nat_qk(r)
                            if r >= LK:
                                nat_pv(r - LK)
                        attn_finish(ob_, obk, pb, db_, dbk, pb, T, 2 + th, pb, 2 + th)

                stage(f"{job}{l}b{b}-mixB")
                nkt = (L // 128) if job == "P" else NKT_S
                for ch in range(NCH):
                    kbase = (t0 + ch * CW) if job == "P" else 0
                    for h in range(4):
                        kv = h // 2
                        mt, pb = 6 + h // 2, 64 * (h % 2)
                        full_attn(CW, nkt,
                                  lambda h=h, ch=ch: (GQM[:, h, ch * CW:(ch + 1) * CW], [("GQ", h)]),
                                  lambda i, kbase=kbase: (GK[:, kbase + i * 128:kbase + (i + 1) * 128],
                                                          [("GK", x) for x in (list(range(NB)) + ["ctx"])] if job == "S"
                                                          else [("GK", b)]),
                                  lambda i, kv=kv, kbase=kbase: (GV[:, kbase // 128 + i, kv, :],
                                                                 [("GV", kbase // 128 + i)]),
                                  mt, pb, 6 + h // 2, c0=ch * CW)

                stage(f"{job}{l}b{b}-mixD")
                s1, s1k = bank("st")
                s2, s2k = bank("st")
                for ot in range(8):
                    bk, bkk = bank("mm")
                    for kt in range(8):
                        mm(bk[:, 0:T], WO[:, kt, ot * 128:(ot + 1) * 128], MIX[:, kt, 0:T], kt == 0, kt == 7,
                           ["WO", ("MIX", kt, 0), ("MIX", kt, 1)], [bkk])
                    T1 = FS[:, 5 + (ot % 2), 0:T]
                    act(T1, bk[:, 0:T], AF.Identity, [bkk, LVK], [("FS", 5 + (ot % 2))], bias=LV[:, 3, ot:ot + 1],
                        scale=LV[:, 2, ot:ot + 1])
                    stt(XB[:, ot, 0:T], XB[:, ot, 0:T], float(ALPHA), T1, ALU.mult, ALU.add,
                        [("XB", ot), ("FS", 5 + (ot % 2))], [("XB", ot)])
                    act(SQ[:, 0:T], XB[:, ot, 0:T], AF.Square, [("XB", ot)], ["SQ"])
                    cp("dve", VBF[:, 0:T], XB[:, ot, 0:T], [("XB", ot)], ["VBF"])
                    mm(s1[:, 0:T], ONESB[:, :], VBF[:, 0:T], ot == 0, ot == 7, ["ONESB", "VBF"], [s1k])
                    mm(s2[:, 0:T], ONESB[:, :], SQ[:, 0:T], ot == 0, ot == 7, ["ONESB", "SQ"], [s2k])
                if b + 1 < NB:
                    stage_h(job, l, t0 + T, T)
                ln_stats(float(D), 0)
                last = (l == nl - 1)
                dst = (yT_p if job == "P" else yT_s) if last else xs[(job, (l + 1) % 2)]
                for ot in range(8):
                    tt("dve", XB[:, ot, 0:T], XB[:, ot, 0:T], MEAN, ALU.subtract, [("XB", ot), ("FS", 2)],
                       [("XB", ot)])
                    tt("dve", XB[:, ot, 0:T], XB[:, ot, 0:T], RSTD, ALU.mult, [("XB", ot), ("FS", 4)], [("XB", ot)])
                    ts("dve", XB[:, ot, 0:T], XB[:, ot, 0:T], pv("lng", l, ot), pv("lnb", l, ot), ALU.mult, ALU.add,
                       [("XB", ot), "PRM"], [("XB", ot)])
                wkey = okey() if last else ("xs", job, (l + 1) % 2, b)
                dma("sp", dst.rearrange("(kt p) t -> p kt t", p=128)[:, :, t0:t0 + T], XB[:, :, 0:T],
                    [("XB", k) for k in range(8)], [wkey])

    S.stopped = False
    S.add("sp", lambda e: e.nop(), [k for k in S.outkeys if k in S.lastw], [])
    S.emit(nc, stack)
    stack.close()
    return nc


def host_constants(LS, LP):
    ident = np.eye(128, dtype=np.float32)
    ones = np.ones((128, 128), np.float32)
    bd = np.zeros((128, 128), np.float32)
    bd[:64, :64] = 1.0
    bd[64:, 64:] = 1.0
    perm = np.zeros((128, 128), np.float32)
    for m in range(128):
        d = m % 64
        blk = d // 16
        partner = m + 16 if blk % 2 == 0 else m - 16
        perm[partner, m] = 1.0
    consts = np.stack([ident, ones, bd, perm]).astype(np.float32)
    half, nf = HD // 2, HD // 4
    t = np.arange(LS)
    inv = (10000.0 ** (-np.arange(nf, dtype=np.float32) * 2.0 / half)).astype(np.float32)
    cos = np.zeros((128, LS), np.float32)
    sin = np.zeros((128, LS), np.float32)
    for p in range(128):
        d = p % 64
        pos = (t // GW) if d < half else (t % GW)
        dd = d % half
        f = dd % nf
        ang = pos.astype(np.float32) * inv[f]
        cos[p] = np.cos(ang).astype(np.float32)
        s = np.sin(ang).astype(np.float32)
        sin[p] = -s if dd < nf else s
    rope = np.stack([cos, sin]).astype(np.float32)
    col = np.arange(GW)
    cs = np.clip(col - 8, 0, GW - 16)
    col_in = (col[None, :] >= cs[:, None]) & (col[None, :] < cs[:, None] + 16)
    cmask = np.zeros((128, 64), np.float32)
    for krl in range(2):
        cmask[krl * 64:(krl + 1) * 64, :] = col_in.T.astype(np.float32)
    return consts, rope, cmask


def nat_master(nat_bias):
    nl = nat_bias.shape[0]
    krl = np.arange(2)[:, None, None, None]
    kc = np.arange(64)[None, :, None, None]
    m = np.arange(14)[None, None, :, None]
    qc = np.arange(64)[None, None, None, :]
    dr = np.broadcast_to(krl + m, (2, 64, 14, 64))
    dc = np.broadcast_to(np.clip(kc - qc, -15, 15) + 15, (2, 64, 14, 64))
    out = nat_bias[:, :, dr, dc]
    return np.ascontiguousarray(out.reshape(nl, 4, 128, 14 * 64)).astype(np.float32)


def build_params(nl, inp, c_vec, c_ctx, bperm, L_for_edges=None):
    PO, NPAR = param_layout(nl)
    prm = np.zeros((128, NPAR), np.float32)

    def put(name, l, arr):
        o, c = PO[name]
        prm[:, o + l * c:o + l * c + c] = arr.T

    for l in range(nl):
        put("bmod", l, inp["b_mod"][l].reshape(24, 128))
        put("bin", l, bperm[l].reshape(24, 128))
        put("pscale", l, inp["pool_scale"][l].reshape(2, 128))
        put("convw", l, inp["conv_w"][l].reshape(31, 2, 128).reshape(62, 128))
        put("convb", l, inp["conv_b"][l].reshape(2, 128))
        put("clng", l, inp["conv_ln_g"][l].reshape(2, 128))
        put("clnb", l, inp["conv_ln_b"][l].reshape(2, 128))
        put("bout", l, inp["b_out"][l].reshape(8, 128))
        put("lng", l, inp["ln_g"][l].reshape(8, 128))
        put("lnb", l, inp["ln_b"][l].reshape(8, 128))
        put("qn", l, np.tile(inp["q_norm"][l], 2)[None, :])
        put("kn", l, np.tile(inp["k_norm"][l], 2)[None, :])
    o, _ = PO["cond"]
    cc = np.stack([c_ctx.reshape(8, 128), c_vec.reshape(8, 128)], axis=-1)
    prm[:, o:o + 16] = cc.transpose(1, 0, 2).reshape(128, 16)
    wins = [(2, 4), (8, 16)]
    o, _ = PO["invwin"]
    oL, _ = PO["edgeL"]
    oR, _ = PO["edgeR"]
    for t in range(2):
        for half in range(2):
            w = wins[t][half]
            ps = slice(half * 64, half * 64 + 64)
            prm[ps, o + t] = 1.0 / w
            for i in range(8):
                cl = i + w // 2 - max(i - w // 2, 0)
                prm[ps, oL + t * 8 + i] = 1.0 / cl
                cr = min(w, 8 - i + w // 2)
                prm[ps, oR + t * 8 + i] = 1.0 / cr
    return prm


def prepare_inputs(inp, nl=NL, LS=4096, NPS=4, LP=256, n_cores=N_CORES):
    perm = col_perm()
    f = lambda a: np.ascontiguousarray(np.asarray(a, dtype=np.float32))
    inp = {k: f(v) for k, v in inp.items()}
    w_in_p = np.ascontiguousarray(inp["w_in"][:nl][:, :, perm])
    bperm = inp["b_in"][:nl][:, perm]
    b_row = np.ascontiguousarray(bperm[:, 1152:1536].reshape(1, nl * 384))
    consts, rope, cmask = host_constants(LS, LP)
    pool_bd = np.zeros((nl, 2, 128, 128), np.float32)
    for l in range(nl):
        for g in range(4):
            t, hh = g // 2, g % 2
            pool_bd[l, t, hh * 64:(hh + 1) * 64, hh * 64:(hh + 1) * 64] = inp["pool_w"][l, g]
    bm = nat_master(inp["nat_bias"][:nl])
    shared = dict(w_mod=np.ascontiguousarray(inp["w_mod"][:nl]), w_in_p=w_in_p, b_row=b_row,
                  w_out=np.ascontiguousarray(inp["w_out"][:nl]), pool_bd=pool_bd,
                  conv_pw=np.ascontiguousarray(inp["conv_pw"][:nl]), nat_bm=bm, colmask=cmask, rope=rope,
                  consts=consts)
    maps = []
    for i in range(n_cores):
        xp = inp["x_prompt"][i * NPS:(i + 1) * NPS].reshape(NPS * LP, D)
        m = dict(shared)
        m["xT_p"] = np.ascontiguousarray(xp.T)
        m["xT_s"] = np.ascontiguousarray(inp["x_sample"][i].T)
        m["params"] = build_params(nl, inp, inp["c"][i], inp["c_ctx"], bperm)
        m["nkc"] = np.ascontiguousarray(inp["cache_nat_k"][i, :nl].reshape(nl, PAST, 256).transpose(0, 2, 1))
        m["nvc"] = np.ascontiguousarray(inp["cache_nat_v"][i, :nl].reshape(nl, PAST, 256))
        m["gkc"] = np.ascontiguousarray(inp["cache_gqa_k"][i, :nl].reshape(nl, PAST, 128).transpose(0, 2, 1))
        m["gvc"] = np.ascontiguousarray(inp["cache_gqa_v"][i, :nl].reshape(nl, PAST, 128))
        maps.append(m)
    return maps


_NC_CACHE = {}


def kernel(**inputs):
    key = "full"
    if key not in _NC_CACHE:
        _NC_CACHE[key] = build_program()
    nc = _NC_CACHE[key]
    maps = prepare_inputs(inputs)
    res = run_bass_kernel_spmd(nc, maps, core_ids=list(range(N_CORES)))
    rs = res.results
    B, SEQ, DB, DSEQ = 32, 256, 8, 4096
    y_prompt = np.zeros((B, SEQ, D), np.float32)
    y_sample = np.zeros((DB, DSEQ, D), np.float32)
    nk = np.zeros((B, NL, SEQ, 4, HD), np.float32)
    nv = np.zeros((B, NL, SEQ, 4, HD), np.float32)
    gk = np.zeros((B, NL, SEQ, 2, HD), np.float32)
    gv = np.zeros((B, NL, SEQ, 2, HD), np.float32)
    for i, r in enumerate(rs):
        y_prompt[4 * i:4 * i + 4] = np.asarray(r["yT_p"]).T.reshape(4, SEQ, D)
        y_sample[i] = np.asarray(r["yT_s"]).T
        a = np.asarray(r["nkT_o"]).reshape(NL, 256, 4, SEQ)
        nk[4 * i:4 * i + 4] = a.transpose(2, 0, 3, 1).reshape(4, NL, SEQ, 4, HD)
        a = np.asarray(r["gkT_o"]).reshape(NL, 128, 4, SEQ)
        gk[4 * i:4 * i + 4] = a.transpose(2, 0, 3, 1).reshape(4, NL, SEQ, 2, HD)
        a = np.asarray(r["nv_o"]).reshape(NL, 4, SEQ, 256)
        nv[4 * i:4 * i + 4] = a.transpose(1, 0, 2, 3).reshape(4, NL, SEQ, 4, HD)
        a = np.asarray(r["gv_o"]).reshape(NL, 4, SEQ, 128)
        gv[4 * i:4 * i + 4] = a.transpose(1, 0, 2, 3).reshape(4, NL, SEQ, 2, HD)
    return (y_prompt, y_sample, nk, nv, gk, gv)
```
